# Optimizing a Trainium2 kernel written in Bass

```python
import math
import jax
import jax.numpy as jnp
from jax import lax
import numpy as np

D_MODEL = 2048
BATCH = 8
SEQ = 2048
DEPTH = 2

CTX_LEN = 256
GRID_W = 64
HEAD_DIM = 128
Q_BLOCK = 128
ROPE_THETA = 10000.0
NORM_EPS = 1e-6
LN_EPS = 1e-5
MIX_HALF = D_MODEL // 2

CONV_CH = MIX_HALF
CONV_WIDTH = 31
GQA_Q_HEADS = MIX_HALF // HEAD_DIM
GQA_KV_HEADS = max(1, GQA_Q_HEADS // 4)
GQA_GROUP = GQA_Q_HEADS // GQA_KV_HEADS
GQA_Q = GQA_Q_HEADS * HEAD_DIM
GQA_KV = GQA_KV_HEADS * HEAD_DIM
IN_EVEN = 2 * CONV_CH + GQA_Q + 2 * GQA_KV
MIX_EVEN = CONV_CH + GQA_Q

HYENA_CH = MIX_HALF
HYENA_ORDER = 2
HYENA_SHORT = 3
HYENA_POS_DIM = 33
HYENA_FILTER_WIDTH = 64
HYENA_FAST_DECAY = 0.3
HYENA_SLOW_DECAY = 1.5
HYENA_TARGET = 1e-2
HYENA_MAX_DECAY = math.log(HYENA_TARGET) / HYENA_FAST_DECAY
HYENA_MIN_DECAY = math.log(HYENA_TARGET) / HYENA_SLOW_DECAY
HYENA_IN = (HYENA_ORDER + 1) * HYENA_CH
MLA_HEADS = MIX_HALF // HEAD_DIM
MLA_Q_RANK = D_MODEL // 4
MLA_KV_RANK = D_MODEL // 8
MLA_NOPE = HEAD_DIM
MLA_ROPE = HEAD_DIM // 2
MLA_V = HEAD_DIM
MLA_QK = MLA_NOPE + MLA_ROPE
IN_ODD = HYENA_IN + MLA_Q_RANK + MLA_KV_RANK + MLA_ROPE
MIX_ODD = HYENA_CH + MLA_HEADS * MLA_V

FFN_RAW = -(-8 * D_MODEL // 3)
FFN_HIDDEN = -(-FFN_RAW // 256) * 256

kernel_name = 'hybrid_diffusion_conv_gqa_hyena_mla'


def rms_norm(x, g, eps=NORM_EPS):
    xf = x.astype(jnp.float32)
    y = xf * lax.rsqrt(jnp.mean(xf * xf, axis=-1, keepdims=True) + eps)
    return (y * g.astype(jnp.float32)).astype(x.dtype)


def layer_norm(x, g, b, eps=LN_EPS):
    xf = x.astype(jnp.float32)
    mu = jnp.mean(xf, axis=-1, keepdims=True)
    xc = xf - mu
    var = jnp.mean(xc * xc, axis=-1, keepdims=True)
    y = xc * lax.rsqrt(var + eps) * g.astype(jnp.float32) + b.astype(jnp.float32)
    return y.astype(x.dtype)


def modulate(h, shift, scale):
    return h * (1 + scale) + shift


def rope_1d(x, pos):
    d = x.shape[-1]
    half = d // 2
    inv = ROPE_THETA ** (-jnp.arange(half, dtype=jnp.float32) / half)
    ang = pos[:, None] * inv[None, :]
    shape = (pos.shape[0],) + (1,) * (x.ndim - 3) + (half,)
    cos = jnp.cos(ang).reshape(shape)
    sin = jnp.sin(ang).reshape(shape)
    xf = x.astype(jnp.float32)
    x1, x2 = xf[..., :half], xf[..., half:]
    return jnp.concatenate([x1 * cos - x2 * sin, x1 * sin + x2 * cos], axis=-1).astype(x.dtype)


def axial_rope(x, row, col):
    h = x.shape[-1] // 2
    return jnp.concatenate([rope_1d(x[..., :h], row), rope_1d(x[..., h:], col)], axis=-1)


def blocked_attention(q, k, v, scale):
    B, L, Hk, G, dk = q.shape
    nb = L // Q_BLOCK
    qb = jnp.moveaxis(q.reshape(B, nb, Q_BLOCK, Hk, G, dk), 1, 0)

    def one_block(qi):
        s = jnp.einsum('bqhgd,bthd->bhgqt', qi, k, preferred_element_type=jnp.float32) * scale
        p = jax.nn.softmax(s, axis=-1).astype(v.dtype)
        return jnp.einsum('bhgqt,bthd->bqhgd', p, v)

    out = lax.map(one_block, qb)
    return jnp.moveaxis(out, 0, 1).reshape(B, L, Hk, G, v.shape[-1])


def depthwise_conv(x, w, b):
    K, C = w.shape
    pad = (K - 1) // 2
    y = lax.conv_general_dilated(x, w.astype(x.dtype)[:, None, :], window_strides=(1,),
                                 padding=[(pad, pad)], dimension_numbers=('NWC', 'WIO', 'NWC'),
                                 feature_group_count=C)
    return y + b.astype(x.dtype)


def swiglu(h, wg, wu, wd):
    return (jax.nn.silu(h @ wg) * (h @ wu)) @ wd


def conformer_conv(z, dw_w, dw_b, ln_g, ln_b):
    a, g = jnp.split(z, 2, axis=-1)
    u = a * jax.nn.sigmoid(g)
    u = depthwise_conv(u, dw_w, dw_b)
    return jax.nn.silu(layer_norm(u, ln_g, ln_b))


def gqa_q(zq, qn_g):
    B, L = zq.shape[:2]
    return rms_norm(zq.reshape(B, L, GQA_KV_HEADS, GQA_GROUP, HEAD_DIM), qn_g)


def gqa_kv(zkv, kn_g):
    B, L = zkv.shape[:2]
    k = rms_norm(zkv[..., :GQA_KV].reshape(B, L, GQA_KV_HEADS, HEAD_DIM), kn_g)
    v = zkv[..., GQA_KV:].reshape(B, L, GQA_KV_HEADS, HEAD_DIM)
    return k, v


def even_mixer(h, hc, w_in, w_out, dw_w, dw_b, ln_g, ln_b, qn_g, kn_g, row, col, ctx_out):
    B, L, _ = h.shape
    Lc = hc.shape[1]
    q0 = 2 * CONV_CH
    kv0 = q0 + GQA_Q
    scale = HEAD_DIM ** -0.5
    z = h @ w_in
    q = axial_rope(gqa_q(z[..., q0:kv0], qn_g), row, col)
    k, v = gqa_kv(z[..., kv0:], kn_g)
    k = axial_rope(k, row, col)
    kc, vc = gqa_kv(hc @ w_in[:, kv0:], kn_g)
    att = blocked_attention(q, jnp.concatenate([kc, k], axis=1), jnp.concatenate([vc, v], axis=1), scale)
    conv = conformer_conv(z[..., :q0], dw_w, dw_b, ln_g, ln_b)
    out = jnp.concatenate([conv, att.reshape(B, L, GQA_Q)], axis=-1) @ w_out
    if not ctx_out:
        return out, None
    zc = hc @ w_in[:, :kv0]
    att_c = blocked_attention(gqa_q(zc[..., q0:], qn_g), kc, vc, scale)
    conv_c = conformer_conv(zc[..., :q0], dw_w, dw_b, ln_g, ln_b)
    out_c = jnp.concatenate([conv_c, att_c.reshape(B, Lc, GQA_Q)], axis=-1) @ w_out
    return out, out_c


def hyena_filters(L, w1, b1, w2, b2, w3, b3, w4, freq):
    f32 = jnp.float32
    t = jnp.linspace(0.0, 1.0, L, dtype=f32)[:, None]
    bands = (HYENA_POS_DIM - 1) // 2
    w = (2.0 * math.pi / L) * jnp.arange(L, dtype=f32)[:, None]
    f = jnp.linspace(1e-4, bands - 1, bands, dtype=f32)[None, :]
    feat = jnp.concatenate([t, jnp.cos(f * w), -jnp.sin(f * w)], axis=-1)
    fr = freq.astype(f32)
    hh = jnp.sin(fr[0] * (feat @ w1.astype(f32) + b1.astype(f32)))
    hh = jnp.sin(fr[1] * (hh @ w2.astype(f32) + b2.astype(f32)))
    hh = jnp.sin(fr[2] * (hh @ w3.astype(f32) + b3.astype(f32)))
    hh = (hh @ w4.astype(f32)).reshape(L, 2, HYENA_ORDER, HYENA_CH)
    deltas = jnp.abs(jnp.linspace(HYENA_MIN_DECAY, HYENA_MAX_DECAY, HYENA_CH, dtype=f32))
    hh = hh * jnp.exp(-t * deltas[None, :])[:, None, None, :]
    return hh / (jnp.sum(jnp.abs(hh), axis=0, keepdims=True) + 1e-6)


def bidir_long_conv(u, h_fwd, h_bwd, skip):
    L, C = h_fwd.shape
    k = jnp.concatenate([h_fwd, jnp.zeros((1, C), h_fwd.dtype), h_bwd[:0:-1]], axis=0)
    kf = jnp.fft.rfft(k, axis=0)
    uf32 = u.astype(jnp.float32)
    uf = jnp.fft.rfft(uf32, n=2 * L, axis=1)
    y = jnp.fft.irfft(uf * kf[None], n=2 * L, axis=1)[:, :L]
    return (y + uf32 * skip.astype(jnp.float32)).astype(u.dtype)


def hyena_mix(z, short_w, short_b, filt, skip):
    L = z.shape[1]
    z = depthwise_conv(z, short_w, short_b)
    parts = jnp.split(z, HYENA_ORDER + 1, axis=-1)
    hh = hyena_filters(L, *filt)
    y = parts[-1]
    for n in range(HYENA_ORDER):
        y = parts[n] * bidir_long_conv(y, hh[:, 0, n], hh[:, 1, n], skip[n])
    return y


def mla_q(zq, qn_g, w_uq, row, col):
    B, L = zq.shape[:2]
    q = (rms_norm(zq, qn_g) @ w_uq).reshape(B, L, MLA_HEADS, MLA_QK)
    q_nope, q_rope = q[..., :MLA_NOPE], q[..., MLA_NOPE:]
    if row is not None:
        q_rope = axial_rope(q_rope, row, col)
    return jnp.concatenate([q_nope, q_rope], axis=-1)[:, :, :, None, :]


def mla_kv(zkv, kvn_g, w_ukv, row, col):
    B, L = zkv.shape[:2]
    ckv = rms_norm(zkv[..., :MLA_KV_RANK], kvn_g)
    k_rope = zkv[..., MLA_KV_RANK:]
    if row is not None:
        k_rope = axial_rope(k_rope, row, col)
    kv = (ckv @ w_ukv).reshape(B, L, MLA_HEADS, MLA_NOPE + MLA_V)
    k_nope, v = kv[..., :MLA_NOPE], kv[..., MLA_NOPE:]
    k = jnp.concatenate([k_nope, jnp.broadcast_to(k_rope[:, :, None, :], (B, L, MLA_HEADS, MLA_ROPE))], axis=-1)
    return k, v


def odd_mixer(h, hc, w_in, w_out, short_w, short_b, filt, skip, qn_g, kvn_g, w_uq, w_ukv, row, col, ctx_out):
    B, L, _ = h.shape
    Lc = hc.shape[1]
    kv0 = HYENA_IN + MLA_Q_RANK
    scale = MLA_QK ** -0.5
    z = h @ w_in
    y_h = hyena_mix(z[..., :HYENA_IN], short_w, short_b, filt, skip)
    q = mla_q(z[..., HYENA_IN:kv0], qn_g, w_uq, row, col)
    k, v = mla_kv(z[..., kv0:], kvn_g, w_ukv, row, col)
    kc, vc = mla_kv(hc @ w_in[:, kv0:], kvn_g, w_ukv, None, None)
    att = blocked_attention(q, jnp.concatenate([kc, k], axis=1), jnp.concatenate([vc, v], axis=1), scale)
    out = jnp.concatenate([y_h, att.reshape(B, L, MLA_HEADS * MLA_V)], axis=-1) @ w_out
    if not ctx_out:
        return out, None
    zc = hc @ w_in[:, :kv0]
    y_hc = hyena_mix(zc[..., :HYENA_IN], short_w, short_b, filt, skip)
    att_c = blocked_attention(mla_q(zc[..., HYENA_IN:], qn_g, w_uq, None, None), kc, vc, scale)
    out_c = jnp.concatenate([y_hc, att_c.reshape(B, Lc, MLA_HEADS * MLA_V)], axis=-1) @ w_out
    return out, out_c


def setup_inputs(seed: int = 0) -> dict:
    key = jax.random.key(seed)
    ks = iter(jax.random.split(key, 64))
    D = D_MODEL
    n_even = (DEPTH + 1) // 2
    n_odd = DEPTH // 2
    F = HYENA_FILTER_WIDTH

    def nrm(shape, scale):
        return scale * jax.random.normal(next(ks), shape, jnp.float32)

    def gain(shape):
        return 1.0 + nrm(shape, 0.01)

    return {
        'x': nrm((BATCH, SEQ, D), 1.0),
        'c': nrm((BATCH, D), 1.0),
        'ctx': nrm((BATCH, CTX_LEN, D), 1.0),
        'c_ctx': nrm((D,), 1.0),
        'ada_w': nrm((DEPTH, D, 6 * D), D ** -0.5),
        'ada_b': nrm((DEPTH, 6 * D), 0.01),
        'norm_mix_g': gain((DEPTH, D)),
        'norm_ffn_g': gain((DEPTH, D)),
        'e_w_in': nrm((n_even, D, IN_EVEN), D ** -0.5),
        'e_w_out': nrm((n_even, MIX_EVEN, D), MIX_EVEN ** -0.5),
        'e_dw_w': nrm((n_even, CONV_WIDTH, CONV_CH), CONV_WIDTH ** -0.5),
        'e_dw_b': nrm((n_even, CONV_CH), 0.01),
        'e_ln_g': gain((n_even, CONV_CH)),
        'e_ln_b': nrm((n_even, CONV_CH), 0.01),
        'e_qn_g': gain((n_even, HEAD_DIM)),
        'e_kn_g': gain((n_even, HEAD_DIM)),
        'o_w_in': nrm((n_odd, D, IN_ODD), D ** -0.5),
        'o_w_out': nrm((n_odd, MIX_ODD, D), MIX_ODD ** -0.5),
        'o_short_w': nrm((n_odd, HYENA_SHORT, HYENA_IN), HYENA_SHORT ** -0.5),
        'o_short_b': nrm((n_odd, HYENA_IN), 0.01),
        'o_f_w1': nrm((n_odd, HYENA_POS_DIM, F), HYENA_POS_DIM ** -0.5),
        'o_f_b1': nrm((n_odd, F), 0.01),
        'o_f_w2': nrm((n_odd, F, F), F ** -0.5),
        'o_f_b2': nrm((n_odd, F), 0.01),
        'o_f_w3': nrm((n_odd, F, F), F ** -0.5),
        'o_f_b3': nrm((n_odd, F), 0.01),
        'o_f_w4': nrm((n_odd, F, 2 * HYENA_ORDER * HYENA_CH), F ** -0.5),
        'o_f_freq': gain((n_odd, 3, F)),
        'o_skip': nrm((n_odd, HYENA_ORDER, HYENA_CH), 0.5),
        'o_q_norm_g': gain((n_odd, MLA_Q_RANK)),
        'o_kv_norm_g': gain((n_odd, MLA_KV_RANK)),
        'o_w_uq': nrm((n_odd, MLA_Q_RANK, MLA_HEADS * MLA_QK), MLA_Q_RANK ** -0.5),
        'o_w_ukv': nrm((n_odd, MLA_KV_RANK, MLA_HEADS * (MLA_NOPE + MLA_V)), MLA_KV_RANK ** -0.5),
        'ffn_w_gate': nrm((DEPTH, D, FFN_HIDDEN), D ** -0.5),
        'ffn_w_up': nrm((DEPTH, D, FFN_HIDDEN), D ** -0.5),
        'ffn_w_down': nrm((DEPTH, FFN_HIDDEN, D), FFN_HIDDEN ** -0.5),
        'final_norm_g': gain((D,)),
    }


def reference(x, c, ctx, c_ctx, ada_w, ada_b, norm_mix_g, norm_ffn_g,
              e_w_in, e_w_out, e_dw_w, e_dw_b, e_ln_g, e_ln_b, e_qn_g, e_kn_g,
              o_w_in, o_w_out, o_short_w, o_short_b, o_f_w1, o_f_b1, o_f_w2, o_f_b2,
              o_f_w3, o_f_b3, o_f_w4, o_f_freq, o_skip, o_q_norm_g, o_kv_norm_g, o_w_uq, o_w_ukv,
              ffn_w_gate, ffn_w_up, ffn_w_down, final_norm_g):
    S = x.shape[1]
    ROWS = S // GRID_W
    row = jnp.repeat(jnp.arange(ROWS, dtype=jnp.float32), GRID_W)
    col = jnp.tile(jnp.arange(GRID_W, dtype=jnp.float32), ROWS)
    cond = jax.nn.silu(c)
    cond_ctx = jax.nn.silu(c_ctx)
    for i in range(DEPTH):
        ctx_out = i < DEPTH - 1
        mod = cond @ ada_w[i] + ada_b[i]
        mod_c = cond_ctx @ ada_w[i] + ada_b[i]
        sm, scm, gm, sf, scf, gf = jnp.split(mod[:, None, :], 6, axis=-1)
        csm, cscm, cgm, csf, cscf, cgf = jnp.split(mod_c[None, None, :], 6, axis=-1)
        h = modulate(rms_norm(x, norm_mix_g[i]), sm, scm)
        hc = modulate(rms_norm(ctx, norm_mix_g[i]), csm, cscm)
        j = i // 2
        if i % 2 == 0:
            o, oc = even_mixer(h, hc, e_w_in[j], e_w_out[j], e_dw_w[j], e_dw_b[j], e_ln_g[j], e_ln_b[j],
                               e_qn_g[j], e_kn_g[j], row, col, ctx_out)
        else:
            filt = (o_f_w1[j], o_f_b1[j], o_f_w2[j], o_f_b2[j], o_f_w3[j], o_f_b3[j], o_f_w4[j], o_f_freq[j])
            o, oc = odd_mixer(h, hc, o_w_in[j], o_w_out[j], o_short_w[j], o_short_b[j], filt, o_skip[j],
                              o_q_norm_g[j], o_kv_norm_g[j], o_w_uq[j], o_w_ukv[j], row, col, ctx_out)
        x = x + gm * o
        x = x + gf * swiglu(modulate(rms_norm(x, norm_ffn_g[i]), sf, scf), ffn_w_gate[i], ffn_w_up[i], ffn_w_down[i])
        if ctx_out:
            ctx = ctx + cgm * oc
            ctx = ctx + cgf * swiglu(modulate(rms_norm(ctx, norm_ffn_g[i]), csf, cscf),
                                     ffn_w_gate[i], ffn_w_up[i], ffn_w_down[i])
    return rms_norm(x, final_norm_g)
```

```python
import contextlib
import math
import numpy as np
import ml_dtypes
import concourse.bass as bass
import concourse.mybir as mybir
from concourse.bass_utils import run_bass_kernel_spmd

F32 = mybir.dt.float32
BF16 = mybir.dt.bfloat16
AF = mybir.ActivationFunctionType
ALU = mybir.AluOpType


class Tok:
    __slots__ = ("w", "r")

    def __init__(self):
        self.w = None
        self.r = {}


class Op:
    __slots__ = ("eng", "fn", "deps", "sig", "cnt", "dma", "dsem", "dval", "idx", "epoch", "drain")


class Prog:
    ENGS = ("pe", "act", "dve", "pool", "sp")
    DMAQ = ("sp", "pool", "act")
    NDMA = 8

    def __init__(self, nc):
        self.nc = nc
        self.ops = []
        self.start = 0
        self.epoch = 0
        self.nops = 0
        self.ndma = {e: 0 for e in self.ENGS}
        self.cnt = {e: 0 for e in self.ENGS}
        self.waited = {e: {} for e in self.ENGS}
        self.esem = {e: nc.alloc_semaphore(name="s_" + e) for e in self.ENGS}
        self.dsem = {}
        for e in self.DMAQ:
            for j in range(self.NDMA):
                self.dsem[(e, j)] = nc.alloc_semaphore(name="d_%s%d" % (e, j))

    def add(self, eng, fn, reads=(), writes=(), dma=False, drain=False):
        op = Op()
        op.eng, op.fn, op.dma, op.sig, op.cnt, op.drain = eng, fn, dma, False, 0, drain
        op.idx = self.nops
        self.nops += 1
        op.epoch = self.epoch
        deps = {}
        raw = set()
        for t in reads:
            if t.w is not None:
                deps[t.w.idx] = t.w
                raw.add(t.w.idx)
        for t in writes:
            if t.w is not None:
                deps[t.w.idx] = t.w
            for o in t.r.values():
                deps[o.idx] = o
        op.deps = [d for d in deps.values() if d.epoch == self.epoch and
                   (d.dma or d.eng != eng or (eng != "pe" and d.idx in raw))]
        for d in op.deps:
            d.sig = True
        key = ("d", op.idx) if dma else eng
        for t in reads:
            t.r[key] = op
        for t in writes:
            t.w = op
            t.r = {}
        if dma:
            k = self.ndma[eng]
            self.ndma[eng] += 1
            op.dsem = (eng, k % self.NDMA)
            op.dval = 16 * (k // self.NDMA + 1)
        self.ops.append(op)
        return op

    def barrier(self):
        toks = {e: Tok() for e in self.ENGS}
        for e in self.ENGS:
            self.add(e, lambda h: (h.drain(), h.nop())[1], writes=[toks[e]], drain=True)
        for e in self.ENGS:
            self.add(e, lambda h: h.nop(), reads=list(toks.values()))
        self.epoch += 1
        self.flush()

    def flush(self):
        nc = self.nc
        ops = self.ops[self.start:]
        self.start = len(self.ops)
        for op in ops:
            if not op.dma and op.sig:
                self.cnt[op.eng] += 1
                op.cnt = self.cnt[op.eng]
        per = {e: [o for o in ops if o.eng == e] for e in self.ENGS}
        issued = {e: 0 for e in self.ENGS}
        base = {e: self.ndma[e] - sum(1 for o in per[e] if o.dma) for e in self.ENGS}

        def run(e, h):
            waited = self.waited[e]
            n_issued = base[e]
            for op in per[e]:
                need = {}
                for d in op.deps:
                    if d.dma:
                        k, v = ("D",) + d.dsem, d.dval
                    else:
                        k, v = ("E", d.eng), d.cnt
                    if need.get(k, 0) < v:
                        need[k] = v
                if op.dma and op.dval > 16:
                    k = ("D",) + op.dsem
                    need[k] = max(need.get(k, 0), op.dval - 16)
                if op.drain and e in self.DMAQ:
                    for j in range(self.NDMA):
                        c = (n_issued - j + self.NDMA - 1) // self.NDMA
                        if c > 0:
                            need[("D", e, j)] = max(need.get(("D", e, j), 0), 16 * c)
                for k, v in need.items():
                    if waited.get(k, 0) < v:
                        s = self.dsem[k[1:]] if k[0] == "D" else self.esem[k[1]]
                        h.wait_ge(s, v)
                        waited[k] = v
                ins = op.fn(h)
                inc = None
                if op.dma:
                    n_issued += 1
                    ins.then_inc(self.dsem[op.dsem], 16)
                    inc = (("D",) + op.dsem, 16)
                elif op.sig:
                    ins.then_inc(self.esem[e], 1)
                    inc = (("E", e), 1)
                self.simlog[e].append((dict(need), inc, op.idx))

        self.simlog = {e: [] for e in self.ENGS}
        with nc.Block() as block:
            if per["pe"]:
                block.tensor(lambda h: run("pe", h))
            if per["act"]:
                block.scalar(lambda h: run("act", h))
            if per["dve"]:
                block.vector(lambda h: run("dve", h))
            if per["pool"]:
                block.gpsimd(lambda h: run("pool", h))
            if per["sp"]:
                block.sync(lambda h: run("sp", h))
        self.simcheck()

    def simcheck(self):
        if not hasattr(self, "simval"):
            self.simval = {}
        val = self.simval
        pos = {e: 0 for e in self.ENGS}
        progress = True
        while progress:
            progress = False
            for e in self.ENGS:
                lg = self.simlog[e]
                while pos[e] < len(lg):
                    need, inc, idx = lg[pos[e]]
                    if all(val.get(k, 0) >= v for k, v in need.items()):
                        if inc is not None:
                            val[inc[0]] = val.get(inc[0], 0) + inc[1]
                        pos[e] += 1
                        progress = True
                    else:
                        break
        for e in self.ENGS:
            if pos[e] < len(self.simlog[e]):
                need, inc, idx = self.simlog[e][pos[e]]
                bad = {k: (v, val.get(k, 0)) for k, v in need.items() if val.get(k, 0) < v}
                raise RuntimeError("DEADLOCK on %s at op %d: waits %s" % (e, idx, bad))

    def dma(self, q, out, in_, reads=(), writes=(), **kw):
        return self.add(q, lambda h: h.dma_start(out=out, in_=in_, **kw), reads, writes, dma=True)

    def mm(self, out, lhsT, rhs, start, stop, reads=(), writes=()):
        return self.add("pe", lambda h: h.matmul(out, lhsT, rhs, start=start, stop=stop), reads, writes)

    def tr(self, out, in_, ident, reads=(), writes=()):
        return self.add("pe", lambda h: h.transpose(out, in_, ident), reads, writes)

    def act(self, out, in_, func, reads=(), writes=(), **kw):
        return self.add("act", lambda h: h.activation(out, in_, func, **kw), reads, writes)

    def ts(self, eng, out, in0, s1, s2, op0, op1=None, reads=(), writes=()):
        if op1 is None:
            return self.add(eng, lambda h: h.tensor_scalar(out, in0, s1, None, op0), reads, writes)
        return self.add(eng, lambda h: h.tensor_scalar(out, in0, s1, s2, op0, op1), reads, writes)

    def tt(self, eng, out, in0, in1, op, reads=(), writes=()):
        return self.add(eng, lambda h: h.tensor_tensor(out, in0, in1, op), reads, writes)

    def stt(self, out, in0, scalar, in1, op0, op1, reads=(), writes=()):
        return self.add("dve", lambda h: h.scalar_tensor_tensor(out, in0, scalar, in1, op0, op1), reads, writes)

    def cp(self, eng, out, in_, reads=(), writes=()):
        if eng == "act":
            return self.add(eng, lambda h: h.copy(out, in_), reads, writes)
        return self.add(eng, lambda h: h.tensor_copy(out, in_), reads, writes)

    def memset(self, eng, ap, val, writes=()):
        return self.add(eng, lambda h: h.memset(ap, val), (), writes)

    def rsum(self, out, in_, reads=(), writes=()):
        return self.add("dve", lambda h: h.tensor_reduce(out, in_, mybir.AxisListType.X, ALU.add), reads, writes)

    def recip(self, out, in_, reads=(), writes=()):
        return self.add("dve", lambda h: h.reciprocal(out, in_), reads, writes)


class Ring:
    def __init__(self, items):
        self.items = items
        self.i = 0

    def next(self):
        it = self.items[self.i % len(self.items)]
        self.i += 1
        return it


D = 2048
L = 2048
LC = 256
T = L + LC
NT = T // 128
COLT = [(0, 256), (256, 512), (768, 512), (1280, 512), (1792, 512)]
FFH = 5632
NHC = FFH // 128
NORM_EPS = 1e-6
LN_EPS = 1e-5
THETA = 10000.0
IN_EVEN = 3584
IN_ODD = 3904

WEIGHT_SPECS = [
    ("ada_w", (2, 2048, 12288)), ("ada_b", (2, 12288)), ("norm_mix_g", (2, 2048)), ("norm_ffn_g", (2, 2048)),
    ("e_w_in", (2048, 3584)), ("e_w_out", (2048, 2048)), ("e_dw_w", (31, 1024)), ("e_dw_b", (1, 1024)),
    ("e_ln_g", (1, 1024)), ("e_ln_b", (1, 1024)), ("e_qn_g", (1, 128)), ("e_kn_g", (1, 128)),
    ("o_w_in", (2048, 3904)), ("o_w_out", (2048, 2048)), ("o_short_w", (3, 3072)), ("o_short_b", (1, 3072)),
    ("o_f_w1", (33, 64)), ("o_f_b1", (1, 64)), ("o_f_w2", (64, 64)), ("o_f_b2", (1, 64)), ("o_f_w3", (64, 64)),
    ("o_f_b3", (1, 64)), ("o_f_w4", (64, 4096)), ("o_f_freq", (3, 64)), ("o_skip", (2, 1024)),
    ("o_q_norm_g", (1, 512)), ("o_kv_norm_g", (1, 256)), ("o_w_uq", (512, 1536)), ("o_w_ukv", (256, 2048)),
    ("ffn_w_gate", (2, 2048, 5632)), ("ffn_w_up", (2, 2048, 5632)), ("ffn_w_down", (2, 5632, 2048)),
    ("final_norm_g", (1, 2048)),
]

_CONST_CACHE = {}


def host_constants():
    if _CONST_CACHE:
        return _CONST_CACHE
    bf = ml_dtypes.bfloat16
    c = {}
    c["ident_f"] = np.eye(128, dtype=np.float32)
    c["ident_b"] = np.eye(128, dtype=np.float32).astype(bf)
    c["ones_f"] = np.ones((128, 128), np.float32)
    c["ones_b"] = np.ones((128, 128), np.float32).astype(bf)
    p128 = np.zeros((128, 128), np.float32)
    for m in range(128):
        if (m % 64) < 32:
            p128[m + 32, m] = -1.0
        else:
            p128[m - 32, m] = 1.0
    c["perm128"] = p128
    c["perm128b"] = p128.astype(bf)
    p64 = np.zeros((64, 64), np.float32)
    for m in range(64):
        if (m % 32) < 16:
            p64[m + 16, m] = -1.0
        else:
            p64[m - 16, m] = 1.0
    c["perm64"] = p64
    t = np.arange(L)
    row = (t // 64).astype(np.float32)
    col = (t % 64).astype(np.float32)
    inv32 = (THETA ** (-np.arange(32, dtype=np.float32) / 32)).astype(np.float32)
    inv16 = (THETA ** (-np.arange(16, dtype=np.float32) / 16)).astype(np.float32)
    ang128 = np.zeros((128, L), np.float32)
    for d in range(128):
        pos = row if d < 64 else col
        ang128[d] = pos * inv32[d % 32]
    c["cos128"] = np.cos(ang128).astype(np.float32)
    c["sin128"] = np.sin(ang128).astype(np.float32)
    ang64 = np.zeros((64, L), np.float32)
    for d in range(64):
        pos = row if d < 32 else col
        ang64[d] = pos * inv16[d % 16]
    c["cos64"] = np.cos(ang64).astype(np.float32)
    c["sin64"] = np.sin(ang64).astype(np.float32)
    tl = np.linspace(0.0, 1.0, L, dtype=np.float32)[:, None]
    w = (2.0 * math.pi / L) * np.arange(L, dtype=np.float32)[:, None]
    f = np.linspace(1e-4, 15, 16, dtype=np.float32)[None, :]
    feat = np.concatenate([tl, np.cos(f * w), -np.sin(f * w)], axis=-1).astype(np.float32)
    c["featT"] = np.ascontiguousarray(feat.T)
    c["tneg"] = np.ascontiguousarray((-tl[:, 0]).reshape(16, 128).T).astype(np.float32)
    max_decay = math.log(1e-2) / 0.3
    min_decay = math.log(1e-2) / 1.5
    c["deltas"] = np.abs(np.linspace(min_decay, max_decay, 1024, dtype=np.float32)).reshape(1, 1024).astype(np.float32)
    ff = (np.arange(L, dtype=np.float64) + 0.5) * (2.0 * math.pi / 4096.0)
    tt = np.arange(L, dtype=np.float64)
    ang = np.outer(tt, ff)
    C = np.cos(ang)
    S = np.sin(ang)
    c["dft_cf"] = np.ascontiguousarray(C.reshape(16, 128, 16, 128).transpose(2, 1, 0, 3).reshape(16, 128, 2048)).astype(bf)
    c["dft_sf"] = np.ascontiguousarray(S.reshape(16, 128, 16, 128).transpose(2, 1, 0, 3).reshape(16, 128, 2048)).astype(bf)
    CT = C.T
    ST = -S.T
    c["dft_ci"] = np.ascontiguousarray(CT.reshape(16, 128, 16, 128).transpose(2, 1, 0, 3).reshape(16, 128, 2048)).astype(bf)
    c["dft_si"] = np.ascontiguousarray(ST.reshape(16, 128, 16, 128).transpose(2, 1, 0, 3).reshape(16, 128, 2048)).astype(bf)
    _CONST_CACHE.update(c)
    return c


CONST_DT = {"ident_b": BF16, "ones_b": BF16, "perm128b": BF16, "dft_cf": BF16, "dft_sf": BF16, "dft_ci": BF16, "dft_si": BF16}


class KB:
    def __init__(self, stop=None, taps=()):
        self.stop = stop
        self.taps = set(taps)
        self.nc = bass.Bass("TRN2", target_bir_lowering=False)
        self.P = Prog(self.nc)
        self.uid = 0
        nc = self.nc
        specs = {"x": ([L, D], F32), "ctx": ([LC, D], F32), "cvec": ([2, D], F32)}
        for name, shape in WEIGHT_SPECS:
            specs[name] = (list(shape), F32)
        for name, arr in host_constants().items():
            specs[name] = (list(arr.shape), CONST_DT.get(name, F32))
        kb = self

        class LazyD(dict):
            def __missing__(self, name):
                shape, dt = specs[name]
                ap = nc.dram_tensor(name, shape, dt, kind="ExternalInput").ap()
                self[name] = ap
                return ap

        d = LazyD()
        d["out"] = nc.dram_tensor("out", [L, D], F32, kind="ExternalOutput").ap()
        self.d = d
        self.outs = ["out"]
        self.modd = nc.dram_tensor("modd", [2, 2, 12288], F32).ap()
        self.xres = nc.dram_tensor("xres", [T, D], F32).ap()
        self.t_xres = [[Tok() for _ in range(4)] for _ in range(NT)]
        self.mixd = nc.dram_tensor("mixd", [D, T], BF16).ap()
        self.t_mixd = [[Tok() for _ in range(len(COLT))] for _ in range(16)]
        self.convd = nc.dram_tensor("convd", [1024, T], F32).ap()
        self.t_convd = [Tok() for _ in range(8)]
        self.zt = nc.dram_tensor("zt", [L, 3072], F32).ap()
        self.t_zt = [[Tok() for _ in range(6)] for _ in range(16)]
        self.kfd = nc.dram_tensor("kfd", [2, 2, L, 1024], F32).ap()
        self.t_kfd = [[[Tok() for _ in range(16)] for _ in range(2)] for _ in range(2)]
        self.pes = contextlib.ExitStack()

    def nm(self, name):
        self.uid += 1
        return "%s_%d" % (name, self.uid)

    def sb(self, es, name, shape, dt):
        return es.enter_context(self.nc.sbuf_tensor(self.nm(name), list(shape), dt))

    def ps(self, es, name, shape, dt=F32):
        return es.enter_context(self.nc.psum_tensor(self.nm(name), list(shape), dt))

    def ring(self, es, name, n, shape, dt, psum=False):
        f = self.ps if psum else self.sb
        return Ring([(f(es, name, shape, dt), Tok()) for _ in range(n)])

    def tap(self, name, shape, dt=F32):
        ap = self.nc.dram_tensor(name, list(shape), dt, kind="ExternalOutput").ap()
        self.outs.append(name)
        return ap

    def setup(self):
        P, es, d = self.P, self.pes, self.d
        self.c = {}
        self.ct = {}
        for name, shape, dt in [("ident_f", [128, 128], F32), ("ident_b", [128, 128], BF16), ("ones_f", [128, 128], F32),
                                ("ones_b", [128, 128], BF16), ("perm128b", [128, 128], BF16), ("perm64", [64, 64], F32)]:
            t = self.sb(es, name, shape, dt)
            tk = Tok()
            P.dma("sp", t[:], d[name][:, :], writes=[tk])
            self.c[name] = t
            self.ct[name] = tk
        self.cols = self.sb(es, "cols", [128, 448], F32)
        self.t_cols = Tok()
        self.ncol = 0
        self.colreg = {}
        self.fcols = self.sb(es, "fcols", [64, 8], F32)
        self.t_fcols = Tok()
        self.modT = [[self.sb(es, "modT", [128, 96], F32) for _ in range(2)] for _ in range(2)]
        self.t_modT = [[Tok() for _ in range(2)] for _ in range(2)]
        self.acols = self.sb(es, "acols", [128, 128], F32)
        self.t_acols = Tok()
        self.condTb = self.sb(es, "condTb", [128, 32], BF16)
        self.t_condTb = Tok()
        with contextlib.ExitStack() as s2:
            stage = self.ring(s2, "cstage", 2, [128, 128], F32)
            cps = self.ring(s2, "cps", 2, [128, 128], F32, psum=True)

            def load_cols(key, src, R):
                off = self.ncol
                self.ncol += R
                self.colreg[key] = off
                st, t_st = stage.next()
                P.dma("sp", st[0:R, :], src, writes=[t_st])
                pt, t_pt = cps.next()
                P.tr(pt[:, 0:R], st[0:R, :], self.c["ident_f"][0:R, 0:R], reads=[t_st, self.ct["ident_f"]], writes=[t_pt])
                P.cp("dve", self.cols[:, off:off + R], pt[:, 0:R], reads=[t_pt], writes=[self.t_cols])

            r128 = lambda ap: ap.rearrange("a (k p) -> (a k) p", p=128)
            load_cols("gmix", r128(d["norm_mix_g"]), 32)
            load_cols("gffn", r128(d["norm_ffn_g"]), 32)
            dww = r128(d["e_dw_w"])
            load_cols("dww", dww[0:128, :], 128)
            load_cols("dww2", dww[128:248, :], 120)
            load_cols("dwb", r128(d["e_dw_b"]), 8)
            load_cols("lng", r128(d["e_ln_g"]), 8)
            load_cols("lnb", r128(d["e_ln_b"]), 8)
            load_cols("qng", d["e_qn_g"], 1)
            load_cols("kng", d["e_kn_g"], 1)
            load_cols("shw", r128(d["o_short_w"]), 72)
            load_cols("shb", r128(d["o_short_b"]), 24)
            load_cols("oqg", r128(d["o_q_norm_g"]), 4)
            load_cols("okg", r128(d["o_kv_norm_g"]), 2)
            st, t_st = stage.next()
            for j, nme in enumerate(["o_f_b1", "o_f_b2", "o_f_b3"]):
                P.dma("sp", st[j:j + 1, 0:64], d[nme], writes=[t_st])
            P.dma("sp", st[3:6, 0:64], d["o_f_freq"], writes=[t_st])
            pt, t_pt = cps.next()
            P.tr(pt[0:64, 0:6], st[0:6, 0:64], self.c["ident_f"][0:6, 0:6], reads=[t_st, self.ct["ident_f"]], writes=[t_pt])
            P.cp("dve", self.fcols[:, 0:6], pt[0:64, 0:6], reads=[t_pt], writes=[self.t_fcols])
            P.barrier()

    def col(self, key, j=0, n=1):
        o = self.colreg[key] + j
        return self.cols[:, o:o + n]

    def phase0(self):
        P, d = self.P, self.d
        with contextlib.ExitStack() as es:
            cc = self.sb(es, "cc", [32, 128], F32)
            t_cc = Tok()
            cv = d["cvec"]
            P.dma("sp", cc[0:16, :], cv[0:1, :].rearrange("a (k p) -> (a k) p", p=128), writes=[t_cc])
            P.dma("sp", cc[16:32, :], cv[1:2, :].rearrange("a (k p) -> (a k) p", p=128), writes=[t_cc])
            P.act(cc[:], cc[:], AF.Silu, reads=[t_cc], writes=[t_cc])
            self._pc2 = self.ps(es, "pc", [128, 128], F32)
            self._t_pc2 = Tok()
            P.tr(self._pc2[:, 0:32], cc[0:32, :], self.c["ident_f"][0:32, 0:32], reads=[t_cc, self.ct["ident_f"]], writes=[self._t_pc2])
            P.cp("dve", self.condTb[:], self._pc2[:, 0:32], reads=[self._t_pc2], writes=[self.t_condTb])
            for _ in self.mods_gen(es, 0, mode="mixed"):
                pass
            if "mod" in self.taps:
                tp = self.tap("tap_mod", [2, 2, 12288])
                P.dma("sp", tp[0:1, :, :], self.modd[0:1, :, :])
            P.barrier()

    def mods_gen(self, es, l, mode="dma", SW=512):
        P, d, c, ct = self.P, self.d, self.c, self.ct
        nsl = 12288 // SW
        FW = 16 * SW
        lhs = lambda k: self.condTb[:, :].rearrange("p (s k) -> p k s", k=16)[:, k, :]
        wr = Ring([(self.sb(es, "adawb", [128, FW], BF16), [Tok(), Tok(), Tok()]) for _ in range(2 if mode == "dma" else 3)])
        if mode != "dma":
            sr = self.ring(es, "adaws", 2, [128, FW], F32)
        br = self.ring(es, "adabb", 2, [2, SW], F32)
        pm = self.ring(es, "pmb", 1 if mode == "dma" else 2, [2, SW], F32, psum=True)
        st = self.ring(es, "mstb", 2, [2, SW], F32)
        t_modd = [Tok() for _ in range(nsl)]
        c1 = (FW // 3) // 64 * 64
        c2 = 2 * c1
        nh = 0
        for n in range(nsl):
            w, t_w = wr.next()
            src = d["ada_w"][l, :, n * SW:(n + 1) * SW].rearrange("(k p) c -> p k c", p=128)
            if mode == "hw" or (mode == "mixed" and n % 3 != 0):
                sf, t_sf = sr.next()
                P.dma("sp" if nh % 2 == 0 else "act", sf[:].rearrange("p (k c) -> p k c", c=SW), src, writes=[t_sf])
                nh += 1
                P.cp("dve", w[:, 0:c1], sf[:, 0:c1], reads=[t_sf], writes=[t_w[0]])
                P.cp("act", w[:, c1:c2], sf[:, c1:c2], reads=[t_sf], writes=[t_w[1]])
                P.cp("pool", w[:, c2:FW], sf[:, c2:FW], reads=[t_sf], writes=[t_w[2]])
            else:
                P.dma("pool", w[:].rearrange("p (k c) -> p k c", c=SW), src, writes=t_w)
            b, t_b = br.next()
            P.dma("sp", b[:], d["ada_b"][l, n * SW:(n + 1) * SW].partition_broadcast(2), writes=[t_b])
            p, t_p = pm.next()
            for k in range(16):
                P.mm(p[:], lhs(k), w[:, k * SW:(k + 1) * SW], k == 0, k == 15, reads=[self.t_condTb] + t_w, writes=[t_p])
            s_, t_s = st.next()
            P.tt("dve", s_[:], p[:], b[:], ALU.add, reads=[t_p, t_b], writes=[t_s])
            P.dma("sp", self.modd[l, :, n * SW:(n + 1) * SW], s_[:], reads=[t_s], writes=[t_modd[n]])
            yield
        st2 = self.ring(es, "mst2b", 2, [96, 128], F32)
        for s in range(2):
            t2, t_t2 = st2.next()
            P.dma("sp", t2[:], self.modd[l, s, :].rearrange("(r c) -> r c", c=128), reads=t_modd, writes=[t_t2])
            P.tr(self._pc2[:, 0:96], t2[0:96, :], c["ident_f"][0:96, 0:96], reads=[t_t2, ct["ident_f"]], writes=[self._t_pc2])
            P.cp("dve", self.modT[l][s][:], self._pc2[:, 0:96], reads=[self._t_pc2], writes=[self.t_modT[l][s]])
            for kind, (sc0, gkey) in enumerate([(16, "gmix"), (64, "gffn")]):
                o = ((l * 2 + s) * 2 + kind) * 16
                P.stt(self.acols[:, o:o + 16], self.modT[l][s][:, sc0:sc0 + 16], 1.0, self.col(gkey, l * 16, 16),
                      ALU.add, ALU.mult, reads=[self.t_modT[l][s], self.t_cols], writes=[self.t_acols])
        yield

    def a_col(self, l, s, kind, k):
        o = ((l * 2 + s) * 2 + kind) * 16 + k
        return self.acols[:, o:o + 1]

    def b_col(self, l, s, kind, k):
        o = (0 if kind == 0 else 48) + k
        return self.modT[l][s][:, o:o + 1]

    def src_rows(self, l, i):
        if l == 0:
            if i < 2:
                return self.d["ctx"][i * 128:(i + 1) * 128, :], []
            return self.d["x"][(i - 2) * 128:(i - 1) * 128, :], []
        return self.xres[i * 128:(i + 1) * 128, :], self.t_xres[i]

    def norm_tile(self, es_rings, l, i, kind, src, src_toks, dst, dst_col, dst_tok, dstw):
        P = self.P
        xr, jr, ssr, xnr, ptr = es_rings
        s = 1 if i < 2 else 0
        xt, t_xt = xr.next()
        P.dma("sp" if i % 2 == 0 else "pool", xt[:], src, reads=src_toks, writes=[t_xt])
        jk, t_jk = jr.next()
        ss, t_ss = ssr.next()
        P.act(jk[:], xt[:], AF.Square, reads=[t_xt], writes=[t_jk])
        import os
        step = int(os.environ.get("NT_STEP", "9"))
        if step < 2:
            return
        P.rsum(ss[:, 0:1], jk[:], reads=[t_jk], writes=[t_ss])
        if step < 3:
            return
        P.act(ss[:, 1:2], ss[:, 0:1], AF.Sqrt, reads=[t_ss], writes=[t_ss], scale=1.0 / D, bias=NORM_EPS)
        P.recip(ss[:, 2:3], ss[:, 1:2], reads=[t_ss], writes=[t_ss])
        if step < 4:
            return
        xn, t_xn = xnr.next()
        P.ts("dve", xn[:], xt[:], ss[:, 2:3], None, ALU.mult, reads=[t_xt, t_ss], writes=[t_xn])
        if step < 5:
            return
        pt, t_pt = ptr.next()
        for k in range(16):
            P.tr(pt[:, k * 128:(k + 1) * 128], xn[:, k * 128:(k + 1) * 128], self.c["ident_b"][:],
                 reads=[t_xn, self.ct["ident_b"]], writes=[t_pt])
        if step < 6:
            return
        for k in range(16):
            o = dst[:, k * dstw + dst_col:k * dstw + dst_col + 128]
            a, b = self.a_col(l, s, kind, k), self.b_col(l, s, kind, k)
            if k < 8:
                P.act(o, pt[:, k * 128:(k + 1) * 128], AF.Identity, reads=[t_pt, self.t_acols, self.t_modT[l][s]],
                      writes=[dst_tok[0]], scale=a, bias=b)
            else:
                P.ts("dve", o, pt[:, k * 128:(k + 1) * 128], a, b, ALU.mult, ALU.add,
                     reads=[t_pt, self.t_acols, self.t_modT[l][s]], writes=[dst_tok[1]])
        return ss, t_ss

    def norm_rings(self, es):
        return (self.ring(es, "nx", 3, [128, D], F32), self.ring(es, "njk", 2, [128, D], F32),
                self.ring(es, "nss", 4, [128, 4], F32), self.ring(es, "nxn", 2, [128, D], BF16),
                self.ring(es, "npt", 2, [128, D], BF16, psum=True))

    def phaseA(self, l, hT, t_hT):
        with contextlib.ExitStack() as es:
            rings = self.norm_rings(es)
            for i in range(NT):
                src, toks = self.src_rows(l, i)
                self.norm_tile(rings, l, i, 0, src, toks, hT, i * 128, t_hT[i], T)
            if ("hT%d" % l) in self.taps:
                tp = self.tap("tap_hT%d" % l, [D, T], BF16)
                for k in range(16):
                    self.P.dma("sp", tp[k * 128:(k + 1) * 128, :], hT[:, k * T:(k + 1) * T], reads=self.hT_toks(t_hT, 0, T))
            self.P.barrier()

    def hT_toks(self, t_hT, c0, n):
        return [t for pr in t_hT[c0 // 128:(c0 + n) // 128] for t in pr]

    def load_w(self, wb, t_wb, w, c0, ncol, nk=16, q="pool"):
        self.P.dma(q, wb[:, 0:nk * ncol].rearrange("p (k c) -> p k c", c=ncol),
                   w[0:nk * 128, c0:c0 + ncol].rearrange("(k p) c -> p k c", p=128), writes=[t_wb])

    def even_qkv(self, hT, t_hT, qT, t_qT, kT, t_kT, Vsb, t_V):
        P, d, c, ct = self.P, self.d, self.c, self.ct
        with contextlib.ExitStack() as es:
            wr = self.ring(es, "wqkv", 2, [128, 16 * 512], BF16)
            cosT = self.sb(es, "cosT", [128, L], F32)
            sinT = self.sb(es, "sinT", [128, L], F32)
            t_rope = Tok()
            P.dma("sp", cosT[:], d["cos128"][:, :], writes=[t_rope])
            P.dma("sp", sinT[:], d["sin128"][:, :], writes=[t_rope])
            pz = self.ring(es, "pz", 3, [128, 512], F32, psum=True)
            pss = self.ring(es, "pss", 2, [128, 512], F32, psum=True)
            ppq = self.ring(es, "ppq", 2, [128, 512], F32, psum=True)
            wk = {nme: self.ring(es, nme, 3 if nme in ("qf", "rt") else 2, [128, 512], BF16 if nme in ("q2", "qn") else F32)
                  for nme in ("qf", "q2", "rt", "qn", "t1", "t2")}

            def qk_process(z, t_z, n, c0, gcol, out, t_out):
                qf, t_qf = wk["qf"].next()
                P.cp("act", qf[:, :n], z[:, :n], reads=[t_z], writes=[t_qf])
                q2, t_q2 = wk["q2"].next()
                P.act(q2[:, :n], z[:, :n], AF.Square, reads=[t_z], writes=[t_q2])
                ssp, t_ssp = pss.next()
                P.mm(ssp[:, :n], c["ones_b"][:], q2[:, :n], True, True, reads=[t_q2, ct["ones_b"]], writes=[t_ssp])
                rt, t_rt = wk["rt"].next()
                P.act(rt[:, :n], ssp[:, :n], AF.Sqrt, reads=[t_ssp], writes=[t_rt], scale=1.0 / 128, bias=NORM_EPS)
                P.recip(rt[:, :n], rt[:, :n], reads=[t_rt], writes=[t_rt])
                if c0 < LC:
                    P.stt(out, qf[:, :n], gcol, rt[:, :n], ALU.mult, ALU.mult, reads=[t_qf, t_rt, self.t_cols], writes=[t_out])
                    return
                qn, t_qn = wk["qn"].next()
                P.stt(qn[:, :n], qf[:, :n], gcol, rt[:, :n], ALU.mult, ALU.mult, reads=[t_qf, t_rt, self.t_cols], writes=[t_qn])
                pq, t_pq = ppq.next()
                P.mm(pq[:, :n], c["perm128b"][:], qn[:, :n], True, True, reads=[t_qn, ct["perm128b"]], writes=[t_pq])
                t1, t_t1 = wk["t1"].next()
                P.tt("pool", t1[:, :n], qn[:, :n], cosT[:, c0 - LC:c0 - LC + n], ALU.mult, reads=[t_qn, t_rope], writes=[t_t1])
                t2, t_t2 = wk["t2"].next()
                P.tt("dve", t2[:, :n], pq[:, :n], sinT[:, c0 - LC:c0 - LC + n], ALU.mult, reads=[t_pq, t_rope], writes=[t_t2])
                P.tt("pool", out, t1[:, :n], t2[:, :n], ALU.add, reads=[t_t1, t_t2], writes=[t_out])

            def proj(wb, t_wb, wc0, c0, n):
                z, t_z = pz.next()
                for k in range(16):
                    P.mm(z[:, :n], wb[:, k * 512 + wc0:k * 512 + wc0 + 128], hT[:, k * T + c0:k * T + c0 + n], k == 0, k == 15,
                         reads=[t_wb] + self.hT_toks(t_hT, c0, n), writes=[t_z])
                return z, t_z

            wb, t_wb = wr.next()
            self.load_w(wb, t_wb, d["e_w_in"], 3072, 512)
            for hk in range(2):
                for ci, (c0, n) in enumerate(COLT):
                    z, t_z = proj(wb, t_wb, hk * 128, c0, n)
                    qk_process(z, t_z, n, c0, self.col("kng"), kT[:, hk * T + c0:hk * T + c0 + n], t_kT[hk][ci])
            for i in range(NT):
                z, t_z = pz.next()
                for k in range(16):
                    P.mm(z[:, 0:256], hT[:, k * T + i * 128:k * T + (i + 1) * 128], wb[:, k * 512 + 256:k * 512 + 512], k == 0, k == 15,
                         reads=[t_wb] + list(t_hT[i]), writes=[t_z])
                P.cp("act", Vsb[:, i * 256:(i + 1) * 256], z[:, 0:256], reads=[t_z], writes=[t_V[i]])
            for g in range(2):
                wb, t_wb = wr.next()
                self.load_w(wb, t_wb, d["e_w_in"], 2048 + g * 512, 512)
                for hh in range(4):
                    h = g * 4 + hh
                    for ci, (c0, n) in enumerate(COLT):
                        z, t_z = proj(wb, t_wb, hh * 128, c0, n)
                        qk_process(z, t_z, n, c0, self.col("qng"), qT[:, h * T + c0:h * T + c0 + n], t_qT[h][ci])
            if "qk0" in self.taps:
                tq = self.tap("tap_qT", [1024, T], BF16)
                tk_ = self.tap("tap_kT", [256, T], BF16)
                tv = self.tap("tap_V", [T, 256], BF16)
                for h in range(8):
                    P.dma("sp", tq[h * 128:(h + 1) * 128, :], qT[:, h * T:(h + 1) * T], reads=t_qT[h])
                for h in range(2):
                    P.dma("sp", tk_[h * 128:(h + 1) * 128, :], kT[:, h * T:(h + 1) * T], reads=t_kT[h])
                for i in range(NT):
                    P.dma("sp", tv[i * 128:(i + 1) * 128, :], Vsb[:, i * 256:(i + 1) * 256], reads=[t_V[i]])
            P.barrier()

    def attention(self, es, n_heads, q_tiles, s_mm, v_ap, scale, out_row0, prep=None, pS=None):
        P, c, ct = self.P, self.c, self.ct
        if pS is None:
            pS = self.ring(es, "pS", 3, [128, 512], F32, psum=True)
        pO = self.ring(es, "pO", 2, [128, 512], F32, psum=True)
        pL = self.ring(es, "pL", 2, [128, 512], F32, psum=True)
        PT = self.ring(es, "PT", 4, [128, 512], BF16)
        rc = self.ring(es, "rc", 2, [128, 512], F32)
        ao = self.ring(es, "ao", 2, [128, 512], BF16)
        for h in range(n_heads):
            if prep is not None:
                prep(h)
            for (ci, c0, n, keys) in q_tiles:
                O, t_O = pO.next()
                Ls, t_L = pL.next()
                Sq = []
                LOOK = 2
                for j in range(min(LOOK, len(keys))):
                    S, t_S = pS.next()
                    s_mm(h, S, t_S, keys[j], c0, n)
                    Sq.append((S, t_S))
                for j, kc in enumerate(keys):
                    if j + LOOK < len(keys):
                        S, t_S = pS.next()
                        s_mm(h, S, t_S, keys[j + LOOK], c0, n)
                        Sq.append((S, t_S))
                    S, t_S = Sq.pop(0)
                    pt, t_pt = PT.next()
                    P.act(pt[:, :n], S[:, :n], AF.Exp, reads=[t_S], writes=[t_pt], scale=scale)
                    va, vt = v_ap(h, kc)
                    P.mm(O[:, :n], va, pt[:, :n], j == 0, j == len(keys) - 1, reads=[t_pt] + vt, writes=[t_O])
                    P.mm(Ls[:, :n], c["ones_b"][:], pt[:, :n], j == 0, j == len(keys) - 1, reads=[t_pt, ct["ones_b"]], writes=[t_L])
                r, t_r = rc.next()
                P.recip(r[:, :n], Ls[:, :n], reads=[t_L], writes=[t_r])
                a, t_a = ao.next()
                P.tt("dve", a[:, :n], O[:, :n], r[:, :n], ALU.mult, reads=[t_O, t_r], writes=[t_a])
                rk = (out_row0 + h * 128) // 128
                P.dma("sp", self.mixd[out_row0 + h * 128:out_row0 + (h + 1) * 128, c0:c0 + n], a[:, :n], reads=[t_a],
                      writes=[self.t_mixd[rk][ci]])

    def even_attention(self, qT, t_qT, kT, t_kT, Vsb, t_V):
        P = self.P
        with contextlib.ExitStack() as es:
            def s_mm(h, S, t_S, kc, c0, n):
                kvh = h // 4
                ci = [x[0] for x in COLT].index(c0)
                kci = 0 if kc < 2 else 1 + (kc - 2) // 4
                P.mm(S[:, :n], kT[:, kvh * T + kc * 128:kvh * T + (kc + 1) * 128], qT[:, h * T + c0:h * T + c0 + n], True, True,
                     reads=[t_kT[kvh][kci], t_qT[h][ci]], writes=[t_S])

            def v_ap(h, kc):
                kvh = h // 4
                return Vsb[:, kc * 256 + kvh * 128:kc * 256 + (kvh + 1) * 128], [t_V[kc]]

            q_tiles = [(0, 0, 256, [0, 1])] + [(ci, c0, n, list(range(NT))) for ci, (c0, n) in enumerate(COLT) if ci > 0]
            self.attention(es, 8, q_tiles, s_mm, v_ap, 128 ** -0.5, 1024)
            P.barrier()

    def even_conv(self, hT, t_hT):
        P, d, c, ct = self.P, self.d, self.c, self.ct
        W = 15 + LC + 15 + L + 15
        boff = lambda c0: 15 + c0 if c0 < LC else 30 + c0
        with contextlib.ExitStack() as es:
            wr = self.ring(es, "wconv", 3, [128, 16 * 512], BF16)
            up = self.ring(es, "upad", 2, [128, W], BF16)
            for u, t_u in up.items:
                P.memset("pool", u[:], 0.0, writes=[t_u])
            dgr = self.ring(es, "dg", 2, [128, 31 * 128], BF16)
            pa = self.ring(es, "pa", 2, [128, 512], F32, psum=True)
            pg = self.ring(es, "pg", 2, [128, 512], F32, psum=True)
            pcv = self.ring(es, "pcv", 2, [128, 512], F32, psum=True)
            sgr = self.ring(es, "sg", 2, [128, 512], F32)
            cst = self.ring(es, "cst", 3, [128, 512], F32)
            gen_live = False
            for half in range(2):
                wa, t_wa = wr.next()
                self.load_w(wa, t_wa, d["e_w_in"], half * 512, 512)
                wg, t_wg = wr.next()
                self.load_w(wg, t_wg, d["e_w_in"], 1024 + half * 512, 512)
                for cc in range(4):
                    ch = half * 4 + cc
                    u, t_u = up.next()
                    dg, t_dg = dgr.next()
                    wcol = lambda k: (self.col("dww", k * 8 + ch) if k * 8 + ch < 128 else self.col("dww2", k * 8 + ch - 128))
                    for k in range(31):
                        P.ts("pool" if k % 2 else "dve", dg[:, k * 128:(k + 1) * 128], c["ident_f"][:], wcol(k), None, ALU.mult,
                             reads=[ct["ident_f"], self.t_cols], writes=[t_dg])
                    for (c0, n) in COLT:
                        za, t_za = pa.next()
                        zg, t_zg = pg.next()
                        for k in range(16):
                            P.mm(za[:, :n], wa[:, k * 512 + cc * 128:k * 512 + (cc + 1) * 128], hT[:, k * T + c0:k * T + c0 + n],
                                 k == 0, k == 15, reads=[t_wa] + self.hT_toks(t_hT, c0, n), writes=[t_za])
                        for k in range(16):
                            P.mm(zg[:, :n], wg[:, k * 512 + cc * 128:k * 512 + (cc + 1) * 128], hT[:, k * T + c0:k * T + c0 + n],
                                 k == 0, k == 15, reads=[t_wg] + self.hT_toks(t_hT, c0, n), writes=[t_zg])
                        sg, t_sg = sgr.next()
                        P.act(sg[:, :n], zg[:, :n], AF.Sigmoid, reads=[t_zg], writes=[t_sg])
                        P.tt("dve", u[:, boff(c0):boff(c0) + n], za[:, :n], sg[:, :n], ALU.mult, reads=[t_za, t_sg], writes=[t_u])
                        if gen_live:
                            gen_live = next(gen, "end") != "end"
                    for (c0, n) in COLT:
                        p0 = boff(c0) - 15
                        cv, t_cv = pcv.next()
                        for k in range(31):
                            P.mm(cv[:, :n], dg[:, k * 128:(k + 1) * 128], u[:, p0 + k:p0 + k + n], k == 0, k == 30, reads=[t_dg, t_u], writes=[t_cv])
                        st, t_st = cst.next()
                        P.act(st[:, :n], cv[:, :n], AF.Identity, reads=[t_cv, self.t_cols], writes=[t_st], bias=self.col("dwb", ch))
                        P.dma("sp", self.convd[ch * 128:(ch + 1) * 128, c0:c0 + n], st[:, :n], reads=[t_st], writes=[self.t_convd[ch]])
            while gen_live:
                gen_live = next(gen, "end") != "end"
            P.barrier()
        with contextlib.ExitStack() as es:
            cvr = self.ring(es, "cv", 2, [128, 8 * 512], F32)
            cvbr = self.ring(es, "cvb", 2, [128, 8 * 512], BF16)
            sqr = self.ring(es, "csq", 2, [128, 8 * 512], BF16)
            pS = self.ring(es, "lnS", 2, [128, 512], F32, psum=True)
            pQ = self.ring(es, "lnQ", 2, [128, 512], F32, psum=True)
            mr = self.ring(es, "lnm", 2, [128, 512], F32)
            vr = self.ring(es, "lnv", 2, [128, 512], F32)
            tr_ = self.ring(es, "lnt", 3, [128, 512], F32)
            yr = self.ring(es, "lny", 3, [128, 512], BF16)
            for ci, (c0, n) in enumerate(COLT):
                cv, t_cv = cvr.next()
                P.dma("sp", cv[:].rearrange("p (c t) -> p c t", t=512)[:, :, 0:n],
                      self.convd[:, c0:c0 + n].rearrange("(c p) t -> p c t", p=128), reads=self.t_convd, writes=[t_cv])
                cvb, t_cvb = cvbr.next()
                P.dma("pool", cvb[:].rearrange("p (c t) -> p c t", t=512)[:, :, 0:n],
                      self.convd[:, c0:c0 + n].rearrange("(c p) t -> p c t", p=128), reads=self.t_convd, writes=[t_cvb])
                sq, t_sq = sqr.next()
                for ch in range(8):
                    P.act(sq[:, ch * 512:ch * 512 + n], cv[:, ch * 512:ch * 512 + n], AF.Square, reads=[t_cv], writes=[t_sq])
                S, t_S = pS.next()
                Q, t_Q = pQ.next()
                for ch in range(8):
                    P.mm(S[:, :n], c["ones_b"][:], cvb[:, ch * 512:ch * 512 + n], ch == 0, ch == 7, reads=[t_cvb, ct["ones_b"]], writes=[t_S])
                for ch in range(8):
                    P.mm(Q[:, :n], c["ones_b"][:], sq[:, ch * 512:ch * 512 + n], ch == 0, ch == 7, reads=[t_sq, ct["ones_b"]], writes=[t_Q])
                m, t_m = mr.next()
                P.act(m[:, :n], S[:, :n], AF.Identity, reads=[t_S], writes=[t_m], scale=1.0 / 1024)
                v, t_v = vr.next()
                P.tt("pool", v[:, :n], m[:, :n], m[:, :n], ALU.mult, reads=[t_m], writes=[t_v])
                P.stt(v[:, :n], Q[:, :n], 1.0 / 1024, v[:, :n], ALU.mult, ALU.subtract, reads=[t_Q, t_v], writes=[t_v])
                P.act(v[:, :n], v[:, :n], AF.Sqrt, reads=[t_v], writes=[t_v], bias=LN_EPS)
                P.recip(v[:, :n], v[:, :n], reads=[t_v], writes=[t_v])
                for ch in range(8):
                    t1, t_t1 = tr_.next()
                    P.tt("dve", t1[:, :n], cv[:, ch * 512:ch * 512 + n], m[:, :n], ALU.subtract, reads=[t_cv, t_m], writes=[t_t1])
                    P.tt("pool", t1[:, :n], t1[:, :n], v[:, :n], ALU.mult, reads=[t_t1, t_v], writes=[t_t1])
                    y, t_y = yr.next()
                    P.act(y[:, :n], t1[:, :n], AF.Silu, reads=[t_t1, self.t_cols], writes=[t_y], scale=self.col("lng", ch), bias=self.col("lnb", ch))
                    P.dma("sp", self.mixd[ch * 128:(ch + 1) * 128, c0:c0 + n], y[:, :n], reads=[t_y], writes=[self.t_mixd[ch][ci]])
            P.barrier()

    def out_proj(self, l, w_out, tiles, with_mods=None):
        P, d = self.P, self.d
        with contextlib.ExitStack() as es:
            mixT = self.sb(es, "mixT", [128, 16 * T], BF16)
            t_mixc = [Tok() for _ in COLT]
            qi = 0
            for ci, (c0, n) in enumerate(COLT):
                if c0 < LC and 0 not in tiles:
                    continue
                P.dma("sp" if qi % 2 == 0 else "act", mixT[:, :].rearrange("p (k t) -> p k t", t=T)[:, :, c0:c0 + n],
                      self.mixd[:, c0:c0 + n].rearrange("(k p) t -> p k t", p=128), reads=[self.t_mixd[k][ci] for k in range(16)], writes=[t_mixc[ci]])
                qi += 1
            t_mix_of = lambda i: t_mixc[0 if i < 2 else 1 + (i - 2) // 4]
            gbc = self.sb(es, "gbc", [128, 2 * D], F32)
            t_gbc = Tok()
            for s in range(2):
                P.dma("sp", gbc[:, s * D:(s + 1) * D], self.modd[l, s, 2 * D:3 * D].partition_broadcast(128), writes=[t_gbc])
            wr = self.ring(es, "wout", 2, [128, 16 * 512], BF16)
            po = self.ring(es, "po", 4, [128, 512], F32, psum=True)
            xo = self.ring(es, "xo", 4, [128, 512], F32)
            tm = self.ring(es, "otm", 3, [128, 512], F32)
            wbs = {}

            def issue(ns_):
                wb_, t_wb_ = wr.next()
                self.load_w(wb_, t_wb_, w_out, ns_ * 512, 512)
                wbs[ns_] = (wb_, t_wb_)

            gen = None
            if with_mods is not None:
                self._pc2 = self.ps(es, "pc2b", [128, 128], F32)
                self._t_pc2 = Tok()
                gen = self.mods_gen(es, with_mods, mode="hw", SW=256)
            issue(0)
            for ns in range(4):
                if ns + 1 < 4:
                    issue(ns + 1)
                wb, t_wb = wbs[ns]
                for i in tiles:
                    if gen is not None and next(gen, "end") == "end":
                        gen = None
                    s = 1 if i < 2 else 0
                    o, t_o = po.next()
                    for k in range(16):
                        P.mm(o[:], mixT[:, k * T + i * 128:k * T + (i + 1) * 128], wb[:, k * 512:(k + 1) * 512], k == 0, k == 15,
                             reads=[t_mix_of(i), t_wb], writes=[t_o])
                    src, toks = self.src_rows(l, i)
                    x, t_x = xo.next()
                    P.dma("sp", x[:], src[:, ns * 512:(ns + 1) * 512], reads=([toks[ns]] if toks else []), writes=[t_x])
                    tmp, t_tmp = tm.next()
                    P.tt("dve", tmp[:], o[:], gbc[:, s * D + ns * 512:s * D + (ns + 1) * 512], ALU.mult, reads=[t_o, t_gbc], writes=[t_tmp])
                    P.tt("pool", x[:], x[:], tmp[:], ALU.add, reads=[t_x, t_tmp], writes=[t_x])
                    P.dma("act", self.xres[i * 128:(i + 1) * 128, ns * 512:(ns + 1) * 512], x[:], reads=[t_x], writes=[self.t_xres[i][ns]])
            while gen is not None:
                if next(gen, "end") == "end":
                    gen = None
            if ("xmid%d" % l) in self.taps:
                tp = self.tap("tap_xmid%d" % l, [T, D])
                for i in tiles:
                    P.dma("sp", tp[i * 128:(i + 1) * 128, :], self.xres[i * 128:(i + 1) * 128, :], reads=self.t_xres[i])
            P.barrier()

    def ffn(self, l, blocks):
        P, d = self.P, self.d
        BW = 768
        wg_d, wu_d, wd_d = d["ffn_w_gate"][l], d["ffn_w_up"][l], d["ffn_w_down"][l]
        with contextlib.ExitStack() as es:
            h2T = self.sb(es, "h2T", [128, 16 * BW], BF16)
            actT = self.sb(es, "actT", [128, NHC * BW], BF16)
            gbc = self.sb(es, "gfbc", [128, 2 * D], F32)
            t_gbc = Tok()
            for s in range(2):
                P.dma("sp", gbc[:, s * D:(s + 1) * D], self.modd[l, s, 5 * D:6 * D].partition_broadcast(128), writes=[t_gbc])
            xr = self.ring(es, "fx", 2, [128, D], F32)
            jr = self.ring(es, "fjk", 1, [128, D], BF16)
            ssr = self.ring(es, "fss", 4, [128, 4], F32)
            xnr = self.ring(es, "fxn", 1, [128, D], BF16)
            banks = [(self.ps(es, "fbank", [128, 512], F32), Tok()) for _ in range(8)]
            ptr = Ring(banks[0:2])
            wgr = self.ring(es, "wg", 2, [128, 16 * 256], BF16)
            wur = self.ring(es, "wu", 2, [128, 16 * 256], BF16)
            wdr = self.ring(es, "wd", 2, [128, 11 * 512], BF16)
            pG = Ring(banks[0:2])
            pU = Ring(banks[2:4])
            pD = banks[2:8]
            sgr = self.ring(es, "fsg", 2, [128, 512], F32)
            xo = self.ring(es, "fxo", 2, [128, 512], F32)
            tm = self.ring(es, "ftm", 2, [128, 512], F32)
            for blk in blocks:
                nb = len(blk)
                t_h2 = [(Tok(), Tok()) for _ in range(nb)]
                t_act = [[Tok() for _ in range(nb)] for _ in range(NHC)]
                ctiles = []
                c0 = 0
                while c0 < nb * 128:
                    n = min(512, nb * 128 - c0)
                    ctiles.append((c0, n))
                    c0 += n
                for j, i in enumerate(blk):
                    s = 1 if i < 2 else 0
                    xt, t_xt = xr.next()
                    P.dma("sp", xt[:], self.xres[i * 128:(i + 1) * 128, :], reads=self.t_xres[i], writes=[t_xt])
                    jk, t_jk = jr.next()
                    ss, t_ss = ssr.next()
                    P.act(jk[:], xt[:], AF.Square, reads=[t_xt], writes=[t_jk])
                    P.rsum(ss[:, 0:1], jk[:], reads=[t_jk], writes=[t_ss])
                    P.act(ss[:, 1:2], ss[:, 0:1], AF.Sqrt, reads=[t_ss], writes=[t_ss], scale=1.0 / D, bias=NORM_EPS)
                    P.recip(ss[:, 2:3], ss[:, 1:2], reads=[t_ss], writes=[t_ss])
                    xn, t_xn = xnr.next()
                    P.ts("dve", xn[:], xt[:], ss[:, 2:3], None, ALU.mult, reads=[t_xt, t_ss], writes=[t_xn])
                    for kq in range(4):
                        pt, t_pt = ptr.next()
                        ptb = pt[:, 0:256].bitcast(BF16)
                        for kk in range(4):
                            k = kq * 4 + kk
                            P.tr(ptb[:, kk * 128:(kk + 1) * 128], xn[:, k * 128:(k + 1) * 128], self.c["ident_b"][:],
                                 reads=[t_xn, self.ct["ident_b"]], writes=[t_pt])
                        for kk in range(4):
                            k = kq * 4 + kk
                            o = h2T[:, k * BW + j * 128:k * BW + (j + 1) * 128]
                            a, b = self.a_col(l, s, 1, k), self.b_col(l, s, 1, k)
                            if kq % 2 == 0:
                                P.act(o, ptb[:, kk * 128:(kk + 1) * 128], AF.Identity, reads=[t_pt, self.t_acols, self.t_modT[l][s]],
                                      writes=[t_h2[j][0]], scale=a, bias=b)
                            else:
                                P.ts("dve", o, ptb[:, kk * 128:(kk + 1) * 128], a, b, ALU.mult, ALU.add,
                                     reads=[t_pt, self.t_acols, self.t_modT[l][s]], writes=[t_h2[j][1]])
                for jp in range(NHC // 2):
                    wg, t_wg = wgr.next()
                    self.load_w(wg, t_wg, wg_d, jp * 256, 256)
                    wu, t_wu = wur.next()
                    self.load_w(wu, t_wu, wu_d, jp * 256, 256)
                    for jj in range(2):
                        hc = jp * 2 + jj
                        for (c0, n) in ctiles:
                            ht = [t for pr in t_h2[c0 // 128:(c0 + n) // 128] for t in pr]
                            G, t_G = pG.next()
                            U, t_U = pU.next()
                            for k in range(16):
                                P.mm(G[:, :n], wg[:, k * 256 + jj * 128:k * 256 + (jj + 1) * 128], h2T[:, k * BW + c0:k * BW + c0 + n],
                                     k == 0, k == 15, reads=[t_wg] + ht, writes=[t_G])
                            for k in range(16):
                                P.mm(U[:, :n], wu[:, k * 256 + jj * 128:k * 256 + (jj + 1) * 128], h2T[:, k * BW + c0:k * BW + c0 + n],
                                     k == 0, k == 15, reads=[t_wu] + ht, writes=[t_U])
                            sg, t_sg = sgr.next()
                            P.act(sg[:, :n], G[:, :n], AF.Silu, reads=[t_G], writes=[t_sg])
                            P.tt("dve", actT[:, hc * BW + c0:hc * BW + c0 + n], U[:, :n], sg[:, :n], ALU.mult, reads=[t_U, t_sg],
                                 writes=t_act[hc][c0 // 128:(c0 + n) // 128])
                for ns in range(4):
                    for pc in range(4):
                        wd, t_wd = wdr.next()
                        P.dma("pool", wd[:].rearrange("p (j c) -> p j c", c=512),
                              wd_d[pc * 11 * 128:(pc + 1) * 11 * 128, ns * 512:(ns + 1) * 512].rearrange("(j p) c -> p j c", p=128),
                              writes=[t_wd])
                        for j, i in enumerate(blk):
                            o, t_o = pD[j]
                            for jj in range(11):
                                hc = pc * 11 + jj
                                P.mm(o[:], actT[:, hc * BW + j * 128:hc * BW + (j + 1) * 128], wd[:, jj * 512:(jj + 1) * 512],
                                     hc == 0, hc == NHC - 1, reads=[t_act[hc][j], t_wd], writes=[t_o])
                    for j, i in enumerate(blk):
                        s = 1 if i < 2 else 0
                        o, t_o = pD[j]
                        x, t_x = xo.next()
                        P.dma("sp", x[:], self.xres[i * 128:(i + 1) * 128, ns * 512:(ns + 1) * 512], reads=[self.t_xres[i][ns]], writes=[t_x])
                        tmp, t_tmp = tm.next()
                        P.tt("dve", tmp[:], o[:], gbc[:, s * D + ns * 512:s * D + (ns + 1) * 512], ALU.mult, reads=[t_o, t_gbc], writes=[t_tmp])
                        P.tt("pool", x[:], x[:], tmp[:], ALU.add, reads=[t_x, t_tmp], writes=[t_x])
                        P.dma("sp", self.xres[i * 128:(i + 1) * 128, ns * 512:(ns + 1) * 512], x[:], reads=[t_x], writes=[self.t_xres[i][ns]])
            if ("x%d" % l) in self.taps:
                tp = self.tap("tap_x%d" % l, [T, D])
                for blk in blocks:
                    for i in blk:
                        P.dma("sp", tp[i * 128:(i + 1) * 128, :], self.xres[i * 128:(i + 1) * 128, :], reads=self.t_xres[i])
            P.barrier()

    def final(self):
        P, d = self.P, self.d
        with contextlib.ExitStack() as es:
            gb = self.sb(es, "gfin", [128, D], F32)
            t_gb = Tok()
            P.dma("sp", gb[:], d["final_norm_g"][0, :].partition_broadcast(128), writes=[t_gb])
            xr = self.ring(es, "lx", 3, [128, D], F32)
            jr = self.ring(es, "ljk", 2, [128, D], F32)
            ssr = self.ring(es, "lss", 4, [128, 4], F32)
            for i in range(2, NT):
                xt, t_xt = xr.next()
                P.dma("sp", xt[:], self.xres[i * 128:(i + 1) * 128, :], reads=self.t_xres[i], writes=[t_xt])
                jk, t_jk = jr.next()
                ss, t_ss = ssr.next()
                P.act(jk[:], xt[:], AF.Square, reads=[t_xt], writes=[t_jk])
                P.rsum(ss[:, 0:1], jk[:], reads=[t_jk], writes=[t_ss])
                P.act(ss[:, 1:2], ss[:, 0:1], AF.Sqrt, reads=[t_ss], writes=[t_ss], scale=1.0 / D, bias=NORM_EPS)
                P.recip(ss[:, 2:3], ss[:, 1:2], reads=[t_ss], writes=[t_ss])
                P.stt(xt[:], xt[:], ss[:, 2:3], gb[:], ALU.mult, ALU.mult, reads=[t_xt, t_ss, t_gb], writes=[t_xt])
                P.dma("act", d["out"][(i - 2) * 128:(i - 1) * 128, :], xt[:], reads=[t_xt])
            P.barrier()


    def odd_hyena_proj(self, hT, t_hT):
        P, d, c, ct = self.P, self.d, self.c, self.ct
        LT = COLT[1:]
        with contextlib.ExitStack() as es:
            wr = self.ring(es, "whp", 2, [128, 16 * 512], BF16)
            zr = self.ring(es, "zraw", 2, [128, L + 2], F32)
            for z, t_z in zr.items:
                P.memset("pool", z[:], 0.0, writes=[t_z])
            zcr = self.ring(es, "zc", 8, [128, L], F32)
            pz = self.ring(es, "hpz", 3, [128, 512], F32, psum=True)
            ptr = self.ring(es, "hpt", 2, [128, 512], F32, psum=True)
            stg = self.ring(es, "hstg", 3, [128, 512], F32)
            for g in range(6):
                wb, t_wb = wr.next()
                self.load_w(wb, t_wb, d["o_w_in"], g * 512, 512)
                zcs = []
                for cc in range(4):
                    ch = g * 4 + cc
                    zraw, t_zraw = zr.next()
                    for (c0, n) in LT:
                        z, t_z = pz.next()
                        for k in range(16):
                            P.mm(z[:, :n], wb[:, k * 512 + cc * 128:k * 512 + (cc + 1) * 128], hT[:, k * T + c0:k * T + c0 + n],
                                 k == 0, k == 15, reads=[t_wb] + self.hT_toks(t_hT, c0, n), writes=[t_z])
                        P.cp("act", zraw[:, 1 + c0 - LC:1 + c0 - LC + n], z[:, :n], reads=[t_z], writes=[t_zraw])
                    zc, t_zc = zcr.next()
                    w = lambda k: self.col("shw", k * 24 + ch)
                    P.ts("dve", zc[:], zraw[:, 0:L], w(0), self.col("shb", ch), ALU.mult, ALU.add, reads=[t_zraw, self.t_cols], writes=[t_zc])
                    P.stt(zc[:], zraw[:, 1:L + 1], w(1), zc[:], ALU.mult, ALU.add, reads=[t_zraw, t_zc, self.t_cols], writes=[t_zc])
                    P.stt(zc[:], zraw[:, 2:L + 2], w(2), zc[:], ALU.mult, ALU.add, reads=[t_zraw, t_zc, self.t_cols], writes=[t_zc])
                    zcs.append((zc, t_zc))
                for tt in range(16):
                    pt, t_pt = ptr.next()
                    for cc in range(4):
                        zc, t_zc = zcs[cc]
                        P.tr(pt[:, cc * 128:(cc + 1) * 128], zc[:, tt * 128:(tt + 1) * 128], c["ident_f"][:], reads=[t_zc, ct["ident_f"]], writes=[t_pt])
                    st, t_st = stg.next()
                    P.cp("act" if tt % 2 == 0 else "dve", st[:], pt[:], reads=[t_pt], writes=[t_st])
                    P.dma("sp", self.zt[tt * 128:(tt + 1) * 128, g * 512:(g + 1) * 512], st[:], reads=[t_st], writes=[self.t_zt[tt][g]])
            if "zt" in self.taps:
                tp = self.tap("tap_zt", [L, 3072])
                for tt in range(16):
                    P.dma("sp", tp[tt * 128:(tt + 1) * 128, :], self.zt[tt * 128:(tt + 1) * 128, :], reads=self.t_zt[tt])
            P.barrier()

    def rope64(self, rings, src_psum, t_src, n, lc0, cosT, sinT, t_rope, out, t_out):
        P, c, ct = self.P, self.c, self.ct
        rf, ppq, t1r, t2r = rings
        f, t_f = rf.next()
        P.cp("act", f[0:64, :n], src_psum[0:64, :n], reads=[t_src], writes=[t_f])
        pq, t_pq = ppq.next()
        P.mm(pq[0:64, :n], c["perm64"][:], f[0:64, :n], True, True, reads=[t_f, ct["perm64"]], writes=[t_pq])
        t1, t_t1 = t1r.next()
        P.tt("pool", t1[0:64, :n], f[0:64, :n], cosT[:, lc0:lc0 + n], ALU.mult, reads=[t_f, t_rope], writes=[t_t1])
        t2, t_t2 = t2r.next()
        P.tt("dve", t2[0:64, :n], pq[0:64, :n], sinT[:, lc0:lc0 + n], ALU.mult, reads=[t_pq, t_rope], writes=[t_t2])
        P.tt("pool", out, t1[0:64, :n], t2[0:64, :n], ALU.add, reads=[t_t1, t_t2], writes=[t_out])

    def odd_mla(self, hT, t_hT):
        P, d, c, ct = self.P, self.d, self.c, self.ct
        LT = COLT[1:]
        with contextlib.ExitStack() as es:
            zqnT = self.sb(es, "zqnT", [128, 4 * L], BF16)
            t_zqn = [Tok() for _ in LT]
            ckvT = self.sb(es, "ckvT", [128, 2 * T], BF16)
            t_ckv = [Tok() for _ in COLT]
            krT = self.sb(es, "krT", [64, T], BF16)
            t_kr = [Tok() for _ in COLT]
            cosT = self.sb(es, "cos64", [64, L], F32)
            sinT = self.sb(es, "sin64", [64, L], F32)
            t_rope = Tok()
            P.dma("sp", cosT[:], d["cos64"][:, :], writes=[t_rope])
            P.dma("sp", sinT[:], d["sin64"][:, :], writes=[t_rope])
            rrings = (self.ring(es, "rf", 2, [64, 512], F32), self.ring(es, "rpq", 1, [128, 512], F32, psum=True),
                      self.ring(es, "rt1", 2, [64, 512], F32), self.ring(es, "rt2", 2, [64, 512], F32))
            with contextlib.ExitStack() as es2:
                wb = self.sb(es2, "wmla", [128, 16 * 832], BF16)
                t_wb = Tok()
                self.load_w(wb, t_wb, d["o_w_in"], 3072, 832)
                pz = self.ring(es2, "mpz", 4, [128, 512], F32, psum=True)
                pss = self.ring(es2, "mpss", 1, [128, 512], F32, psum=True)
                qfr = self.ring(es2, "mqf", 2, [128, 4 * 512], F32)
                q2r = self.ring(es2, "mq2", 2, [128, 4 * 512], BF16)
                rtr = self.ring(es2, "mrt", 2, [128, 512], F32)

                def normed(nch, wc0, tiles, gkey, dst, dstw, dcol, t_dst):
                    for ci, (c0, n) in tiles:
                        zs = []
                        for j in range(nch):
                            z, t_z = pz.next()
                            for k in range(16):
                                P.mm(z[:, :n], wb[:, k * 832 + wc0 + j * 128:k * 832 + wc0 + (j + 1) * 128], hT[:, k * T + c0:k * T + c0 + n],
                                     k == 0, k == 15, reads=[t_wb] + self.hT_toks(t_hT, c0, n), writes=[t_z])
                            zs.append((z, t_z))
                        qf, t_qf = qfr.next()
                        q2, t_q2 = q2r.next()
                        for j, (z, t_z) in enumerate(zs):
                            P.cp("act", qf[:, j * 512:j * 512 + n], z[:, :n], reads=[t_z], writes=[t_qf])
                            P.act(q2[:, j * 512:j * 512 + n], z[:, :n], AF.Square, reads=[t_z], writes=[t_q2])
                        ssp, t_ssp = pss.next()
                        for j in range(nch):
                            P.mm(ssp[:, :n], c["ones_b"][:], q2[:, j * 512:j * 512 + n], j == 0, j == nch - 1, reads=[t_q2, ct["ones_b"]], writes=[t_ssp])
                        rt, t_rt = rtr.next()
                        P.act(rt[:, :n], ssp[:, :n], AF.Sqrt, reads=[t_ssp], writes=[t_rt], scale=1.0 / (nch * 128), bias=NORM_EPS)
                        P.recip(rt[:, :n], rt[:, :n], reads=[t_rt], writes=[t_rt])
                        for j in range(nch):
                            P.stt(dst[:, j * dstw + dcol(c0):j * dstw + dcol(c0) + n], qf[:, j * 512:j * 512 + n], self.col(gkey, j), rt[:, :n],
                                  ALU.mult, ALU.mult, reads=[t_qf, t_rt, self.t_cols], writes=[t_dst[ci]])

                normed(4, 0, list(enumerate(LT)), "oqg", zqnT, L, lambda c0: c0 - LC, t_zqn)
                normed(2, 512, list(enumerate(COLT)), "okg", ckvT, T, lambda c0: c0, t_ckv)
                for ci, (c0, n) in enumerate(COLT):
                    z, t_z = pz.next()
                    for k in range(16):
                        P.mm(z[0:64, :n], wb[:, k * 832 + 768:k * 832 + 832], hT[:, k * T + c0:k * T + c0 + n], k == 0, k == 15,
                             reads=[t_wb] + self.hT_toks(t_hT, c0, n), writes=[t_z])
                    if c0 < LC:
                        P.cp("act", krT[:, c0:c0 + n], z[0:64, :n], reads=[t_z], writes=[t_kr[ci]])
                    else:
                        self.rope64(rrings, z, t_z, n, c0 - LC, cosT, sinT, t_rope, krT[:, c0:c0 + n], t_kr[ci])
                if "mla" in self.taps:
                    t1 = self.tap("tap_zqnT", [512, L], BF16)
                    for j in range(4):
                        P.dma("sp", t1[j * 128:(j + 1) * 128, :], zqnT[:, j * L:(j + 1) * L], reads=t_zqn)
                    t2 = self.tap("tap_ckvT", [256, T], BF16)
                    for j in range(2):
                        P.dma("sp", t2[j * 128:(j + 1) * 128, :], ckvT[:, j * T:(j + 1) * T], reads=t_ckv)
                    t3 = self.tap("tap_krT", [64, T], BF16)
                    P.dma("sp", t3[:, :], krT[:], reads=t_kr)
                P.barrier()
            with contextlib.ExitStack() as es2:
                wuq = self.sb(es2, "wuq", [128, 4 * 1536], BF16)
                t_wuq = Tok()
                self.load_w(wuq, t_wuq, d["o_w_uq"], 0, 1536, nk=4)
                wukv = self.sb(es2, "wukv", [128, 2 * 2048], BF16)
                t_wukv = Tok()
                self.load_w(wukv, t_wukv, d["o_w_ukv"], 0, 2048, nk=2)
                pu = self.ring(es2, "pS", 3, [128, 512], F32, psum=True)
                qnr = self.ring(es2, "qnh", 2, [128, L], BF16)
                qrr = self.ring(es2, "qrh", 2, [64, L], BF16)
                knr = self.ring(es2, "knh", 2, [128, T], BF16)
                vhr = self.ring(es2, "vh", 2, [128, NT * 128], BF16)
                cur = {}

                def prep(h):
                    qn, t_qn = qnr.next()
                    qr, t_qr = qrr.next()
                    kn, t_kn = knr.next()
                    vh, t_vh = vhr.next()
                    cur.update(qn=qn, t_qn=t_qn, qr=qr, t_qr=t_qr, kn=kn, t_kn=t_kn, vh=vh, t_vh=t_vh)
                    for ci, (c0, n) in enumerate(LT):
                        z, t_z = pu.next()
                        for k in range(4):
                            P.mm(z[:, :n], wuq[:, k * 1536 + h * 192:k * 1536 + h * 192 + 128], zqnT[:, k * L + c0 - LC:k * L + c0 - LC + n],
                                 k == 0, k == 3, reads=[t_wuq, t_zqn[ci]], writes=[t_z])
                        P.cp("act", qn[:, c0 - LC:c0 - LC + n], z[:, :n], reads=[t_z], writes=[t_qn])
                        z, t_z = pu.next()
                        for k in range(4):
                            P.mm(z[0:64, :n], wuq[:, k * 1536 + h * 192 + 128:k * 1536 + h * 192 + 192], zqnT[:, k * L + c0 - LC:k * L + c0 - LC + n],
                                 k == 0, k == 3, reads=[t_wuq, t_zqn[ci]], writes=[t_z])
                        self.rope64(rrings, z, t_z, n, c0 - LC, cosT, sinT, t_rope, qr[:, c0 - LC:c0 - LC + n], t_qr)
                    for ci, (c0, n) in enumerate(COLT):
                        z, t_z = pu.next()
                        for k in range(2):
                            P.mm(z[:, :n], wukv[:, k * 2048 + h * 256:k * 2048 + h * 256 + 128], ckvT[:, k * T + c0:k * T + c0 + n],
                                 k == 0, k == 1, reads=[t_wukv, t_ckv[ci]], writes=[t_z])
                        P.cp("act", kn[:, c0:c0 + n], z[:, :n], reads=[t_z], writes=[t_kn])
                    for i0 in range(0, NT, 4):
                        z, t_z = pu.next()
                        nn = min(4, NT - i0)
                        for ii in range(nn):
                            i = i0 + ii
                            kci = 0 if i < 2 else 1 + (i - 2) // 4
                            for k in range(2):
                                P.mm(z[:, ii * 128:(ii + 1) * 128], ckvT[:, k * T + i * 128:k * T + (i + 1) * 128],
                                     wukv[:, k * 2048 + h * 256 + 128:k * 2048 + h * 256 + 256], k == 0, k == 1,
                                     reads=[t_wukv, t_ckv[kci]], writes=[t_z])
                        P.cp("act", vh[:, i0 * 128:(i0 + nn) * 128], z[:, 0:nn * 128], reads=[t_z], writes=[t_vh])

                def s_mm(h, S, t_S, kc, c0, n):
                    kci = 0 if kc < 2 else 1 + (kc - 2) // 4
                    P.mm(S[:, :n], cur["kn"][:, kc * 128:(kc + 1) * 128], cur["qn"][:, c0 - LC:c0 - LC + n], True, False,
                         reads=[cur["t_kn"], cur["t_qn"]], writes=[t_S])
                    P.mm(S[:, :n], krT[:, kc * 128:(kc + 1) * 128], cur["qr"][:, c0 - LC:c0 - LC + n], False, True,
                         reads=[t_kr[kci], cur["t_qr"]], writes=[t_S])

                def v_ap(h, kc):
                    return cur["vh"][:, kc * 128:(kc + 1) * 128], [cur["t_vh"]]

                q_tiles = [(ci, c0, n, list(range(NT))) for ci, (c0, n) in enumerate(COLT) if ci > 0]
                self.attention(es2, 8, q_tiles, s_mm, v_ap, 192 ** -0.5, 1024, prep=prep, pS=pu)
                P.barrier()

    def hyena_filters(self):
        P, d, c, ct = self.P, self.d, self.c, self.ct
        PI = math.pi
        with contextlib.ExitStack() as es:
            featT = self.sb(es, "featT", [33, L], F32)
            w1 = self.sb(es, "fw1", [33, 64], F32)
            w2 = self.sb(es, "fw2", [64, 64], F32)
            w3 = self.sb(es, "fw3", [64, 64], F32)
            w4 = self.sb(es, "fw4", [64, 4096], BF16)
            tneg = self.sb(es, "tneg", [128, 16], F32)
            dbc = self.sb(es, "dbc", [128, 1024], F32)
            t_in = Tok()
            P.dma("sp", featT[:], d["featT"][:, :], writes=[t_in])
            P.dma("sp", w1[:], d["o_f_w1"][:, :], writes=[t_in])
            P.dma("sp", w2[:], d["o_f_w2"][:, :], writes=[t_in])
            P.dma("sp", w3[:], d["o_f_w3"][:, :], writes=[t_in])
            P.dma("pool", w4[:], d["o_f_w4"][:, :], writes=[t_in])
            P.dma("sp", tneg[:], d["tneg"][:, :], writes=[t_in])
            P.dma("sp", dbc[:], d["deltas"][0, :].partition_broadcast(128), writes=[t_in])
            P.tt("dve", self.fcols[:, 6:8], self.fcols[:, 0:2], self.fcols[:, 3:5], ALU.mult, reads=[self.t_fcols], writes=[self.t_fcols])
            bfr2 = self.sb(es, "bfr2", [64, 1], F32)
            t_bfr2 = Tok()
            P.tt("dve", bfr2[:], self.fcols[:, 2:3], self.fcols[:, 5:6], ALU.mult, reads=[self.t_fcols], writes=[t_bfr2])
            hh = [self.sb(es, "hh", [64, L], F32) for _ in range(2)]
            t_hh = [Tok(), Tok()]
            pf = self.ring(es, "pf", 3, [128, 512], F32, psum=True)
            argr = self.ring(es, "farg", 2, [64, 512], F32)
            m1r = self.ring(es, "fm1", 2, [64, 512], F32)
            m2r = self.ring(es, "fm2", 2, [64, 512], F32)
            layers = [(w1, 33, featT, t_in), (w2, 64, hh[0], t_hh[0]), (w3, 64, hh[1], t_hh[1])]
            for i, (w, kdim, src, t_src) in enumerate(layers):
                dst, t_dst = hh[i % 2], t_hh[i % 2]
                bcol = self.fcols[:, 6 + i:7 + i] if i < 2 else bfr2[:, 0:1]
                for q in range(4):
                    p, t_p = pf.next()
                    P.mm(p[0:64, :], w[0:kdim, :], src[0:kdim, q * 512:(q + 1) * 512], True, True, reads=[t_in, t_src], writes=[t_p])
                    a, t_a = argr.next()
                    P.ts("dve", a[:], p[0:64, :], self.fcols[:, 3 + i:4 + i], bcol, ALU.mult, ALU.add, reads=[t_p, self.t_fcols, t_bfr2], writes=[t_a])
                    m1, t_m1 = m1r.next()
                    P.ts("dve", m1[:], a[:], PI, -2.0 * PI, ALU.is_gt, ALU.mult, reads=[t_a], writes=[t_m1])
                    m2, t_m2 = m2r.next()
                    P.ts("dve", m2[:], a[:], -PI, 2.0 * PI, ALU.is_lt, ALU.mult, reads=[t_a], writes=[t_m2])
                    P.tt("pool", m1[:], m1[:], m2[:], ALU.add, reads=[t_m1, t_m2], writes=[t_m1])
                    P.tt("pool", a[:], a[:], m1[:], ALU.add, reads=[t_a, t_m1], writes=[t_a])
                    P.act(dst[:, q * 512:(q + 1) * 512], a[:], AF.Sin, reads=[t_a], writes=[t_dst])
            h3f, t_h3f = hh[0], t_hh[0]
            h3 = self.sb(es, "hh3b", [64, L], BF16)
            t_h3 = Tok()
            P.cp("act", h3[:], h3f[:], reads=[t_h3f], writes=[t_h3])
            if "hh3" in self.taps:
                tp = self.tap("tap_hh3", [64, L])
                P.dma("sp", tp[:, :], h3f[:], reads=[t_h3f])
            hd = [self.sb(es, "hd", [128, 16 * 512], F32) for _ in range(2)]
            t_hd = [[Tok() for _ in range(16)] for _ in range(2)]
            A = self.sb(es, "fA", [128, 16 * 512], BF16)
            B = self.sb(es, "fB", [128, 16 * 512], BF16)
            t_A, t_B = [Tok() for _ in range(16)], [Tok() for _ in range(16)]
            decr = self.ring(es, "dec", 2, [128, 512], F32)
            abr = self.ring(es, "fab", 3, [128, 512], BF16)
            pn = self.ring(es, "pn", 1, [128, 512], F32, psum=True)
            rnr = self.ring(es, "frn", 2, [128, 512], F32)
            cfr = self.ring(es, "fcf", 3, [128, 2048], BF16)
            sfr = self.ring(es, "fsf", 3, [128, 2048], BF16)
            pk = self.ring(es, "pk", 2, [128, 512], F32, psum=True)
            pki = self.ring(es, "pki", 2, [128, 512], F32, psum=True)
            kst = self.ring(es, "kst", 2, [128, 512], F32)
            ksti = self.ring(es, "ksti", 2, [128, 512], F32)
            for n in range(2):
                for s in range(2):
                    rns = []
                    for dirn in range(2):
                        col0 = dirn * 2048 + n * 1024 + s * 512
                        H, t_H = hd[dirn], t_hd[dirn]
                        nrm, t_nrm = pn.next()

                        def gen_hraw(jt_):
                            p_, t_p_ = pf.next()
                            P.mm(p_[:], h3[0:64, jt_ * 128:(jt_ + 1) * 128], w4[0:64, col0:col0 + 512], True, True, reads=[t_h3, t_in], writes=[t_p_])
                            return p_, t_p_
                        pend = [gen_hraw(0)]
                        for jt in range(16):
                            if jt + 1 < 16:
                                pend.append(gen_hraw(jt + 1))
                            p, t_p = pend.pop(0)
                            dec, t_dec = decr.next()
                            P.act(dec[:], dbc[:, s * 512:(s + 1) * 512], AF.Exp, reads=[t_in], writes=[t_dec], scale=tneg[:, jt:jt + 1])
                            P.tt("dve", H[:, jt * 512:(jt + 1) * 512], p[:], dec[:], ALU.mult, reads=[t_p, t_dec], writes=[t_H[jt]])
                            ab, t_ab = abr.next()
                            P.act(ab[:], H[:, jt * 512:(jt + 1) * 512], AF.Abs, reads=[t_H[jt]], writes=[t_ab])
                            P.mm(nrm[:], c["ones_b"][:], ab[:], jt == 0, jt == 15, reads=[t_ab, ct["ones_b"]], writes=[t_nrm])
                        rn, t_rn = rnr.next()
                        P.ts("dve", rn[:], nrm[:], 1e-6, None, ALU.add, reads=[t_nrm], writes=[t_rn])
                        P.recip(rn[:], rn[:], reads=[t_rn], writes=[t_rn])
                        rns.append((rn, t_rn))
                    H0, H1 = hd[0], hd[1]
                    P.memset("dve", H1[0:1, 0:512], 0.0, writes=[t_hd[1][0]])
                    for jt in range(16):
                        sl = slice(jt * 512, (jt + 1) * 512)
                        P.tt("dve", H0[:, sl], H0[:, sl], rns[0][0][:], ALU.mult, reads=[t_hd[0][jt], rns[0][1]], writes=[t_hd[0][jt]])
                        P.tt("dve", H1[:, sl], H1[:, sl], rns[1][0][:], ALU.mult, reads=[t_hd[1][jt], rns[1][1]], writes=[t_hd[1][jt]])
                        P.tt("dve", A[:, sl], H0[:, sl], H1[:, sl], ALU.add, reads=[t_hd[0][jt], t_hd[1][jt]], writes=[t_A[jt]])
                        P.tt("pool", B[:, sl], H1[:, sl], H0[:, sl], ALU.subtract, reads=[t_hd[0][jt], t_hd[1][jt]], writes=[t_B[jt]])
                    if "hfilt" in self.taps and n == 0 and s == 0:
                        tp = self.tap("tap_hd0", [128, 16 * 512])
                        P.dma("sp", tp[:, :], hd[0][:], reads=t_hd[0])
                        tp = self.tap("tap_hd1", [128, 16 * 512])
                        P.dma("sp", tp[:, :], hd[1][:], reads=t_hd[1])
                    for fc in range(16):
                        cf, t_cf = cfr.next()
                        P.dma("sp", cf[:], d["dft_cf"][fc, :, :], writes=[t_cf])
                        sf, t_sf = sfr.next()
                        P.dma("sp", sf[:], d["dft_sf"][fc, :, :], writes=[t_sf])
                        kr, t_kr = pk.next()
                        ki, t_ki = pki.next()
                        for jt in range(16):
                            P.mm(kr[:], cf[:, jt * 128:(jt + 1) * 128], A[:, jt * 512:(jt + 1) * 512], jt == 0, jt == 15, reads=[t_cf, t_A[jt]], writes=[t_kr])
                        for jt in range(16):
                            P.mm(ki[:], sf[:, jt * 128:(jt + 1) * 128], B[:, jt * 512:(jt + 1) * 512], jt == 0, jt == 15, reads=[t_sf, t_B[jt]], writes=[t_ki])
                        st, t_st = kst.next()
                        P.cp("act", st[:], kr[:], reads=[t_kr], writes=[t_st])
                        P.dma("sp", self.kfd[n, 0, fc * 128:(fc + 1) * 128, s * 512:(s + 1) * 512], st[:], reads=[t_st], writes=[self.t_kfd[n][s][fc]])
                        st2, t_st2 = ksti.next()
                        P.cp("dve", st2[:], ki[:], reads=[t_ki], writes=[t_st2])
                        P.dma("sp", self.kfd[n, 1, fc * 128:(fc + 1) * 128, s * 512:(s + 1) * 512], st2[:], reads=[t_st2], writes=[self.t_kfd[n][s][fc]])
            P.barrier()

    def hyena_conv(self):
        P, d, c, ct = self.P, self.d, self.c, self.ct
        with contextlib.ExitStack() as es:
            ya = self.sb(es, "ya", [128, 16 * 1024], BF16)
            yb = self.sb(es, "yb", [128, 16 * 1024], BF16)
            t_y = {id(ya): [Tok() for _ in range(16)], id(yb): [Tok() for _ in range(16)]}
            skb = self.sb(es, "skb", [128, 2048], F32)
            t_skb = Tok()
            P.dma("sp", skb[:], d["o_skip"].rearrange("a c -> (a c)").partition_broadcast(128), writes=[t_skb])
            for tt in range(16):
                P.dma("pool", ya[:, tt * 1024:(tt + 1) * 1024], self.zt[tt * 128:(tt + 1) * 128, 2048:3072],
                      reads=self.t_zt[tt][4:6], writes=[t_y[id(ya)][tt]])
            Yre = self.sb(es, "Yre", [128, 16 * 512], BF16)
            Yim = self.sb(es, "Yim", [128, 16 * 512], BF16)
            t_Yre = [Tok() for _ in range(16)]
            t_Yim = [Tok() for _ in range(16)]
            tw = self.ring(es, "tw", 6, [128, 2048], BF16)
            kre_r = self.ring(es, "kre", 2, [128, 512], F32)
            kim_r = self.ring(es, "kim", 2, [128, 512], F32)
            pU = self.ring(es, "pUr", 2, [128, 512], F32, psum=True)
            pV = self.ring(es, "pUs", 2, [128, 512], F32, psum=True)
            pY = self.ring(es, "pY", 2, [128, 512], F32, psum=True)
            tmp = {k: self.ring(es, "hc" + k, 2, [128, 512], F32) for k in ("t1", "t2", "t3", "t4", "ts", "r")}
            xgr = self.ring(es, "xg", 2, [128, 512], F32)
            for n in range(2):
                yin, yout = (ya, yb) if n == 0 else (yb, ya)
                t_in, t_out = t_y[id(yin)], t_y[id(yout)]
                for s in range(2):
                    for fc in range(16):
                        cf, t_cf = tw.next()
                        P.dma("sp", cf[:], d["dft_cf"][fc, :, :], writes=[t_cf])
                        sf, t_sf = tw.next()
                        P.dma("sp", sf[:], d["dft_sf"][fc, :, :], writes=[t_sf])
                        kre, t_kre = kre_r.next()
                        P.dma("sp", kre[:], self.kfd[n, 0, fc * 128:(fc + 1) * 128, s * 512:(s + 1) * 512], reads=[self.t_kfd[n][s][fc]], writes=[t_kre])
                        kim, t_kim = kim_r.next()
                        P.dma("sp", kim[:], self.kfd[n, 1, fc * 128:(fc + 1) * 128, s * 512:(s + 1) * 512], reads=[self.t_kfd[n][s][fc]], writes=[t_kim])
                        U, t_U = pU.next()
                        V, t_V = pV.next()
                        for tt in range(16):
                            P.mm(U[:], cf[:, tt * 128:(tt + 1) * 128], yin[:, tt * 1024 + s * 512:tt * 1024 + (s + 1) * 512], tt == 0, tt == 15,
                                 reads=[t_cf, t_in[tt]], writes=[t_U])
                        for tt in range(16):
                            P.mm(V[:], sf[:, tt * 128:(tt + 1) * 128], yin[:, tt * 1024 + s * 512:tt * 1024 + (s + 1) * 512], tt == 0, tt == 15,
                                 reads=[t_sf, t_in[tt]], writes=[t_V])
                        t1, t_t1 = tmp["t1"].next()
                        P.tt("dve", t1[:], U[:], kre[:], ALU.mult, reads=[t_U, t_kre], writes=[t_t1])
                        t2, t_t2 = tmp["t2"].next()
                        P.tt("dve", t2[:], V[:], kim[:], ALU.mult, reads=[t_V, t_kim], writes=[t_t2])
                        P.tt("pool", Yre[:, fc * 512:(fc + 1) * 512], t1[:], t2[:], ALU.add, reads=[t_t1, t_t2], writes=[t_Yre[fc]])
                        t3, t_t3 = tmp["t3"].next()
                        P.tt("dve", t3[:], U[:], kim[:], ALU.mult, reads=[t_U, t_kim], writes=[t_t3])
                        t4, t_t4 = tmp["t4"].next()
                        P.tt("dve", t4[:], V[:], kre[:], ALU.mult, reads=[t_V, t_kre], writes=[t_t4])
                        P.tt("pool", Yim[:, fc * 512:(fc + 1) * 512], t3[:], t4[:], ALU.subtract, reads=[t_t3, t_t4], writes=[t_Yim[fc]])
                    for tt in range(16):
                        ci_, t_ci = tw.next()
                        P.dma("sp", ci_[:], d["dft_ci"][tt, :, :], writes=[t_ci])
                        si_, t_si = tw.next()
                        P.dma("sp", si_[:], d["dft_si"][tt, :, :], writes=[t_si])
                        Y, t_Y = pY.next()
                        for fc in range(16):
                            P.mm(Y[:], ci_[:, fc * 128:(fc + 1) * 128], Yre[:, fc * 512:(fc + 1) * 512], fc == 0, False, reads=[t_ci, t_Yre[fc]], writes=[t_Y])
                        for fc in range(16):
                            P.mm(Y[:], si_[:, fc * 128:(fc + 1) * 128], Yim[:, fc * 512:(fc + 1) * 512], False, fc == 15, reads=[t_si, t_Yim[fc]], writes=[t_Y])
                        xg, t_xg = xgr.next()
                        P.dma("sp", xg[:], self.zt[tt * 128:(tt + 1) * 128, n * 1024 + s * 512:n * 1024 + (s + 1) * 512],
                              reads=[self.t_zt[tt][n * 2 + s]], writes=[t_xg])
                        tsk, t_tsk = tmp["ts"].next()
                        P.tt("pool", tsk[:], yin[:, tt * 1024 + s * 512:tt * 1024 + (s + 1) * 512], skb[:, n * 1024 + s * 512:n * 1024 + (s + 1) * 512],
                             ALU.mult, reads=[t_in[tt], t_skb], writes=[t_tsk])
                        r, t_r = tmp["r"].next()
                        P.stt(r[:], Y[:], 1.0 / 2048, tsk[:], ALU.mult, ALU.add, reads=[t_Y, t_tsk], writes=[t_r])
                        P.tt("pool", yout[:, tt * 1024 + s * 512:tt * 1024 + (s + 1) * 512], r[:], xg[:], ALU.mult, reads=[t_r, t_xg], writes=[t_out[tt]])
            yfin, t_fin = ya, t_y[id(ya)]
            if "yh" in self.taps:
                tp = self.tap("tap_yh", [L, 1024], BF16)
                for tt in range(16):
                    P.dma("sp", tp[tt * 128:(tt + 1) * 128, :], yfin[:, tt * 1024:(tt + 1) * 1024], reads=[t_fin[tt]])
            ptr = self.ring(es, "ypt", 1, [128, 512], F32, psum=True)
            stg = self.ring(es, "ystg", 2, [128, L], BF16)
            for ch in range(8):
                st, t_st = stg.next()
                for q in range(4):
                    pt, t_pt = ptr.next()
                    ptb = pt[:, 0:256].bitcast(BF16)
                    for kk in range(4):
                        tt = q * 4 + kk
                        P.tr(ptb[:, kk * 128:(kk + 1) * 128], yfin[:, tt * 1024 + ch * 128:tt * 1024 + (ch + 1) * 128], c["ident_b"][:],
                             reads=[t_fin[tt], ct["ident_b"]], writes=[t_pt])
                    P.cp("act", st[:, q * 512:(q + 1) * 512], ptb[:, :], reads=[t_pt], writes=[t_st])
                for ci in range(1, 5):
                    c0, nn = COLT[ci]
                    P.dma("sp", self.mixd[ch * 128:(ch + 1) * 128, c0:c0 + nn], st[:, c0 - LC:c0 - LC + nn], reads=[t_st], writes=[self.t_mixd[ch][ci]])
            P.barrier()

    def layer_odd(self, l):
        P, d = self.P, self.d
        with contextlib.ExitStack() as es:
            hT = self.sb(es, "hT", [128, 16 * T], BF16)
            t_hT = [(Tok(), Tok()) for _ in range(NT)]
            self.phaseA(l, hT, t_hT)
            if self.stop == "A1":
                return False
            self.odd_hyena_proj(hT, t_hT)
            if self.stop == "zt":
                return False
            self.odd_mla(hT, t_hT)
        if self.stop == "mla":
            return False
        self.hyena_filters()
        if self.stop == "filt":
            return False
        self.hyena_conv()
        if self.stop == "hconv":
            return False
        lat = list(range(2, NT))
        self.out_proj(l, d["o_w_out"], lat)
        if self.stop == "xmid1":
            return False
        self.ffn(l, [lat[0:6], lat[6:12], lat[12:16]])
        return self.stop != "x1"

    def layer_even(self, l):
        P, d = self.P, self.d
        with contextlib.ExitStack() as es:
            hT = self.sb(es, "hT", [128, 16 * T], BF16)
            import os
            if os.environ.get("EVAC_SINGLE"):
                t_hT = [(lambda t: (t, t))(Tok()) for _ in range(NT)]
            else:
                t_hT = [(Tok(), Tok()) for _ in range(NT)]
            self.phaseA(l, hT, t_hT)
            if self.stop == "A0":
                return False
            self.even_conv(hT, t_hT)
            if self.stop == "conv0":
                return False
            with contextlib.ExitStack() as es2:
                qT = self.sb(es2, "qT", [128, 8 * T], BF16)
                t_qT = [[Tok() for _ in COLT] for _ in range(8)]
                kT = self.sb(es2, "kT", [128, 2 * T], BF16)
                t_kT = [[Tok() for _ in COLT] for _ in range(2)]
                Vsb = self.sb(es2, "Vsb", [128, NT * 256], BF16)
                t_V = [Tok() for _ in range(NT)]
                self.even_qkv(hT, t_hT, qT, t_qT, kT, t_kT, Vsb, t_V)
                if self.stop == "qkv0":
                    return False
                self.even_attention(qT, t_qT, kT, t_kT, Vsb, t_V)
        if self.stop == "att0":
            return False
        self.out_proj(l, d["e_w_out"], list(range(NT)), with_mods=1)
        if self.stop == "xmid0":
            return False
        self.ffn(l, [list(range(0, 6)), list(range(6, 12)), list(range(12, 18))])
        return self.stop != "x0"

    def build(self):
        import os
        self.setup()
        if not os.environ.get("SKIP0"):
            self.phase0()
        ok = self.stop != "mod"
        if ok and not os.environ.get("SKIP_EVEN"):
            ok = self.layer_even(0)
        if ok and hasattr(self, "layer_odd"):
            ok = self.layer_odd(1)
        if ok:
            self.final()
        self.pes.close()
        return self.nc


def make_in_maps(inputs, names=None):
    n = 8
    consts = host_constants()
    shared = {}
    for name, shape in WEIGHT_SPECS:
        shared[name] = np.ascontiguousarray(np.asarray(inputs[name], dtype=np.float32).reshape(shape))
    for name, arr in consts.items():
        shared[name] = arr
    maps = []
    x = np.asarray(inputs["x"], dtype=np.float32)
    ctx = np.asarray(inputs["ctx"], dtype=np.float32)
    cc = np.asarray(inputs["c"], dtype=np.float32)
    c_ctx = np.asarray(inputs["c_ctx"], dtype=np.float32)
    for b in range(n):
        m = dict(shared)
        m["x"] = np.ascontiguousarray(x[b])
        m["ctx"] = np.ascontiguousarray(ctx[b])
        m["cvec"] = np.ascontiguousarray(np.stack([cc[b], c_ctx], axis=0))
        if names is not None:
            m = {k: v for k, v in m.items() if k in names}
        maps.append(m)
    return maps


def kernel(**inputs):
    kb = KB()
    nc = kb.build()
    maps = make_in_maps(inputs, set(kb.d.keys()))
    res = run_bass_kernel_spmd(nc, maps, core_ids=list(range(8)))
    return np.stack([np.asarray(r["out"], dtype=np.float32) for r in res.results], axis=0)
```

```python
import contextlib
import math
import numpy as np
import ml_dtypes
import concourse.bass as bass
import concourse.mybir as mybir
from concourse.bass_utils import run_bass_kernel_spmd

F32 = mybir.dt.float32
BF16 = mybir.dt.bfloat16
AF = mybir.ActivationFunctionType
ALU = mybir.AluOpType


class Tok:
    __slots__ = ("w", "r")

    def __init__(self):
        self.w = None
        self.r = {}


class Op:
    __slots__ = ("eng", "fn", "deps", "sig", "cnt", "dma", "dsem", "dval", "idx", "epoch", "drain")


class Prog:
    ENGS = ("pe", "act", "dve", "pool", "sp")
    DMAQ = ("sp", "pool", "act")
    NDMA = 8

    def __init__(self, nc):
        self.nc = nc
        self.ops = []
        self.start = 0
        self.epoch = 0
        self.nops = 0
        self.ndma = {e: 0 for e in self.ENGS}
        self.cnt = {e: 0 for e in self.ENGS}
        self.waited = {e: {} for e in self.ENGS}
        self.esem = {e: nc.alloc_semaphore(name="s_" + e) for e in self.ENGS}
        self.dsem = {}
        for e in self.DMAQ:
            for j in range(self.NDMA):
                self.dsem[(e, j)] = nc.alloc_semaphore(name="d_%s%d" % (e, j))

    def add(self, eng, fn, reads=(), writes=(), dma=False, drain=False):
        op = Op()
        op.eng, op.fn, op.dma, op.sig, op.cnt, op.drain = eng, fn, dma, False, 0, drain
        op.idx = self.nops
        self.nops += 1
        op.epoch = self.epoch
        deps = {}
        raw = set()
        for t in reads:
            if t.w is not None:
                deps[t.w.idx] = t.w
                raw.add(t.w.idx)
        for t in writes:
            if t.w is not None:
                deps[t.w.idx] = t.w
            for o in t.r.values():
                deps[o.idx] = o
        op.deps = [d for d in deps.values() if d.epoch == self.epoch and
                   (d.dma or d.eng != eng or (eng != "pe" and d.idx in raw))]
        for d in op.deps:
            d.sig = True
        key = ("d", op.idx) if dma else eng
        for t in reads:
            t.r[key] = op
        for t in writes:
            t.w = op
            t.r = {}
        if dma:
            k = self.ndma[eng]
            self.ndma[eng] += 1
            op.dsem = (eng, k % self.NDMA)
            op.dval = 16 * (k // self.NDMA + 1)
        self.ops.append(op)
        return op

    def barrier(self):
        toks = {e: Tok() for e in self.ENGS}
        for e in self.ENGS:
            self.add(e, lambda h: (h.drain(), h.nop())[1], writes=[toks[e]], drain=True)
        for e in self.ENGS:
            self.add(e, lambda h: h.nop(), reads=list(toks.values()))
        self.epoch += 1
        self.flush()

    def flush(self):
        nc = self.nc
        ops = self.ops[self.start:]
        self.start = len(self.ops)
        for op in ops:
            if not op.dma and op.sig:
                self.cnt[op.eng] += 1
                op.cnt = self.cnt[op.eng]
        per = {e: [o for o in ops if o.eng == e] for e in self.ENGS}
        issued = {e: 0 for e in self.ENGS}
        base = {e: self.ndma[e] - sum(1 for o in per[e] if o.dma) for e in self.ENGS}

        def run(e, h):
            waited = self.waited[e]
            n_issued = base[e]
            for op in per[e]:
                need = {}
                for d in op.deps:
                    if d.dma:
                        k, v = ("D",) + d.dsem, d.dval
                    else:
                        k, v = ("E", d.eng), d.cnt
                    if need.get(k, 0) < v:
                        need[k] = v
                if op.dma and op.dval > 16:
                    k = ("D",) + op.dsem
                    need[k] = max(need.get(k, 0), op.dval - 16)
                if op.drain and e in self.DMAQ:
                    for j in range(self.NDMA):
                        c = (n_issued - j + self.NDMA - 1) // self.NDMA
                        if c > 0:
                            need[("D", e, j)] = max(need.get(("D", e, j), 0), 16 * c)
                for k, v in need.items():
                    if waited.get(k, 0) < v:
                        s = self.dsem[k[1:]] if k[0] == "D" else self.esem[k[1]]
                        h.wait_ge(s, v)
                        waited[k] = v
                ins = op.fn(h)
                inc = None
                if op.dma:
                    n_issued += 1
                    ins.then_inc(self.dsem[op.dsem], 16)
                    inc = (("D",) + op.dsem, 16)
                elif op.sig:
                    ins.then_inc(self.esem[e], 1)
                    inc = (("E", e), 1)
                self.simlog[e].append((dict(need), inc, op.idx))

        self.simlog = {e: [] for e in self.ENGS}
        with nc.Block() as block:
            if per["pe"]:
                block.tensor(lambda h: run("pe", h))
            if per["act"]:
                block.scalar(lambda h: run("act", h))
            if per["dve"]:
                block.vector(lambda h: run("dve", h))
            if per["pool"]:
                block.gpsimd(lambda h: run("pool", h))
            if per["sp"]:
                block.sync(lambda h: run("sp", h))
        self.simcheck()

    def simcheck(self):
        if not hasattr(self, "simval"):
            self.simval = {}
        val = self.simval
        pos = {e: 0 for e in self.ENGS}
        progress = True
        while progress:
            progress = False
            for e in self.ENGS:
                lg = self.simlog[e]
                while pos[e] < len(lg):
                    need, inc, idx = lg[pos[e]]
                    if all(val.get(k, 0) >= v for k, v in need.items()):
                        if inc is not None:
                            val[inc[0]] = val.get(inc[0], 0) + inc[1]
                        pos[e] += 1
                        progress = True
                    else:
                        break
        for e in self.ENGS:
            if pos[e] < len(self.simlog[e]):
                need, inc, idx = self.simlog[e][pos[e]]
                bad = {k: (v, val.get(k, 0)) for k, v in need.items() if val.get(k, 0) < v}
                raise RuntimeError("DEADLOCK on %s at op %d: waits %s" % (e, idx, bad))

    def dma(self, q, out, in_, reads=(), writes=(), **kw):
        return self.add(q, lambda h: h.dma_start(out=out, in_=in_, **kw), reads, writes, dma=True)

    def mm(self, out, lhsT, rhs, start, stop, reads=(), writes=()):
        return self.add("pe", lambda h: h.matmul(out, lhsT, rhs, start=start, stop=stop), reads, writes)

    def tr(self, out, in_, ident, reads=(), writes=()):
        return self.add("pe", lambda h: h.transpose(out, in_, ident), reads, writes)

    def act(self, out, in_, func, reads=(), writes=(), **kw):
        return self.add("act", lambda h: h.activation(out, in_, func, **kw), reads, writes)

    def ts(self, eng, out, in0, s1, s2, op0, op1=None, reads=(), writes=()):
        if op1 is None:
            return self.add(eng, lambda h: h.tensor_scalar(out, in0, s1, None, op0), reads, writes)
        return self.add(eng, lambda h: h.tensor_scalar(out, in0, s1, s2, op0, op1), reads, writes)

    def tt(self, eng, out, in0, in1, op, reads=(), writes=()):
        return self.add(eng, lambda h: h.tensor_tensor(out, in0, in1, op), reads, writes)

    def stt(self, out, in0, scalar, in1, op0, op1, reads=(), writes=()):
        return self.add("dve", lambda h: h.scalar_tensor_tensor(out, in0, scalar, in1, op0, op1), reads, writes)

    def cp(self, eng, out, in_, reads=(), writes=()):
        if eng == "act":
            return self.add(eng, lambda h: h.copy(out, in_), reads, writes)
        return self.add(eng, lambda h: h.tensor_copy(out, in_), reads, writes)

    def memset(self, eng, ap, val, writes=()):
        return self.add(eng, lambda h: h.memset(ap, val), (), writes)

    def rsum(self, out, in_, reads=(), writes=()):
        return self.add("dve", lambda h: h.tensor_reduce(out, in_, mybir.AxisListType.X, ALU.add), reads, writes)

    def recip(self, out, in_, reads=(), writes=()):
        return self.add("dve", lambda h: h.reciprocal(out, in_), reads, writes)


class Ring:
    def __init__(self, items):
        self.items = items
        self.i = 0

    def next(self):
        it = self.items[self.i % len(self.items)]
        self.i += 1
        return it


D = 2048
L = 2048
LC = 256
T = L + LC
NT = T // 128
COLT = [(0, 256), (256, 512), (768, 512), (1280, 512), (1792, 512)]
FFH = 5632
NHC = FFH // 128
NORM_EPS = 1e-6
LN_EPS = 1e-5
THETA = 10000.0
IN_EVEN = 3584
IN_ODD = 3904

WEIGHT_SPECS = [
    ("ada_w", (2, 2048, 12288)), ("ada_b", (2, 12288)), ("norm_mix_g", (2, 2048)), ("norm_ffn_g", (2, 2048)),
    ("e_w_in", (2048, 3584)), ("e_w_out", (2048, 2048)), ("e_dw_w", (31, 1024)), ("e_dw_b", (1, 1024)),
    ("e_ln_g", (1, 1024)), ("e_ln_b", (1, 1024)), ("e_qn_g", (1, 128)), ("e_kn_g", (1, 128)),
    ("o_w_in", (2048, 3904)), ("o_w_out", (2048, 2048)), ("o_short_w", (3, 3072)), ("o_short_b", (1, 3072)),
    ("o_f_w1", (33, 64)), ("o_f_b1", (1, 64)), ("o_f_w2", (64, 64)), ("o_f_b2", (1, 64)), ("o_f_w3", (64, 64)),
    ("o_f_b3", (1, 64)), ("o_f_w4", (64, 4096)), ("o_f_freq", (3, 64)), ("o_skip", (2, 1024)),
    ("o_q_norm_g", (1, 512)), ("o_kv_norm_g", (1, 256)), ("o_w_uq", (512, 1536)), ("o_w_ukv", (256, 2048)),
    ("ffn_w_gate", (2, 2048, 5632)), ("ffn_w_up", (2, 2048, 5632)), ("ffn_w_down", (2, 5632, 2048)),
    ("final_norm_g", (1, 2048)),
]

_CONST_CACHE = {}


def host_constants():
    if _CONST_CACHE:
        return _CONST_CACHE
    bf = ml_dtypes.bfloat16
    c = {}
    c["ident_f"] = np.eye(128, dtype=np.float32)
    c["ident_b"] = np.eye(128, dtype=np.float32).astype(bf)
    c["ones_f"] = np.ones((128, 128), np.float32)
    c["ones_b"] = np.ones((128, 128), np.float32).astype(bf)
    p128 = np.zeros((128, 128), np.float32)
    for m in range(128):
        if (m % 64) < 32:
            p128[m + 32, m] = -1.0
        else:
            p128[m - 32, m] = 1.0
    c["perm128"] = p128
    c["perm128b"] = p128.astype(bf)
    p64 = np.zeros((64, 64), np.float32)
    for m in range(64):
        if (m % 32) < 16:
            p64[m + 16, m] = -1.0
        else:
            p64[m - 16, m] = 1.0
    c["perm64"] = p64
    t = np.arange(L)
    row = (t // 64).astype(np.float32)
    col = (t % 64).astype(np.float32)
    inv32 = (THETA ** (-np.arange(32, dtype=np.float32) / 32)).astype(np.float32)
    inv16 = (THETA ** (-np.arange(16, dtype=np.float32) / 16)).astype(np.float32)
    ang128 = np.zeros((128, L), np.float32)
    for d in range(128):
        pos = row if d < 64 else col
        ang128[d] = pos * inv32[d % 32]
    c["cos128"] = np.cos(ang128).astype(np.float32)
    c["sin128"] = np.sin(ang128).astype(np.float32)
    ang64 = np.zeros((64, L), np.float32)
    for d in range(64):
        pos = row if d < 32 else col
        ang64[d] = pos * inv16[d % 16]
    c["cos64"] = np.cos(ang64).astype(np.float32)
    c["sin64"] = np.sin(ang64).astype(np.float32)
    tl = np.linspace(0.0, 1.0, L, dtype=np.float32)[:, None]
    w = (2.0 * math.pi / L) * np.arange(L, dtype=np.float32)[:, None]
    f = np.linspace(1e-4, 15, 16, dtype=np.float32)[None, :]
    feat = np.concatenate([tl, np.cos(f * w), -np.sin(f * w)], axis=-1).astype(np.float32)
    c["featT"] = np.ascontiguousarray(feat.T)
    c["tneg"] = np.ascontiguousarray((-tl[:, 0]).reshape(16, 128).T).astype(np.float32)
    max_decay = math.log(1e-2) / 0.3
    min_decay = math.log(1e-2) / 1.5
    c["deltas"] = np.abs(np.linspace(min_decay, max_decay, 1024, dtype=np.float32)).reshape(1, 1024).astype(np.float32)
    ff = (np.arange(L, dtype=np.float64) + 0.5) * (2.0 * math.pi / 4096.0)
    tt = np.arange(L, dtype=np.float64)
    ang = np.outer(tt, ff)
    C = np.cos(ang)
    S = np.sin(ang)
    c["dft_cf"] = np.ascontiguousarray(C.reshape(16, 128, 16, 128).transpose(2, 1, 0, 3).reshape(16, 128, 2048)).astype(bf)
    c["dft_sf"] = np.ascontiguousarray(S.reshape(16, 128, 16, 128).transpose(2, 1, 0, 3).reshape(16, 128, 2048)).astype(bf)
    CT = C.T
    ST = -S.T
    c["dft_ci"] = np.ascontiguousarray(CT.reshape(16, 128, 16, 128).transpose(2, 1, 0, 3).reshape(16, 128, 2048)).astype(bf)
    c["dft_si"] = np.ascontiguousarray(ST.reshape(16, 128, 16, 128).transpose(2, 1, 0, 3).reshape(16, 128, 2048)).astype(bf)
    _CONST_CACHE.update(c)
    return c


CONST_DT = {"ident_b": BF16, "ones_b": BF16, "perm128b": BF16, "dft_cf": BF16, "dft_sf": BF16, "dft_ci": BF16, "dft_si": BF16}


class KB:
    def __init__(self, stop=None, taps=()):
        self.stop = stop
        self.taps = set(taps)
        self.nc = bass.Bass("TRN2", target_bir_lowering=False)
        self.P = Prog(self.nc)
        self.uid = 0
        nc = self.nc
        specs = {"x": ([L, D], F32), "ctx": ([LC, D], F32), "cvec": ([2, D], F32)}
        for name, shape in WEIGHT_SPECS:
            specs[name] = (list(shape), F32)
        for name, arr in host_constants().items():
            specs[name] = (list(arr.shape), CONST_DT.get(name, F32))
        kb = self

        class LazyD(dict):
            def __missing__(self, name):
                shape, dt = specs[name]
                ap = nc.dram_tensor(name, shape, dt, kind="ExternalInput").ap()
                self[name] = ap
                return ap

        d = LazyD()
        d["out"] = nc.dram_tensor("out", [L, D], F32, kind="ExternalOutput").ap()
        self.d = d
        self.outs = ["out"]
        self.modd = nc.dram_tensor("modd", [2, 2, 12288], F32).ap()
        self.xres = nc.dram_tensor("xres", [T, D], F32).ap()
        self.t_xres = [[Tok() for _ in range(4)] for _ in range(NT)]
        self.mixd = nc.dram_tensor("mixd", [D, T], BF16).ap()
        self.t_mixd = [[Tok() for _ in range(len(COLT))] for _ in range(16)]
        self.convd = nc.dram_tensor("convd", [1024, T], F32).ap()
        self.t_convd = [Tok() for _ in range(8)]
        self.zt = nc.dram_tensor("zt", [L, 3072], F32).ap()
        self.t_zt = [[Tok() for _ in range(6)] for _ in range(16)]
        self.kfd = nc.dram_tensor("kfd", [2, 2, L, 1024], F32).ap()
        self.t_kfd = [[[Tok() for _ in range(16)] for _ in range(2)] for _ in range(2)]
        self.pes = contextlib.ExitStack()

    def nm(self, name):
        self.uid += 1
        return "%s_%d" % (name, self.uid)

    def sb(self, es, name, shape, dt):
        return es.enter_context(self.nc.sbuf_tensor(self.nm(name), list(shape), dt))

    def ps(self, es, name, shape, dt=F32):
        return es.enter_context(self.nc.psum_tensor(self.nm(name), list(shape), dt))

    def ring(self, es, name, n, shape, dt, psum=False):
        f = self.ps if psum else self.sb
        return Ring([(f(es, name, shape, dt), Tok()) for _ in range(n)])

    def tap(self, name, shape, dt=F32):
        ap = self.nc.dram_tensor(name, list(shape), dt, kind="ExternalOutput").ap()
        self.outs.append(name)
        return ap

    def setup(self):
        P, es, d = self.P, self.pes, self.d
        self.c = {}
        self.ct = {}
        for name, shape, dt in [("ident_f", [128, 128], F32), ("ident_b", [128, 128], BF16), ("ones_f", [128, 128], F32),
                                ("ones_b", [128, 128], BF16), ("perm128b", [128, 128], BF16), ("perm64", [64, 64], F32)]:
            t = self.sb(es, name, shape, dt)
            tk = Tok()
            P.dma("sp", t[:], d[name][:, :], writes=[tk])
            self.c[name] = t
            self.ct[name] = tk
        self.cols = self.sb(es, "cols", [128, 448], F32)
        self.t_cols = Tok()
        self.ncol = 0
        self.colreg = {}
        self.fcols = self.sb(es, "fcols", [64, 8], F32)
        self.t_fcols = Tok()
        self.modT = [[self.sb(es, "modT", [128, 96], F32) for _ in range(2)] for _ in range(2)]
        self.t_modT = [[Tok() for _ in range(2)] for _ in range(2)]
        self.acols = self.sb(es, "acols", [128, 128], F32)
        self.t_acols = Tok()
        self.condTb = self.sb(es, "condTb", [128, 32], BF16)
        self.t_condTb = Tok()
        with contextlib.ExitStack() as s2:
            stage = self.ring(s2, "cstage", 2, [128, 128], F32)
            cps = self.ring(s2, "cps", 2, [128, 128], F32, psum=True)

            def load_cols(key, src, R):
                off = self.ncol
                self.ncol += R
                self.colreg[key] = off
                st, t_st = stage.next()
                P.dma("sp", st[0:R, :], src, writes=[t_st])
                pt, t_pt = cps.next()
                P.tr(pt[:, 0:R], st[0:R, :], self.c["ident_f"][0:R, 0:R], reads=[t_st, self.ct["ident_f"]], writes=[t_pt])
                P.cp("dve", self.cols[:, off:off + R], pt[:, 0:R], reads=[t_pt], writes=[self.t_cols])

            r128 = lambda ap: ap.rearrange("a (k p) -> (a k) p", p=128)
            load_cols("gmix", r128(d["norm_mix_g"]), 32)
            load_cols("gffn", r128(d["norm_ffn_g"]), 32)
            dww = r128(d["e_dw_w"])
            load_cols("dww", dww[0:128, :], 128)
            load_cols("dww2", dww[128:248, :], 120)
            load_cols("dwb", r128(d["e_dw_b"]), 8)
            load_cols("lng", r128(d["e_ln_g"]), 8)
            load_cols("lnb", r128(d["e_ln_b"]), 8)
            load_cols("qng", d["e_qn_g"], 1)
            load_cols("kng", d["e_kn_g"], 1)
            load_cols("shw", r128(d["o_short_w"]), 72)
            load_cols("shb", r128(d["o_short_b"]), 24)
            load_cols("oqg", r128(d["o_q_norm_g"]), 4)
            load_cols("okg", r128(d["o_kv_norm_g"]), 2)
            st, t_st = stage.next()
            for j, nme in enumerate(["o_f_b1", "o_f_b2", "o_f_b3"]):
                P.dma("sp", st[j:j + 1, 0:64], d[nme], writes=[t_st])
            P.dma("sp", st[3:6, 0:64], d["o_f_freq"], writes=[t_st])
            pt, t_pt = cps.next()
            P.tr(pt[0:64, 0:6], st[0:6, 0:64], self.c["ident_f"][0:6, 0:6], reads=[t_st, self.ct["ident_f"]], writes=[t_pt])
            P.cp("dve", self.fcols[:, 0:6], pt[0:64, 0:6], reads=[t_pt], writes=[self.t_fcols])
            P.barrier()

    def col(self, key, j=0, n=1):
        o = self.colreg[key] + j
        return self.cols[:, o:o + n]

    def phase0(self):
        P, d = self.P, self.d
        with contextlib.ExitStack() as es:
            cc = self.sb(es, "cc", [32, 128], F32)
            t_cc = Tok()
            cv = d["cvec"]
            P.dma("sp", cc[0:16, :], cv[0:1, :].rearrange("a (k p) -> (a k) p", p=128), writes=[t_cc])
            P.dma("sp", cc[16:32, :], cv[1:2, :].rearrange("a (k p) -> (a k) p", p=128), writes=[t_cc])
            P.act(cc[:], cc[:], AF.Silu, reads=[t_cc], writes=[t_cc])
            self._pc2 = self.ps(es, "pc", [128, 128], F32)
            self._t_pc2 = Tok()
            P.tr(self._pc2[:, 0:32], cc[0:32, :], self.c["ident_f"][0:32, 0:32], reads=[t_cc, self.ct["ident_f"]], writes=[self._t_pc2])
            P.cp("dve", self.condTb[:], self._pc2[:, 0:32], reads=[self._t_pc2], writes=[self.t_condTb])
            for _ in self.mods_gen(es, 0, mode="mixed"):
                pass
            if "mod" in self.taps:
                tp = self.tap("tap_mod", [2, 2, 12288])
                P.dma("sp", tp[0:1, :, :], self.modd[0:1, :, :])
            P.barrier()

    def mods_gen(self, es, l, mode="dma", SW=512):
        P, d, c, ct = self.P, self.d, self.c, self.ct
        nsl = 12288 // SW
        FW = 16 * SW
        lhs = lambda k: self.condTb[:, :].rearrange("p (s k) -> p k s", k=16)[:, k, :]
        wr = Ring([(self.sb(es, "adawb", [128, FW], BF16), [Tok(), Tok(), Tok()]) for _ in range(2 if mode == "dma" else 3)])
        if mode != "dma":
            sr = self.ring(es, "adaws", 2, [128, FW], F32)
        br = self.ring(es, "adabb", 2, [2, SW], F32)
        pm = self.ring(es, "pmb", 1 if mode == "dma" else 2, [2, SW], F32, psum=True)
        st = self.ring(es, "mstb", 2, [2, SW], F32)
        t_modd = [Tok() for _ in range(nsl)]
        c1 = (FW // 3) // 64 * 64
        c2 = 2 * c1
        nh = 0
        for n in range(nsl):
            w, t_w = wr.next()
            src = d["ada_w"][l, :, n * SW:(n + 1) * SW].rearrange("(k p) c -> p k c", p=128)
            if mode == "hw" or (mode == "mixed" and n % 3 != 0):
                sf, t_sf = sr.next()
                P.dma("sp" if nh % 2 == 0 else "act", sf[:].rearrange("p (k c) -> p k c", c=SW), src, writes=[t_sf])
                nh += 1
                P.cp("dve", w[:, 0:c1], sf[:, 0:c1], reads=[t_sf], writes=[t_w[0]])
                P.cp("act", w[:, c1:c2], sf[:, c1:c2], reads=[t_sf], writes=[t_w[1]])
                P.cp("pool", w[:, c2:FW], sf[:, c2:FW], reads=[t_sf], writes=[t_w[2]])
            else:
                P.dma("pool", w[:].rearrange("p (k c) -> p k c", c=SW), src, writes=t_w)
            b, t_b = br.next()
            P.dma("sp", b[:], d["ada_b"][l, n * SW:(n + 1) * SW].partition_broadcast(2), writes=[t_b])
            p, t_p = pm.next()
            for k in range(16):
                P.mm(p[:], lhs(k), w[:, k * SW:(k + 1) * SW], k == 0, k == 15, reads=[self.t_condTb] + t_w, writes=[t_p])
            s_, t_s = st.next()
            P.tt("dve", s_[:], p[:], b[:], ALU.add, reads=[t_p, t_b], writes=[t_s])
            P.dma("sp", self.modd[l, :, n * SW:(n + 1) * SW], s_[:], reads=[t_s], writes=[t_modd[n]])
            yield
        st2 = self.ring(es, "mst2b", 2, [96, 128], F32)
        for s in range(2):
            t2, t_t2 = st2.next()
            P.dma("sp", t2[:], self.modd[l, s, :].rearrange("(r c) -> r c", c=128), reads=t_modd, writes=[t_t2])
            P.tr(self._pc2[:, 0:96], t2[0:96, :], c["ident_f"][0:96, 0:96], reads=[t_t2, ct["ident_f"]], writes=[self._t_pc2])
            P.cp("dve", self.modT[l][s][:], self._pc2[:, 0:96], reads=[self._t_pc2], writes=[self.t_modT[l][s]])
            for kind, (sc0, gkey) in enumerate([(16, "gmix"), (64, "gffn")]):
                o = ((l * 2 + s) * 2 + kind) * 16
                P.stt(self.acols[:, o:o + 16], self.modT[l][s][:, sc0:sc0 + 16], 1.0, self.col(gkey, l * 16, 16),
                      ALU.add, ALU.mult, reads=[self.t_modT[l][s], self.t_cols], writes=[self.t_acols])
        yield

    def a_col(self, l, s, kind, k):
        o = ((l * 2 + s) * 2 + kind) * 16 + k
        return self.acols[:, o:o + 1]

    def b_col(self, l, s, kind, k):
        o = (0 if kind == 0 else 48) + k
        return self.modT[l][s][:, o:o + 1]

    def src_rows(self, l, i):
        if l == 0:
            if i < 2:
                return self.d["ctx"][i * 128:(i + 1) * 128, :], []
            return self.d["x"][(i - 2) * 128:(i - 1) * 128, :], []
        return self.xres[i * 128:(i + 1) * 128, :], self.t_xres[i]

    def norm_tile(self, es_rings, l, i, kind, src, src_toks, dst, dst_col, dst_tok, dstw):
        P = self.P
        xr, jr, ssr, xnr, ptr = es_rings
        s = 1 if i < 2 else 0
        xt, t_xt = xr.next()
        P.dma("sp" if i % 2 == 0 else "pool", xt[:], src, reads=src_toks, writes=[t_xt])
        jk, t_jk = jr.next()
        ss, t_ss = ssr.next()
        P.act(jk[:], xt[:], AF.Square, reads=[t_xt], writes=[t_jk])
        import os
        step = int(os.environ.get("NT_STEP", "9"))
        if step < 2:
            return
        P.rsum(ss[:, 0:1], jk[:], reads=[t_jk], writes=[t_ss])
        if step < 3:
            return
        P.act(ss[:, 1:2], ss[:, 0:1], AF.Sqrt, reads=[t_ss], writes=[t_ss], scale=1.0 / D, bias=NORM_EPS)
        P.recip(ss[:, 2:3], ss[:, 1:2], reads=[t_ss], writes=[t_ss])
        if step < 4:
            return
        xn, t_xn = xnr.next()
        P.ts("dve", xn[:], xt[:], ss[:, 2:3], None, ALU.mult, reads=[t_xt, t_ss], writes=[t_xn])
        if step < 5:
            return
        pt, t_pt = ptr.next()
        for k in range(16):
            P.tr(pt[:, k * 128:(k + 1) * 128], xn[:, k * 128:(k + 1) * 128], self.c["ident_b"][:],
                 reads=[t_xn, self.ct["ident_b"]], writes=[t_pt])
        if step < 6:
            return
        for k in range(16):
            o = dst[:, k * dstw + dst_col:k * dstw + dst_col + 128]
            a, b = self.a_col(l, s, kind, k), self.b_col(l, s, kind, k)
            if k < 8:
                P.act(o, pt[:, k * 128:(k + 1) * 128], AF.Identity, reads=[t_pt, self.t_acols, self.t_modT[l][s]],
                      writes=[dst_tok[0]], scale=a, bias=b)
            else:
                P.ts("dve", o, pt[:, k * 128:(k + 1) * 128], a, b, ALU.mult, ALU.add,
                     reads=[t_pt, self.t_acols, self.t_modT[l][s]], writes=[dst_tok[1]])
        return ss, t_ss

    def norm_rings(self, es):
        return (self.ring(es, "nx", 3, [128, D], F32), self.ring(es, "njk", 2, [128, D], F32),
                self.ring(es, "nss", 4, [128, 4], F32), self.ring(es, "nxn", 2, [128, D], BF16),
                self.ring(es, "npt", 2, [128, D], BF16, psum=True))

    def phaseA(self, l, hT, t_hT):
        with contextlib.ExitStack() as es:
            rings = self.norm_rings(es)
            for i in range(NT):
                src, toks = self.src_rows(l, i)
                self.norm_tile(rings, l, i, 0, src, toks, hT, i * 128, t_hT[i], T)
            if ("hT%d" % l) in self.taps:
                tp = self.tap("tap_hT%d" % l, [D, T], BF16)
                for k in range(16):
                    self.P.dma("sp", tp[k * 128:(k + 1) * 128, :], hT[:, k * T:(k + 1) * T], reads=self.hT_toks(t_hT, 0, T))
            self.P.barrier()

    def hT_toks(self, t_hT, c0, n):
        return [t for pr in t_hT[c0 // 128:(c0 + n) // 128] for t in pr]

    def load_w(self, wb, t_wb, w, c0, ncol, nk=16, q="pool"):
        self.P.dma(q, wb[:, 0:nk * ncol].rearrange("p (k c) -> p k c", c=ncol),
                   w[0:nk * 128, c0:c0 + ncol].rearrange("(k p) c -> p k c", p=128), writes=[t_wb])

    def even_qkv(self, hT, t_hT, qT, t_qT, kT, t_kT, Vsb, t_V):
        P, d, c, ct = self.P, self.d, self.c, self.ct
        with contextlib.ExitStack() as es:
            wr = self.ring(es, "wqkv", 2, [128, 16 * 512], BF16)
            cosT = self.sb(es, "cosT", [128, L], F32)
            sinT = self.sb(es, "sinT", [128, L], F32)
            t_rope = Tok()
            P.dma("sp", cosT[:], d["cos128"][:, :], writes=[t_rope])
            P.dma("sp", sinT[:], d["sin128"][:, :], writes=[t_rope])
            pz = self.ring(es, "pz", 3, [128, 512], F32, psum=True)
            pss = self.ring(es, "pss", 2, [128, 512], F32, psum=True)
            ppq = self.ring(es, "ppq", 2, [128, 512], F32, psum=True)
            wk = {nme: self.ring(es, nme, 3 if nme in ("qf", "rt") else 2, [128, 512], BF16 if nme in ("q2", "qn") else F32)
                  for nme in ("qf", "q2", "rt", "qn", "t1", "t2")}

            def qk_process(z, t_z, n, c0, gcol, out, t_out):
                qf, t_qf = wk["qf"].next()
                P.cp("act", qf[:, :n], z[:, :n], reads=[t_z], writes=[t_qf])
                q2, t_q2 = wk["q2"].next()
                P.act(q2[:, :n], z[:, :n], AF.Square, reads=[t_z], writes=[t_q2])
                ssp, t_ssp = pss.next()
                P.mm(ssp[:, :n], c["ones_b"][:], q2[:, :n], True, True, reads=[t_q2, ct["ones_b"]], writes=[t_ssp])
                rt, t_rt = wk["rt"].next()
                P.act(rt[:, :n], ssp[:, :n], AF.Sqrt, reads=[t_ssp], writes=[t_rt], scale=1.0 / 128, bias=NORM_EPS)
                P.recip(rt[:, :n], rt[:, :n], reads=[t_rt], writes=[t_rt])
                if c0 < LC:
                    P.stt(out, qf[:, :n], gcol, rt[:, :n], ALU.mult, ALU.mult, reads=[t_qf, t_rt, self.t_cols], writes=[t_out])
                    return
                qn, t_qn = wk["qn"].next()
                P.stt(qn[:, :n], qf[:, :n], gcol, rt[:, :n], ALU.mult, ALU.mult, reads=[t_qf, t_rt, self.t_cols], writes=[t_qn])
                pq, t_pq = ppq.next()
                P.mm(pq[:, :n], c["perm128b"][:], qn[:, :n], True, True, reads=[t_qn, ct["perm128b"]], writes=[t_pq])
                t1, t_t1 = wk["t1"].next()
                P.tt("pool", t1[:, :n], qn[:, :n], cosT[:, c0 - LC:c0 - LC + n], ALU.mult, reads=[t_qn, t_rope], writes=[t_t1])
                t2, t_t2 = wk["t2"].next()
                P.tt("dve", t2[:, :n], pq[:, :n], sinT[:, c0 - LC:c0 - LC + n], ALU.mult, reads=[t_pq, t_rope], writes=[t_t2])
                P.tt("pool", out, t1[:, :n], t2[:, :n], ALU.add, reads=[t_t1, t_t2], writes=[t_out])

            def proj(wb, t_wb, wc0, c0, n):
                z, t_z = pz.next()
                for k in range(16):
                    P.mm(z[:, :n], wb[:, k * 512 + wc0:k * 512 + wc0 + 128], hT[:, k * T + c0:k * T + c0 + n], k == 0, k == 15,
                         reads=[t_wb] + self.hT_toks(t_hT, c0, n), writes=[t_z])
                return z, t_z

            wb, t_wb = wr.next()
            self.load_w(wb, t_wb, d["e_w_in"], 3072, 512)
            for hk in range(2):
                for ci, (c0, n) in enumerate(COLT):
                    z, t_z = proj(wb, t_wb, hk * 128, c0, n)
                    qk_process(z, t_z, n, c0, self.col("kng"), kT[:, hk * T + c0:hk * T + c0 + n], t_kT[hk][ci])
            for i in range(NT):
                z, t_z = pz.next()
                for k in range(16):
                    P.mm(z[:, 0:256], hT[:, k * T + i * 128:k * T + (i + 1) * 128], wb[:, k * 512 + 256:k * 512 + 512], k == 0, k == 15,
                         reads=[t_wb] + list(t_hT[i]), writes=[t_z])
                P.cp("act", Vsb[:, i * 256:(i + 1) * 256], z[:, 0:256], reads=[t_z], writes=[t_V[i]])
            for g in range(2):
                wb, t_wb = wr.next()
                self.load_w(wb, t_wb, d["e_w_in"], 2048 + g * 512, 512)
                for hh in range(4):
                    h = g * 4 + hh
                    for ci, (c0, n) in enumerate(COLT):
                        z, t_z = proj(wb, t_wb, hh * 128, c0, n)
                        qk_process(z, t_z, n, c0, self.col("qng"), qT[:, h * T + c0:h * T + c0 + n], t_qT[h][ci])
            if "qk0" in self.taps:
                tq = self.tap("tap_qT", [1024, T], BF16)
                tk_ = self.tap("tap_kT", [256, T], BF16)
                tv = self.tap("tap_V", [T, 256], BF16)
                for h in range(8):
                    P.dma("sp", tq[h * 128:(h + 1) * 128, :], qT[:, h * T:(h + 1) * T], reads=t_qT[h])
                for h in range(2):
                    P.dma("sp", tk_[h * 128:(h + 1) * 128, :], kT[:, h * T:(h + 1) * T], reads=t_kT[h])
                for i in range(NT):
                    P.dma("sp", tv[i * 128:(i + 1) * 128, :], Vsb[:, i * 256:(i + 1) * 256], reads=[t_V[i]])
            P.barrier()

    def attention(self, es, n_heads, q_tiles, s_mm, v_ap, scale, out_row0, prep=None, pS=None):
        P, c, ct = self.P, self.c, self.ct
        if pS is None:
            pS = self.ring(es, "pS", 3, [128, 512], F32, psum=True)
        pO = self.ring(es, "pO", 2, [128, 512], F32, psum=True)
        pL = self.ring(es, "pL", 2, [128, 512], F32, psum=True)
        PT = self.ring(es, "PT", 4, [128, 512], BF16)
        rc = self.ring(es, "rc", 2, [128, 512], F32)
        ao = self.ring(es, "ao", 2, [128, 512], BF16)
        for h in range(n_heads):
            if prep is not None:
                prep(h)
            for (ci, c0, n, keys) in q_tiles:
                O, t_O = pO.next()
                Ls, t_L = pL.next()
                Sq = []
                LOOK = 2
                for j in range(min(LOOK, len(keys))):
                    S, t_S = pS.next()
                    s_mm(h, S, t_S, keys[j], c0, n)
                    Sq.append((S, t_S))
                for j, kc in enumerate(keys):
                    if j + LOOK < len(keys):
                        S, t_S = pS.next()
                        s_mm(h, S, t_S, keys[j + LOOK], c0, n)
                        Sq.append((S, t_S))
                    S, t_S = Sq.pop(0)
                    pt, t_pt = PT.next()
                    P.act(pt[:, :n], S[:, :n], AF.Exp, reads=[t_S], writes=[t_pt], scale=scale)
                    va, vt = v_ap(h, kc)
                    P.mm(O[:, :n], va, pt[:, :n], j == 0, j == len(keys) - 1, reads=[t_pt] + vt, writes=[t_O])
                    P.mm(Ls[:, :n], c["ones_b"][:], pt[:, :n], j == 0, j == len(keys) - 1, reads=[t_pt, ct["ones_b"]], writes=[t_L])
                r, t_r = rc.next()
                P.recip(r[:, :n], Ls[:, :n], reads=[t_L], writes=[t_r])
                a, t_a = ao.next()
                P.tt("dve", a[:, :n], O[:, :n], r[:, :n], ALU.mult, reads=[t_O, t_r], writes=[t_a])
                rk = (out_row0 + h * 128) // 128
                P.dma("sp", self.mixd[out_row0 + h * 128:out_row0 + (h + 1) * 128, c0:c0 + n], a[:, :n], reads=[t_a],
                      writes=[self.t_mixd[rk][ci]])

    def even_attention(self, qT, t_qT, kT, t_kT, Vsb, t_V):
        P = self.P
        with contextlib.ExitStack() as es:
            def s_mm(h, S, t_S, kc, c0, n):
                kvh = h // 4
                ci = [x[0] for x in COLT].index(c0)
                kci = 0 if kc < 2 else 1 + (kc - 2) // 4
                P.mm(S[:, :n], kT[:, kvh * T + kc * 128:kvh * T + (kc + 1) * 128], qT[:, h * T + c0:h * T + c0 + n], True, True,
                     reads=[t_kT[kvh][kci], t_qT[h][ci]], writes=[t_S])

            def v_ap(h, kc):
                kvh = h // 4
                return Vsb[:, kc * 256 + kvh * 128:kc * 256 + (kvh + 1) * 128], [t_V[kc]]

            q_tiles = [(0, 0, 256, [0, 1])] + [(ci, c0, n, list(range(NT))) for ci, (c0, n) in enumerate(COLT) if ci > 0]
            self.attention(es, 8, q_tiles, s_mm, v_ap, 128 ** -0.5, 1024)
            P.barrier()

    def even_conv(self, hT, t_hT):
        P, d, c, ct = self.P, self.d, self.c, self.ct
        W = 15 + LC + 15 + L + 15
        boff = lambda c0: 15 + c0 if c0 < LC else 30 + c0
        with contextlib.ExitStack() as es:
            wr = self.ring(es, "wconv", 3, [128, 16 * 512], BF16)
            up = self.ring(es, "upad", 2, [128, W], BF16)
            for u, t_u in up.items:
                P.memset("pool", u[:], 0.0, writes=[t_u])
            dgr = self.ring(es, "dg", 2, [128, 31 * 128], BF16)
            pa = self.ring(es, "pa", 2, [128, 512], F32, psum=True)
            pg = self.ring(es, "pg", 2, [128, 512], F32, psum=True)
            pcv = self.ring(es, "pcv", 2, [128, 512], F32, psum=True)
            sgr = self.ring(es, "sg", 2, [128, 512], F32)
            cst = self.ring(es, "cst", 3, [128, 512], F32)
            self._pc2 = self.ps(es, "pc2b", [128, 128], F32)
            self._t_pc2 = Tok()
            gen = self.mods_gen(es, 1, mode="dma")
            gen_live = True
            for half in range(2):
                wa, t_wa = wr.next()
                self.load_w(wa, t_wa, d["e_w_in"], half * 512, 512)
                wg, t_wg = wr.next()
                self.load_w(wg, t_wg, d["e_w_in"], 1024 + half * 512, 512)
                for cc in range(4):
                    ch = half * 4 + cc
                    u, t_u = up.next()
                    dg, t_dg = dgr.next()
                    wcol = lambda k: (self.col("dww", k * 8 + ch) if k * 8 + ch < 128 else self.col("dww2", k * 8 + ch - 128))
                    for k in range(31):
                        P.ts("pool" if k % 2 else "dve", dg[:, k * 128:(k + 1) * 128], c["ident_f"][:], wcol(k), None, ALU.mult,
                             reads=[ct["ident_f"], self.t_cols], writes=[t_dg])
                    for (c0, n) in COLT:
                        za, t_za = pa.next()
                        zg, t_zg = pg.next()
                        for k in range(16):
                            P.mm(za[:, :n], wa[:, k * 512 + cc * 128:k * 512 + (cc + 1) * 128], hT[:, k * T + c0:k * T + c0 + n],
                                 k == 0, k == 15, reads=[t_wa] + self.hT_toks(t_hT, c0, n), writes=[t_za])
                        for k in range(16):
                            P.mm(zg[:, :n], wg[:, k * 512 + cc * 128:k * 512 + (cc + 1) * 128], hT[:, k * T + c0:k * T + c0 + n],
                                 k == 0, k == 15, reads=[t_wg] + self.hT_toks(t_hT, c0, n), writes=[t_zg])
                        sg, t_sg = sgr.next()
                        P.act(sg[:, :n], zg[:, :n], AF.Sigmoid, reads=[t_zg], writes=[t_sg])
                        P.tt("dve", u[:, boff(c0):boff(c0) + n], za[:, :n], sg[:, :n], ALU.mult, reads=[t_za, t_sg], writes=[t_u])
                        if gen_live:
                            gen_live = next(gen, "end") != "end"
                    for (c0, n) in COLT:
                        p0 = boff(c0) - 15
                        cv, t_cv = pcv.next()
                        for k in range(31):
                            P.mm(cv[:, :n], dg[:, k * 128:(k + 1) * 128], u[:, p0 + k:p0 + k + n], k == 0, k == 30, reads=[t_dg, t_u], writes=[t_cv])
                        st, t_st = cst.next()
                        P.act(st[:, :n], cv[:, :n], AF.Identity, reads=[t_cv, self.t_cols], writes=[t_st], bias=self.col("dwb", ch))
                        P.dma("sp", self.convd[ch * 128:(ch + 1) * 128, c0:c0 + n], st[:, :n], reads=[t_st], writes=[self.t_convd[ch]])
            while gen_live:
                gen_live = next(gen, "end") != "end"
            P.barrier()
        with contextlib.ExitStack() as es:
            cvr = self.ring(es, "cv", 2, [128, 8 * 512], F32)
            cvbr = self.ring(es, "cvb", 2, [128, 8 * 512], BF16)
            sqr = self.ring(es, "csq", 2, [128, 8 * 512], BF16)
            pS = self.ring(es, "lnS", 2, [128, 512], F32, psum=True)
            pQ = self.ring(es, "lnQ", 2, [128, 512], F32, psum=True)
            mr = self.ring(es, "lnm", 2, [128, 512], F32)
            vr = self.ring(es, "lnv", 2, [128, 512], F32)
            tr_ = self.ring(es, "lnt", 3, [128, 512], F32)
            yr = self.ring(es, "lny", 3, [128, 512], BF16)
            for ci, (c0, n) in enumerate(COLT):
                cv, t_cv = cvr.next()
                P.dma("sp", cv[:].rearrange("p (c t) -> p c t", t=512)[:, :, 0:n],
                      self.convd[:, c0:c0 + n].rearrange("(c p) t -> p c t", p=128), reads=self.t_convd, writes=[t_cv])
                cvb, t_cvb = cvbr.next()
                P.dma("pool", cvb[:].rearrange("p (c t) -> p c t", t=512)[:, :, 0:n],
                      self.convd[:, c0:c0 + n].rearrange("(c p) t -> p c t", p=128), reads=self.t_convd, writes=[t_cvb])
                sq, t_sq = sqr.next()
                for ch in range(8):
                    P.act(sq[:, ch * 512:ch * 512 + n], cv[:, ch * 512:ch * 512 + n], AF.Square, reads=[t_cv], writes=[t_sq])
                S, t_S = pS.next()
                Q, t_Q = pQ.next()
                for ch in range(8):
                    P.mm(S[:, :n], c["ones_b"][:], cvb[:, ch * 512:ch * 512 + n], ch == 0, ch == 7, reads=[t_cvb, ct["ones_b"]], writes=[t_S])
                for ch in range(8):
                    P.mm(Q[:, :n], c["ones_b"][:], sq[:, ch * 512:ch * 512 + n], ch == 0, ch == 7, reads=[t_sq, ct["ones_b"]], writes=[t_Q])
                m, t_m = mr.next()
                P.act(m[:, :n], S[:, :n], AF.Identity, reads=[t_S], writes=[t_m], scale=1.0 / 1024)
                v, t_v = vr.next()
                P.tt("pool", v[:, :n], m[:, :n], m[:, :n], ALU.mult, reads=[t_m], writes=[t_v])
                P.stt(v[:, :n], Q[:, :n], 1.0 / 1024, v[:, :n], ALU.mult, ALU.subtract, reads=[t_Q, t_v], writes=[t_v])
                P.act(v[:, :n], v[:, :n], AF.Sqrt, reads=[t_v], writes=[t_v], bias=LN_EPS)
                P.recip(v[:, :n], v[:, :n], reads=[t_v], writes=[t_v])
                for ch in range(8):
                    t1, t_t1 = tr_.next()
                    P.tt("dve", t1[:, :n], cv[:, ch * 512:ch * 512 + n], m[:, :n], ALU.subtract, reads=[t_cv, t_m], writes=[t_t1])
                    P.tt("pool", t1[:, :n], t1[:, :n], v[:, :n], ALU.mult, reads=[t_t1, t_v], writes=[t_t1])
                    y, t_y = yr.next()
                    P.act(y[:, :n], t1[:, :n], AF.Silu, reads=[t_t1, self.t_cols], writes=[t_y], scale=self.col("lng", ch), bias=self.col("lnb", ch))
                    P.dma("sp" if ch % 2 else "act", self.mixd[ch * 128:(ch + 1) * 128, c0:c0 + n], y[:, :n], reads=[t_y], writes=[self.t_mixd[ch][ci]])
            P.barrier()

    def out_proj(self, l, w_out, tiles, with_mods=None):
        P, d = self.P, self.d
        with contextlib.ExitStack() as es:
            mixT = self.sb(es, "mixT", [128, 16 * T], BF16)
            t_mixc = [Tok() for _ in COLT]
            qi = 0
            for ci, (c0, n) in enumerate(COLT):
                if c0 < LC and 0 not in tiles:
                    continue
                P.dma("sp" if qi % 2 == 0 else "act", mixT[:, :].rearrange("p (k t) -> p k t", t=T)[:, :, c0:c0 + n],
                      self.mixd[:, c0:c0 + n].rearrange("(k p) t -> p k t", p=128), reads=[self.t_mixd[k][ci] for k in range(16)], writes=[t_mixc[ci]])
                qi += 1
            t_mix_of = lambda i: t_mixc[0 if i < 2 else 1 + (i - 2) // 4]
            gbc = self.sb(es, "gbc", [128, 2 * D], F32)
            t_gbc = Tok()
            for s in range(2):
                P.dma("sp", gbc[:, s * D:(s + 1) * D], self.modd[l, s, 2 * D:3 * D].partition_broadcast(128), writes=[t_gbc])
            wr = self.ring(es, "wout", 2, [128, 16 * 512], BF16)
            po = self.ring(es, "po", 4, [128, 512], F32, psum=True)
            xo = self.ring(es, "xo", 4, [128, 512], F32)
            tm = self.ring(es, "otm", 3, [128, 512], F32)
            wbs = {}

            def issue(ns_):
                wb_, t_wb_ = wr.next()
                self.load_w(wb_, t_wb_, w_out, ns_ * 512, 512)
                wbs[ns_] = (wb_, t_wb_)

            gen = None
            if with_mods is not None:
                self._pc2 = self.ps(es, "pc2b", [128, 128], F32)
                self._t_pc2 = Tok()
                gen = self.mods_gen(es, with_mods, mode="hw", SW=256)
            issue(0)
            for ns in range(4):
                if ns + 1 < 4:
                    issue(ns + 1)
                wb, t_wb = wbs[ns]
                for i in tiles:
                    if gen is not None and next(gen, "end") == "end":
                        gen = None
                    s = 1 if i < 2 else 0
                    o, t_o = po.next()
                    for k in range(16):
                        P.mm(o[:], mixT[:, k * T + i * 128:k * T + (i + 1) * 128], wb[:, k * 512:(k + 1) * 512], k == 0, k == 15,
                             reads=[t_mix_of(i), t_wb], writes=[t_o])
                    src, toks = self.src_rows(l, i)
                    x, t_x = xo.next()
                    P.dma("sp", x[:], src[:, ns * 512:(ns + 1) * 512], reads=([toks[ns]] if toks else []), writes=[t_x])
                    tmp, t_tmp = tm.next()
                    P.tt("dve", tmp[:], o[:], gbc[:, s * D + ns * 512:s * D + (ns + 1) * 512], ALU.mult, reads=[t_o, t_gbc], writes=[t_tmp])
                    P.tt("pool", x[:], x[:], tmp[:], ALU.add, reads=[t_x, t_tmp], writes=[t_x])
                    P.dma("act", self.xres[i * 128:(i + 1) * 128, ns * 512:(ns + 1) * 512], x[:], reads=[t_x], writes=[self.t_xres[i][ns]])
            while gen is not None:
                if next(gen, "end") == "end":
                    gen = None
            if ("xmid%d" % l) in self.taps:
                tp = self.tap("tap_xmid%d" % l, [T, D])
                for i in tiles:
                    P.dma("sp", tp[i * 128:(i + 1) * 128, :], self.xres[i * 128:(i + 1) * 128, :], reads=self.t_xres[i])
            P.barrier()

    def ffn(self, l, blocks):
        P, d = self.P, self.d
        BW = 768
        wg_d, wu_d, wd_d = d["ffn_w_gate"][l], d["ffn_w_up"][l], d["ffn_w_down"][l]
        with contextlib.ExitStack() as es:
            h2T = self.sb(es, "h2T", [128, 16 * BW], BF16)
            actT = self.sb(es, "actT", [128, NHC * BW], BF16)
            gbc = self.sb(es, "gfbc", [128, 2 * D], F32)
            t_gbc = Tok()
            for s in range(2):
                P.dma("sp", gbc[:, s * D:(s + 1) * D], self.modd[l, s, 5 * D:6 * D].partition_broadcast(128), writes=[t_gbc])
            xr = self.ring(es, "fx", 2, [128, D], F32)
            jr = self.ring(es, "fjk", 1, [128, D], BF16)
            ssr = self.ring(es, "fss", 4, [128, 4], F32)
            xnr = self.ring(es, "fxn", 1, [128, D], BF16)
            banks = [(self.ps(es, "fbank", [128, 512], F32), Tok()) for _ in range(8)]
            ptr = Ring(banks[0:2])
            wgr = self.ring(es, "wg", 2, [128, 16 * 256], BF16)
            wur = self.ring(es, "wu", 2, [128, 16 * 256], BF16)
            wdr = self.ring(es, "wd", 2, [128, 11 * 512], BF16)
            pG = Ring(banks[0:2])
            pU = Ring(banks[2:4])
            pD = banks[2:8]
            sgr = self.ring(es, "fsg", 2, [128, 512], F32)
            xo = self.ring(es, "fxo", 2, [128, 512], F32)
            tm = self.ring(es, "ftm", 2, [128, 512], F32)
            for blk in blocks:
                nb = len(blk)
                t_h2 = [(Tok(), Tok()) for _ in range(nb)]
                t_act = [[Tok() for _ in range(nb)] for _ in range(NHC)]
                ctiles = []
                c0 = 0
                while c0 < nb * 128:
                    n = min(512, nb * 128 - c0)
                    ctiles.append((c0, n))
                    c0 += n
                for j, i in enumerate(blk):
                    s = 1 if i < 2 else 0
                    xt, t_xt = xr.next()
                    P.dma("sp", xt[:], self.xres[i * 128:(i + 1) * 128, :], reads=self.t_xres[i], writes=[t_xt])
                    jk, t_jk = jr.next()
                    ss, t_ss = ssr.next()
                    P.act(jk[:], xt[:], AF.Square, reads=[t_xt], writes=[t_jk])
                    P.rsum(ss[:, 0:1], jk[:], reads=[t_jk], writes=[t_ss])
                    P.act(ss[:, 1:2], ss[:, 0:1], AF.Sqrt, reads=[t_ss], writes=[t_ss], scale=1.0 / D, bias=NORM_EPS)
                    P.recip(ss[:, 2:3], ss[:, 1:2], reads=[t_ss], writes=[t_ss])
                    xn, t_xn = xnr.next()
                    P.ts("dve", xn[:], xt[:], ss[:, 2:3], None, ALU.mult, reads=[t_xt, t_ss], writes=[t_xn])
                    for kq in range(4):
                        pt, t_pt = ptr.next()
                        ptb = pt[:, 0:256].bitcast(BF16)
                        for kk in range(4):
                            k = kq * 4 + kk
                            P.tr(ptb[:, kk * 128:(kk + 1) * 128], xn[:, k * 128:(k + 1) * 128], self.c["ident_b"][:],
                                 reads=[t_xn, self.ct["ident_b"]], writes=[t_pt])
                        for kk in range(4):
                            k = kq * 4 + kk
                            o = h2T[:, k * BW + j * 128:k * BW + (j + 1) * 128]
                            a, b = self.a_col(l, s, 1, k), self.b_col(l, s, 1, k)
                            if kq % 2 == 0:
                                P.act(o, ptb[:, kk * 128:(kk + 1) * 128], AF.Identity, reads=[t_pt, self.t_acols, self.t_modT[l][s]],
                                      writes=[t_h2[j][0]], scale=a, bias=b)
                            else:
                                P.ts("dve", o, ptb[:, kk * 128:(kk + 1) * 128], a, b, ALU.mult, ALU.add,
                                     reads=[t_pt, self.t_acols, self.t_modT[l][s]], writes=[t_h2[j][1]])
                for jp in range(NHC // 2):
                    wg, t_wg = wgr.next()
                    self.load_w(wg, t_wg, wg_d, jp * 256, 256)
                    wu, t_wu = wur.next()
                    self.load_w(wu, t_wu, wu_d, jp * 256, 256)
                    for jj in range(2):
                        hc = jp * 2 + jj
                        for (c0, n) in ctiles:
                            ht = [t for pr in t_h2[c0 // 128:(c0 + n) // 128] for t in pr]
                            G, t_G = pG.next()
                            U, t_U = pU.next()
                            for k in range(16):
                                P.mm(G[:, :n], wg[:, k * 256 + jj * 128:k * 256 + (jj + 1) * 128], h2T[:, k * BW + c0:k * BW + c0 + n],
                                     k == 0, k == 15, reads=[t_wg] + ht, writes=[t_G])
                            for k in range(16):
                                P.mm(U[:, :n], wu[:, k * 256 + jj * 128:k * 256 + (jj + 1) * 128], h2T[:, k * BW + c0:k * BW + c0 + n],
                                     k == 0, k == 15, reads=[t_wu] + ht, writes=[t_U])
                            sg, t_sg = sgr.next()
                            P.act(sg[:, :n], G[:, :n], AF.Silu, reads=[t_G], writes=[t_sg])
                            P.tt("dve", actT[:, hc * BW + c0:hc * BW + c0 + n], U[:, :n], sg[:, :n], ALU.mult, reads=[t_U, t_sg],
                                 writes=t_act[hc][c0 // 128:(c0 + n) // 128])
                for ns in range(4):
                    for pc in range(4):
                        wd, t_wd = wdr.next()
                        P.dma("pool", wd[:].rearrange("p (j c) -> p j c", c=512),
                              wd_d[pc * 11 * 128:(pc + 1) * 11 * 128, ns * 512:(ns + 1) * 512].rearrange("(j p) c -> p j c", p=128),
                              writes=[t_wd])
                        for j, i in enumerate(blk):
                            o, t_o = pD[j]
                            for jj in range(11):
                                hc = pc * 11 + jj
                                P.mm(o[:], actT[:, hc * BW + j * 128:hc * BW + (j + 1) * 128], wd[:, jj * 512:(jj + 1) * 512],
                                     hc == 0, hc == NHC - 1, reads=[t_act[hc][j], t_wd], writes=[t_o])
                    for j, i in enumerate(blk):
                        s = 1 if i < 2 else 0
                        o, t_o = pD[j]
                        x, t_x = xo.next()
                        P.dma("sp", x[:], self.xres[i * 128:(i + 1) * 128, ns * 512:(ns + 1) * 512], reads=[self.t_xres[i][ns]], writes=[t_x])
                        tmp, t_tmp = tm.next()
                        P.tt("dve", tmp[:], o[:], gbc[:, s * D + ns * 512:s * D + (ns + 1) * 512], ALU.mult, reads=[t_o, t_gbc], writes=[t_tmp])
                        P.tt("pool", x[:], x[:], tmp[:], ALU.add, reads=[t_x, t_tmp], writes=[t_x])
                        P.dma("act", self.xres[i * 128:(i + 1) * 128, ns * 512:(ns + 1) * 512], x[:], reads=[t_x], writes=[self.t_xres[i][ns]])
            if ("x%d" % l) in self.taps:
                tp = self.tap("tap_x%d" % l, [T, D])
                for blk in blocks:
                    for i in blk:
                        P.dma("sp", tp[i * 128:(i + 1) * 128, :], self.xres[i * 128:(i + 1) * 128, :], reads=self.t_xres[i])
            P.barrier()

    def final(self):
        P, d = self.P, self.d
        with contextlib.ExitStack() as es:
            gb = self.sb(es, "gfin", [128, D], F32)
            t_gb = Tok()
            P.dma("sp", gb[:], d["final_norm_g"][0, :].partition_broadcast(128), writes=[t_gb])
            xr = self.ring(es, "lx", 3, [128, D], F32)
            jr = self.ring(es, "ljk", 2, [128, D], F32)
            ssr = self.ring(es, "lss", 4, [128, 4], F32)
            for i in range(2, NT):
                xt, t_xt = xr.next()
                P.dma("sp", xt[:], self.xres[i * 128:(i + 1) * 128, :], reads=self.t_xres[i], writes=[t_xt])
                jk, t_jk = jr.next()
                ss, t_ss = ssr.next()
                P.act(jk[:], xt[:], AF.Square, reads=[t_xt], writes=[t_jk])
                P.rsum(ss[:, 0:1], jk[:], reads=[t_jk], writes=[t_ss])
                P.act(ss[:, 1:2], ss[:, 0:1], AF.Sqrt, reads=[t_ss], writes=[t_ss], scale=1.0 / D, bias=NORM_EPS)
                P.recip(ss[:, 2:3], ss[:, 1:2], reads=[t_ss], writes=[t_ss])
                P.stt(xt[:], xt[:], ss[:, 2:3], gb[:], ALU.mult, ALU.mult, reads=[t_xt, t_ss, t_gb], writes=[t_xt])
                P.dma("act", d["out"][(i - 2) * 128:(i - 1) * 128, :], xt[:], reads=[t_xt])
            P.barrier()


    def odd_hyena_proj(self, hT, t_hT):
        P, d, c, ct = self.P, self.d, self.c, self.ct
        LT = COLT[1:]
        with contextlib.ExitStack() as es:
            wr = self.ring(es, "whp", 2, [128, 16 * 512], BF16)
            zr = self.ring(es, "zraw", 2, [128, L + 2], F32)
            for z, t_z in zr.items:
                P.memset("pool", z[:], 0.0, writes=[t_z])
            zcr = self.ring(es, "zc", 8, [128, L], F32)
            pz = self.ring(es, "hpz", 3, [128, 512], F32, psum=True)
            ptr = self.ring(es, "hpt", 2, [128, 512], F32, psum=True)
            stg = self.ring(es, "hstg", 3, [128, 512], F32)
            for g in range(6):
                wb, t_wb = wr.next()
                self.load_w(wb, t_wb, d["o_w_in"], g * 512, 512)
                zcs = []
                for cc in range(4):
                    ch = g * 4 + cc
                    zraw, t_zraw = zr.next()
                    for (c0, n) in LT:
                        z, t_z = pz.next()
                        for k in range(16):
                            P.mm(z[:, :n], wb[:, k * 512 + cc * 128:k * 512 + (cc + 1) * 128], hT[:, k * T + c0:k * T + c0 + n],
                                 k == 0, k == 15, reads=[t_wb] + self.hT_toks(t_hT, c0, n), writes=[t_z])
                        P.cp("act", zraw[:, 1 + c0 - LC:1 + c0 - LC + n], z[:, :n], reads=[t_z], writes=[t_zraw])
                    zc, t_zc = zcr.next()
                    w = lambda k: self.col("shw", k * 24 + ch)
                    P.ts("dve", zc[:], zraw[:, 0:L], w(0), self.col("shb", ch), ALU.mult, ALU.add, reads=[t_zraw, self.t_cols], writes=[t_zc])
                    P.stt(zc[:], zraw[:, 1:L + 1], w(1), zc[:], ALU.mult, ALU.add, reads=[t_zraw, t_zc, self.t_cols], writes=[t_zc])
                    P.stt(zc[:], zraw[:, 2:L + 2], w(2), zc[:], ALU.mult, ALU.add, reads=[t_zraw, t_zc, self.t_cols], writes=[t_zc])
                    zcs.append((zc, t_zc))
                for tt in range(16):
                    pt, t_pt = ptr.next()
                    for cc in range(4):
                        zc, t_zc = zcs[cc]
                        P.tr(pt[:, cc * 128:(cc + 1) * 128], zc[:, tt * 128:(tt + 1) * 128], c["ident_f"][:], reads=[t_zc, ct["ident_f"]], writes=[t_pt])
                    st, t_st = stg.next()
                    P.cp("act" if tt % 2 == 0 else "dve", st[:], pt[:], reads=[t_pt], writes=[t_st])
                    P.dma("sp", self.zt[tt * 128:(tt + 1) * 128, g * 512:(g + 1) * 512], st[:], reads=[t_st], writes=[self.t_zt[tt][g]])
            if "zt" in self.taps:
                tp = self.tap("tap_zt", [L, 3072])
                for tt in range(16):
                    P.dma("sp", tp[tt * 128:(tt + 1) * 128, :], self.zt[tt * 128:(tt + 1) * 128, :], reads=self.t_zt[tt])
            P.barrier()

    def rope64(self, rings, src_psum, t_src, n, lc0, cosT, sinT, t_rope, out, t_out):
        P, c, ct = self.P, self.c, self.ct
        rf, ppq, t1r, t2r = rings
        f, t_f = rf.next()
        P.cp("act", f[0:64, :n], src_psum[0:64, :n], reads=[t_src], writes=[t_f])
        pq, t_pq = ppq.next()
        P.mm(pq[0:64, :n], c["perm64"][:], f[0:64, :n], True, True, reads=[t_f, ct["perm64"]], writes=[t_pq])
        t1, t_t1 = t1r.next()
        P.tt("pool", t1[0:64, :n], f[0:64, :n], cosT[:, lc0:lc0 + n], ALU.mult, reads=[t_f, t_rope], writes=[t_t1])
        t2, t_t2 = t2r.next()
        P.tt("dve", t2[0:64, :n], pq[0:64, :n], sinT[:, lc0:lc0 + n], ALU.mult, reads=[t_pq, t_rope], writes=[t_t2])
        P.tt("pool", out, t1[0:64, :n], t2[0:64, :n], ALU.add, reads=[t_t1, t_t2], writes=[t_out])

    def odd_mla(self, hT, t_hT):
        P, d, c, ct = self.P, self.d, self.c, self.ct
        LT = COLT[1:]
        with contextlib.ExitStack() as es:
            zqnT = self.sb(es, "zqnT", [128, 4 * L], BF16)
            t_zqn = [Tok() for _ in LT]
            ckvT = self.sb(es, "ckvT", [128, 2 * T], BF16)
            t_ckv = [Tok() for _ in COLT]
            krT = self.sb(es, "krT", [64, T], BF16)
            t_kr = [Tok() for _ in COLT]
            cosT = self.sb(es, "cos64", [64, L], F32)
            sinT = self.sb(es, "sin64", [64, L], F32)
            t_rope = Tok()
            P.dma("sp", cosT[:], d["cos64"][:, :], writes=[t_rope])
            P.dma("sp", sinT[:], d["sin64"][:, :], writes=[t_rope])
            rrings = (self.ring(es, "rf", 2, [64, 512], F32), self.ring(es, "rpq", 1, [128, 512], F32, psum=True),
                      self.ring(es, "rt1", 2, [64, 512], F32), self.ring(es, "rt2", 2, [64, 512], F32))
            with contextlib.ExitStack() as es2:
                wb = self.sb(es2, "wmla", [128, 16 * 832], BF16)
                t_wb = Tok()
                self.load_w(wb, t_wb, d["o_w_in"], 3072, 832)
                pz = self.ring(es2, "mpz", 4, [128, 512], F32, psum=True)
                pss = self.ring(es2, "mpss", 1, [128, 512], F32, psum=True)
                qfr = self.ring(es2, "mqf", 2, [128, 4 * 512], F32)
                q2r = self.ring(es2, "mq2", 2, [128, 4 * 512], BF16)
                rtr = self.ring(es2, "mrt", 2, [128, 512], F32)

                def normed(nch, wc0, tiles, gkey, dst, dstw, dcol, t_dst):
                    for ci, (c0, n) in tiles:
                        zs = []
                        for j in range(nch):
                            z, t_z = pz.next()
                            for k in range(16):
                                P.mm(z[:, :n], wb[:, k * 832 + wc0 + j * 128:k * 832 + wc0 + (j + 1) * 128], hT[:, k * T + c0:k * T + c0 + n],
                                     k == 0, k == 15, reads=[t_wb] + self.hT_toks(t_hT, c0, n), writes=[t_z])
                            zs.append((z, t_z))
                        qf, t_qf = qfr.next()
                        q2, t_q2 = q2r.next()
                        for j, (z, t_z) in enumerate(zs):
                            P.cp("act", qf[:, j * 512:j * 512 + n], z[:, :n], reads=[t_z], writes=[t_qf])
                            P.act(q2[:, j * 512:j * 512 + n], z[:, :n], AF.Square, reads=[t_z], writes=[t_q2])
                        ssp, t_ssp = pss.next()
                        for j in range(nch):
                            P.mm(ssp[:, :n], c["ones_b"][:], q2[:, j * 512:j * 512 + n], j == 0, j == nch - 1, reads=[t_q2, ct["ones_b"]], writes=[t_ssp])
                        rt, t_rt = rtr.next()
                        P.act(rt[:, :n], ssp[:, :n], AF.Sqrt, reads=[t_ssp], writes=[t_rt], scale=1.0 / (nch * 128), bias=NORM_EPS)
                        P.recip(rt[:, :n], rt[:, :n], reads=[t_rt], writes=[t_rt])
                        for j in range(nch):
                            P.stt(dst[:, j * dstw + dcol(c0):j * dstw + dcol(c0) + n], qf[:, j * 512:j * 512 + n], self.col(gkey, j), rt[:, :n],
                                  ALU.mult, ALU.mult, reads=[t_qf, t_rt, self.t_cols], writes=[t_dst[ci]])

                normed(4, 0, list(enumerate(LT)), "oqg", zqnT, L, lambda c0: c0 - LC, t_zqn)
                normed(2, 512, list(enumerate(COLT)), "okg", ckvT, T, lambda c0: c0, t_ckv)
                for ci, (c0, n) in enumerate(COLT):
                    z, t_z = pz.next()
                    for k in range(16):
                        P.mm(z[0:64, :n], wb[:, k * 832 + 768:k * 832 + 832], hT[:, k * T + c0:k * T + c0 + n], k == 0, k == 15,
                             reads=[t_wb] + self.hT_toks(t_hT, c0, n), writes=[t_z])
                    if c0 < LC:
                        P.cp("act", krT[:, c0:c0 + n], z[0:64, :n], reads=[t_z], writes=[t_kr[ci]])
                    else:
                        self.rope64(rrings, z, t_z, n, c0 - LC, cosT, sinT, t_rope, krT[:, c0:c0 + n], t_kr[ci])
                if "mla" in self.taps:
                    t1 = self.tap("tap_zqnT", [512, L], BF16)
                    for j in range(4):
                        P.dma("sp", t1[j * 128:(j + 1) * 128, :], zqnT[:, j * L:(j + 1) * L], reads=t_zqn)
                    t2 = self.tap("tap_ckvT", [256, T], BF16)
                    for j in range(2):
                        P.dma("sp", t2[j * 128:(j + 1) * 128, :], ckvT[:, j * T:(j + 1) * T], reads=t_ckv)
                    t3 = self.tap("tap_krT", [64, T], BF16)
                    P.dma("sp", t3[:, :], krT[:], reads=t_kr)
                P.barrier()
            with contextlib.ExitStack() as es2:
                wuq = self.sb(es2, "wuq", [128, 4 * 1536], BF16)
                t_wuq = Tok()
                self.load_w(wuq, t_wuq, d["o_w_uq"], 0, 1536, nk=4)
                wukv = self.sb(es2, "wukv", [128, 2 * 2048], BF16)
                t_wukv = Tok()
                self.load_w(wukv, t_wukv, d["o_w_ukv"], 0, 2048, nk=2)
                pu = self.ring(es2, "pS", 3, [128, 512], F32, psum=True)
                qnr = self.ring(es2, "qnh", 2, [128, L], BF16)
                qrr = self.ring(es2, "qrh", 2, [64, L], BF16)
                knr = self.ring(es2, "knh", 2, [128, T], BF16)
                vhr = self.ring(es2, "vh", 2, [128, NT * 128], BF16)
                cur = {}

                def prep(h):
                    qn, t_qn = qnr.next()
                    qr, t_qr = qrr.next()
                    kn, t_kn = knr.next()
                    vh, t_vh = vhr.next()
                    cur.update(qn=qn, t_qn=t_qn, qr=qr, t_qr=t_qr, kn=kn, t_kn=t_kn, vh=vh, t_vh=t_vh)
                    for ci, (c0, n) in enumerate(LT):
                        z, t_z = pu.next()
                        for k in range(4):
                            P.mm(z[:, :n], wuq[:, k * 1536 + h * 192:k * 1536 + h * 192 + 128], zqnT[:, k * L + c0 - LC:k * L + c0 - LC + n],
                                 k == 0, k == 3, reads=[t_wuq, t_zqn[ci]], writes=[t_z])
                        P.cp("act", qn[:, c0 - LC:c0 - LC + n], z[:, :n], reads=[t_z], writes=[t_qn])
                        z, t_z = pu.next()
                        for k in range(4):
                            P.mm(z[0:64, :n], wuq[:, k * 1536 + h * 192 + 128:k * 1536 + h * 192 + 192], zqnT[:, k * L + c0 - LC:k * L + c0 - LC + n],
                                 k == 0, k == 3, reads=[t_wuq, t_zqn[ci]], writes=[t_z])
                        self.rope64(rrings, z, t_z, n, c0 - LC, cosT, sinT, t_rope, qr[:, c0 - LC:c0 - LC + n], t_qr)
                    for ci, (c0, n) in enumerate(COLT):
                        z, t_z = pu.next()
                        for k in range(2):
                            P.mm(z[:, :n], wukv[:, k * 2048 + h * 256:k * 2048 + h * 256 + 128], ckvT[:, k * T + c0:k * T + c0 + n],
                                 k == 0, k == 1, reads=[t_wukv, t_ckv[ci]], writes=[t_z])
                        P.cp("act", kn[:, c0:c0 + n], z[:, :n], reads=[t_z], writes=[t_kn])
                    for i0 in range(0, NT, 4):
                        z, t_z = pu.next()
                        nn = min(4, NT - i0)
                        for ii in range(nn):
                            i = i0 + ii
                            kci = 0 if i < 2 else 1 + (i - 2) // 4
                            for k in range(2):
                                P.mm(z[:, ii * 128:(ii + 1) * 128], ckvT[:, k * T + i * 128:k * T + (i + 1) * 128],
                                     wukv[:, k * 2048 + h * 256 + 128:k * 2048 + h * 256 + 256], k == 0, k == 1,
                                     reads=[t_wukv, t_ckv[kci]], writes=[t_z])
                        P.cp("act", vh[:, i0 * 128:(i0 + nn) * 128], z[:, 0:nn * 128], reads=[t_z], writes=[t_vh])

                def s_mm(h, S, t_S, kc, c0, n):
                    kci = 0 if kc < 2 else 1 + (kc - 2) // 4
                    P.mm(S[:, :n], cur["kn"][:, kc * 128:(kc + 1) * 128], cur["qn"][:, c0 - LC:c0 - LC + n], True, False,
                         reads=[cur["t_kn"], cur["t_qn"]], writes=[t_S])
                    P.mm(S[:, :n], krT[:, kc * 128:(kc + 1) * 128], cur["qr"][:, c0 - LC:c0 - LC + n], False, True,
                         reads=[t_kr[kci], cur["t_qr"]], writes=[t_S])

                def v_ap(h, kc):
                    return cur["vh"][:, kc * 128:(kc + 1) * 128], [cur["t_vh"]]

                q_tiles = [(ci, c0, n, list(range(NT))) for ci, (c0, n) in enumerate(COLT) if ci > 0]
                self.attention(es2, 8, q_tiles, s_mm, v_ap, 192 ** -0.5, 1024, prep=prep, pS=pu)
                P.barrier()

    def hyena_filters(self):
        P, d, c, ct = self.P, self.d, self.c, self.ct
        PI = math.pi
        with contextlib.ExitStack() as es:
            featT = self.sb(es, "featT", [33, L], F32)
            w1 = self.sb(es, "fw1", [33, 64], F32)
            w2 = self.sb(es, "fw2", [64, 64], F32)
            w3 = self.sb(es, "fw3", [64, 64], F32)
            w4 = self.sb(es, "fw4", [64, 4096], BF16)
            tneg = self.sb(es, "tneg", [128, 16], F32)
            dbc = self.sb(es, "dbc", [128, 1024], F32)
            t_in = Tok()
            P.dma("sp", featT[:], d["featT"][:, :], writes=[t_in])
            P.dma("sp", w1[:], d["o_f_w1"][:, :], writes=[t_in])
            P.dma("sp", w2[:], d["o_f_w2"][:, :], writes=[t_in])
            P.dma("sp", w3[:], d["o_f_w3"][:, :], writes=[t_in])
            P.dma("pool", w4[:], d["o_f_w4"][:, :], writes=[t_in])
            P.dma("sp", tneg[:], d["tneg"][:, :], writes=[t_in])
            P.dma("sp", dbc[:], d["deltas"][0, :].partition_broadcast(128), writes=[t_in])
            P.tt("dve", self.fcols[:, 6:8], self.fcols[:, 0:2], self.fcols[:, 3:5], ALU.mult, reads=[self.t_fcols], writes=[self.t_fcols])
            bfr2 = self.sb(es, "bfr2", [64, 1], F32)
            t_bfr2 = Tok()
            P.tt("dve", bfr2[:], self.fcols[:, 2:3], self.fcols[:, 5:6], ALU.mult, reads=[self.t_fcols], writes=[t_bfr2])
            hh = [self.sb(es, "hh", [64, L], F32) for _ in range(2)]
            t_hh = [Tok(), Tok()]
            pf = self.ring(es, "pf", 3, [128, 512], F32, psum=True)
            argr = self.ring(es, "farg", 2, [64, 512], F32)
            m1r = self.ring(es, "fm1", 2, [64, 512], F32)
            m2r = self.ring(es, "fm2", 2, [64, 512], F32)
            layers = [(w1, 33, featT, t_in), (w2, 64, hh[0], t_hh[0]), (w3, 64, hh[1], t_hh[1])]
            for i, (w, kdim, src, t_src) in enumerate(layers):
                dst, t_dst = hh[i % 2], t_hh[i % 2]
                bcol = self.fcols[:, 6 + i:7 + i] if i < 2 else bfr2[:, 0:1]
                for q in range(4):
                    p, t_p = pf.next()
                    P.mm(p[0:64, :], w[0:kdim, :], src[0:kdim, q * 512:(q + 1) * 512], True, True, reads=[t_in, t_src], writes=[t_p])
                    a, t_a = argr.next()
                    P.ts("dve", a[:], p[0:64, :], self.fcols[:, 3 + i:4 + i], bcol, ALU.mult, ALU.add, reads=[t_p, self.t_fcols, t_bfr2], writes=[t_a])
                    m1, t_m1 = m1r.next()
                    P.ts("dve", m1[:], a[:], PI, -2.0 * PI, ALU.is_gt, ALU.mult, reads=[t_a], writes=[t_m1])
                    m2, t_m2 = m2r.next()
                    P.ts("dve", m2[:], a[:], -PI, 2.0 * PI, ALU.is_lt, ALU.mult, reads=[t_a], writes=[t_m2])
                    P.tt("pool", m1[:], m1[:], m2[:], ALU.add, reads=[t_m1, t_m2], writes=[t_m1])
                    P.tt("pool", a[:], a[:], m1[:], ALU.add, reads=[t_a, t_m1], writes=[t_a])
                    P.act(dst[:, q * 512:(q + 1) * 512], a[:], AF.Sin, reads=[t_a], writes=[t_dst])
            h3f, t_h3f = hh[0], t_hh[0]
            h3 = self.sb(es, "hh3b", [64, L], BF16)
            t_h3 = Tok()
            P.cp("act", h3[:], h3f[:], reads=[t_h3f], writes=[t_h3])
            if "hh3" in self.taps:
                tp = self.tap("tap_hh3", [64, L])
                P.dma("sp", tp[:, :], h3f[:], reads=[t_h3f])
            hd = [self.sb(es, "hd", [128, 16 * 512], F32) for _ in range(2)]
            t_hd = [[Tok() for _ in range(16)] for _ in range(2)]
            A = self.sb(es, "fA", [128, 16 * 512], BF16)
            B = self.sb(es, "fB", [128, 16 * 512], BF16)
            t_A, t_B = [Tok() for _ in range(16)], [Tok() for _ in range(16)]
            decr = self.ring(es, "dec", 2, [128, 512], F32)
            abr = self.ring(es, "fab", 3, [128, 512], BF16)
            pn = self.ring(es, "pn", 1, [128, 512], F32, psum=True)
            rnr = self.ring(es, "frn", 2, [128, 512], F32)
            cfr = self.ring(es, "fcf", 3, [128, 2048], BF16)
            sfr = self.ring(es, "fsf", 3, [128, 2048], BF16)
            pk = self.ring(es, "pk", 2, [128, 512], F32, psum=True)
            pki = self.ring(es, "pki", 2, [128, 512], F32, psum=True)
            kst = self.ring(es, "kst", 2, [128, 512], F32)
            ksti = self.ring(es, "ksti", 2, [128, 512], F32)
            for n in range(2):
                for s in range(2):
                    rns = []
                    for dirn in range(2):
                        col0 = dirn * 2048 + n * 1024 + s * 512
                        H, t_H = hd[dirn], t_hd[dirn]
                        nrm, t_nrm = pn.next()

                        def gen_hraw(jt_):
                            p_, t_p_ = pf.next()
                            P.mm(p_[:], h3[0:64, jt_ * 128:(jt_ + 1) * 128], w4[0:64, col0:col0 + 512], True, True, reads=[t_h3, t_in], writes=[t_p_])
                            return p_, t_p_
                        pend = [gen_hraw(0)]
                        for jt in range(16):
                            if jt + 1 < 16:
                                pend.append(gen_hraw(jt + 1))
                            p, t_p = pend.pop(0)
                            dec, t_dec = decr.next()
                            P.act(dec[:], dbc[:, s * 512:(s + 1) * 512], AF.Exp, reads=[t_in], writes=[t_dec], scale=tneg[:, jt:jt + 1])
                            P.tt("dve", H[:, jt * 512:(jt + 1) * 512], p[:], dec[:], ALU.mult, reads=[t_p, t_dec], writes=[t_H[jt]])
                            ab, t_ab = abr.next()
                            P.act(ab[:], H[:, jt * 512:(jt + 1) * 512], AF.Abs, reads=[t_H[jt]], writes=[t_ab])
                            P.mm(nrm[:], c["ones_b"][:], ab[:], jt == 0, jt == 15, reads=[t_ab, ct["ones_b"]], writes=[t_nrm])
                        rn, t_rn = rnr.next()
                        P.ts("dve", rn[:], nrm[:], 1e-6, None, ALU.add, reads=[t_nrm], writes=[t_rn])
                        P.recip(rn[:], rn[:], reads=[t_rn], writes=[t_rn])
                        rns.append((rn, t_rn))
                    H0, H1 = hd[0], hd[1]
                    P.memset("dve", H1[0:1, 0:512], 0.0, writes=[t_hd[1][0]])
                    for jt in range(16):
                        sl = slice(jt * 512, (jt + 1) * 512)
                        P.tt("dve", H0[:, sl], H0[:, sl], rns[0][0][:], ALU.mult, reads=[t_hd[0][jt], rns[0][1]], writes=[t_hd[0][jt]])
                        P.tt("dve", H1[:, sl], H1[:, sl], rns[1][0][:], ALU.mult, reads=[t_hd[1][jt], rns[1][1]], writes=[t_hd[1][jt]])
                        P.tt("dve", A[:, sl], H0[:, sl], H1[:, sl], ALU.add, reads=[t_hd[0][jt], t_hd[1][jt]], writes=[t_A[jt]])
                        P.tt("pool", B[:, sl], H1[:, sl], H0[:, sl], ALU.subtract, reads=[t_hd[0][jt], t_hd[1][jt]], writes=[t_B[jt]])
                    if "hfilt" in self.taps and n == 0 and s == 0:
                        tp = self.tap("tap_hd0", [128, 16 * 512])
                        P.dma("sp", tp[:, :], hd[0][:], reads=t_hd[0])
                        tp = self.tap("tap_hd1", [128, 16 * 512])
                        P.dma("sp", tp[:, :], hd[1][:], reads=t_hd[1])
                    for fc in range(16):
                        cf, t_cf = cfr.next()
                        P.dma("sp", cf[:], d["dft_cf"][fc, :, :], writes=[t_cf])
                        sf, t_sf = sfr.next()
                        P.dma("sp", sf[:], d["dft_sf"][fc, :, :], writes=[t_sf])
                        kr, t_kr = pk.next()
                        ki, t_ki = pki.next()
                        for jt in range(16):
                            P.mm(kr[:], cf[:, jt * 128:(jt + 1) * 128], A[:, jt * 512:(jt + 1) * 512], jt == 0, jt == 15, reads=[t_cf, t_A[jt]], writes=[t_kr])
                        for jt in range(16):
                            P.mm(ki[:], sf[:, jt * 128:(jt + 1) * 128], B[:, jt * 512:(jt + 1) * 512], jt == 0, jt == 15, reads=[t_sf, t_B[jt]], writes=[t_ki])
                        st, t_st = kst.next()
                        P.cp("act", st[:], kr[:], reads=[t_kr], writes=[t_st])
                        P.dma("act", self.kfd[n, 0, fc * 128:(fc + 1) * 128, s * 512:(s + 1) * 512], st[:], reads=[t_st], writes=[self.t_kfd[n][s][fc]])
                        st2, t_st2 = ksti.next()
                        P.cp("dve", st2[:], ki[:], reads=[t_ki], writes=[t_st2])
                        P.dma("pool", self.kfd[n, 1, fc * 128:(fc + 1) * 128, s * 512:(s + 1) * 512], st2[:], reads=[t_st2], writes=[self.t_kfd[n][s][fc]])
            P.barrier()

    def hyena_conv(self):
        P, d, c, ct = self.P, self.d, self.c, self.ct
        with contextlib.ExitStack() as es:
            ya = self.sb(es, "ya", [128, 16 * 1024], BF16)
            yb = self.sb(es, "yb", [128, 16 * 1024], BF16)
            t_y = {id(ya): [Tok() for _ in range(16)], id(yb): [Tok() for _ in range(16)]}
            skb = self.sb(es, "skb", [128, 2048], F32)
            t_skb = Tok()
            P.dma("sp", skb[:], d["o_skip"].rearrange("a c -> (a c)").partition_broadcast(128), writes=[t_skb])
            for tt in range(16):
                P.dma("pool", ya[:, tt * 1024:(tt + 1) * 1024], self.zt[tt * 128:(tt + 1) * 128, 2048:3072],
                      reads=self.t_zt[tt][4:6], writes=[t_y[id(ya)][tt]])
            Yre = self.sb(es, "Yre", [128, 16 * 512], BF16)
            Yim = self.sb(es, "Yim", [128, 16 * 512], BF16)
            t_Yre = [Tok() for _ in range(16)]
            t_Yim = [Tok() for _ in range(16)]
            tw = self.ring(es, "tw", 6, [128, 2048], BF16)
            kre_r = self.ring(es, "kre", 2, [128, 512], F32)
            kim_r = self.ring(es, "kim", 2, [128, 512], F32)
            pU = self.ring(es, "pUr", 2, [128, 512], F32, psum=True)
            pV = self.ring(es, "pUs", 2, [128, 512], F32, psum=True)
            pY = self.ring(es, "pY", 2, [128, 512], F32, psum=True)
            tmp = {k: self.ring(es, "hc" + k, 2, [128, 512], F32) for k in ("t1", "t2", "t3", "t4", "ts", "r")}
            xgr = self.ring(es, "xg", 2, [128, 512], F32)
            for n in range(2):
                yin, yout = (ya, yb) if n == 0 else (yb, ya)
                t_in, t_out = t_y[id(yin)], t_y[id(yout)]
                for s in range(2):
                    for fc in range(16):
                        cf, t_cf = tw.next()
                        P.dma("sp", cf[:], d["dft_cf"][fc, :, :], writes=[t_cf])
                        sf, t_sf = tw.next()
                        P.dma("sp", sf[:], d["dft_sf"][fc, :, :], writes=[t_sf])
                        kre, t_kre = kre_r.next()
                        P.dma("sp", kre[:], self.kfd[n, 0, fc * 128:(fc + 1) * 128, s * 512:(s + 1) * 512], reads=[self.t_kfd[n][s][fc]], writes=[t_kre])
                        kim, t_kim = kim_r.next()
                        P.dma("sp", kim[:], self.kfd[n, 1, fc * 128:(fc + 1) * 128, s * 512:(s + 1) * 512], reads=[self.t_kfd[n][s][fc]], writes=[t_kim])
                        U, t_U = pU.next()
                        V, t_V = pV.next()
                        for tt in range(16):
                            P.mm(U[:], cf[:, tt * 128:(tt + 1) * 128], yin[:, tt * 1024 + s * 512:tt * 1024 + (s + 1) * 512], tt == 0, tt == 15,
                                 reads=[t_cf, t_in[tt]], writes=[t_U])
                        for tt in range(16):
                            P.mm(V[:], sf[:, tt * 128:(tt + 1) * 128], yin[:, tt * 1024 + s * 512:tt * 1024 + (s + 1) * 512], tt == 0, tt == 15,
                                 reads=[t_sf, t_in[tt]], writes=[t_V])
                        t1, t_t1 = tmp["t1"].next()
                        P.tt("dve", t1[:], U[:], kre[:], ALU.mult, reads=[t_U, t_kre], writes=[t_t1])
                        t2, t_t2 = tmp["t2"].next()
                        P.tt("dve", t2[:], V[:], kim[:], ALU.mult, reads=[t_V, t_kim], writes=[t_t2])
                        P.tt("pool", Yre[:, fc * 512:(fc + 1) * 512], t1[:], t2[:], ALU.add, reads=[t_t1, t_t2], writes=[t_Yre[fc]])
                        t3, t_t3 = tmp["t3"].next()
                        P.tt("dve", t3[:], U[:], kim[:], ALU.mult, reads=[t_U, t_kim], writes=[t_t3])
                        t4, t_t4 = tmp["t4"].next()
                        P.tt("dve", t4[:], V[:], kre[:], ALU.mult, reads=[t_V, t_kre], writes=[t_t4])
                        P.tt("pool", Yim[:, fc * 512:(fc + 1) * 512], t3[:], t4[:], ALU.subtract, reads=[t_t3, t_t4], writes=[t_Yim[fc]])
                    for tt in range(16):
                        ci_, t_ci = tw.next()
                        P.dma("sp", ci_[:], d["dft_ci"][tt, :, :], writes=[t_ci])
                        si_, t_si = tw.next()
                        P.dma("sp", si_[:], d["dft_si"][tt, :, :], writes=[t_si])
                        Y, t_Y = pY.next()
                        for fc in range(16):
                            P.mm(Y[:], ci_[:, fc * 128:(fc + 1) * 128], Yre[:, fc * 512:(fc + 1) * 512], fc == 0, False, reads=[t_ci, t_Yre[fc]], writes=[t_Y])
                        for fc in range(16):
                            P.mm(Y[:], si_[:, fc * 128:(fc + 1) * 128], Yim[:, fc * 512:(fc + 1) * 512], False, fc == 15, reads=[t_si, t_Yim[fc]], writes=[t_Y])
                        xg, t_xg = xgr.next()
                        P.dma("sp", xg[:], self.zt[tt * 128:(tt + 1) * 128, n * 1024 + s * 512:n * 1024 + (s + 1) * 512],
                              reads=[self.t_zt[tt][n * 2 + s]], writes=[t_xg])
                        tsk, t_tsk = tmp["ts"].next()
                        P.tt("pool", tsk[:], yin[:, tt * 1024 + s * 512:tt * 1024 + (s + 1) * 512], skb[:, n * 1024 + s * 512:n * 1024 + (s + 1) * 512],
                             ALU.mult, reads=[t_in[tt], t_skb], writes=[t_tsk])
                        r, t_r = tmp["r"].next()
                        P.stt(r[:], Y[:], 1.0 / 2048, tsk[:], ALU.mult, ALU.add, reads=[t_Y, t_tsk], writes=[t_r])
                        P.tt("pool", yout[:, tt * 1024 + s * 512:tt * 1024 + (s + 1) * 512], r[:], xg[:], ALU.mult, reads=[t_r, t_xg], writes=[t_out[tt]])
            yfin, t_fin = ya, t_y[id(ya)]
            if "yh" in self.taps:
                tp = self.tap("tap_yh", [L, 1024], BF16)
                for tt in range(16):
                    P.dma("sp", tp[tt * 128:(tt + 1) * 128, :], yfin[:, tt * 1024:(tt + 1) * 1024], reads=[t_fin[tt]])
            ptr = self.ring(es, "ypt", 1, [128, 512], F32, psum=True)
            stg = self.ring(es, "ystg", 2, [128, L], BF16)
            for ch in range(8):
                st, t_st = stg.next()
                for q in range(4):
                    pt, t_pt = ptr.next()
                    ptb = pt[:, 0:256].bitcast(BF16)
                    for kk in range(4):
                        tt = q * 4 + kk
                        P.tr(ptb[:, kk * 128:(kk + 1) * 128], yfin[:, tt * 1024 + ch * 128:tt * 1024 + (ch + 1) * 128], c["ident_b"][:],
                             reads=[t_fin[tt], ct["ident_b"]], writes=[t_pt])
                    P.cp("act", st[:, q * 512:(q + 1) * 512], ptb[:, :], reads=[t_pt], writes=[t_st])
                for ci in range(1, 5):
                    c0, nn = COLT[ci]
                    P.dma("sp", self.mixd[ch * 128:(ch + 1) * 128, c0:c0 + nn], st[:, c0 - LC:c0 - LC + nn], reads=[t_st], writes=[self.t_mixd[ch][ci]])
            P.barrier()

    def layer_odd(self, l):
        P, d = self.P, self.d
        with contextlib.ExitStack() as es:
            hT = self.sb(es, "hT", [128, 16 * T], BF16)
            t_hT = [(Tok(), Tok()) for _ in range(NT)]
            self.phaseA(l, hT, t_hT)
            if self.stop == "A1":
                return False
            self.odd_hyena_proj(hT, t_hT)
            if self.stop == "zt":
                return False
            self.odd_mla(hT, t_hT)
        if self.stop == "mla":
            return False
        self.hyena_filters()
        if self.stop == "filt":
            return False
        self.hyena_conv()
        if self.stop == "hconv":
            return False
        lat = list(range(2, NT))
        self.out_proj(l, d["o_w_out"], lat)
        if self.stop == "xmid1":
            return False
        self.ffn(l, [lat[0:6], lat[6:12], lat[12:16]])
        return self.stop != "x1"

    def layer_even(self, l):
        P, d = self.P, self.d
        with contextlib.ExitStack() as es:
            hT = self.sb(es, "hT", [128, 16 * T], BF16)
            import os
            if os.environ.get("EVAC_SINGLE"):
                t_hT = [(lambda t: (t, t))(Tok()) for _ in range(NT)]
            else:
                t_hT = [(Tok(), Tok()) for _ in range(NT)]
            self.phaseA(l, hT, t_hT)
            if self.stop == "A0":
                return False
            self.even_conv(hT, t_hT)
            if self.stop == "conv0":
                return False
            with contextlib.ExitStack() as es2:
                qT = self.sb(es2, "qT", [128, 8 * T], BF16)
                t_qT = [[Tok() for _ in COLT] for _ in range(8)]
                kT = self.sb(es2, "kT", [128, 2 * T], BF16)
                t_kT = [[Tok() for _ in COLT] for _ in range(2)]
                Vsb = self.sb(es2, "Vsb", [128, NT * 256], BF16)
                t_V = [Tok() for _ in range(NT)]
                self.even_qkv(hT, t_hT, qT, t_qT, kT, t_kT, Vsb, t_V)
                if self.stop == "qkv0":
                    return False
                self.even_attention(qT, t_qT, kT, t_kT, Vsb, t_V)
        if self.stop == "att0":
            return False
        self.out_proj(l, d["e_w_out"], list(range(NT)))
        if self.stop == "xmid0":
            return False
        self.ffn(l, [list(range(0, 6)), list(range(6, 12)), list(range(12, 18))])
        return self.stop != "x0"

    def build(self):
        import os
        self.setup()
        if not os.environ.get("SKIP0"):
            self.phase0()
        ok = self.stop != "mod"
        if ok and not os.environ.get("SKIP_EVEN"):
            ok = self.layer_even(0)
        if ok and hasattr(self, "layer_odd"):
            ok = self.layer_odd(1)
        if ok:
            self.final()
        self.pes.close()
        return self.nc


def make_in_maps(inputs, names=None):
    n = 8
    consts = host_constants()
    shared = {}
    for name, shape in WEIGHT_SPECS:
        shared[name] = np.ascontiguousarray(np.asarray(inputs[name], dtype=np.float32).reshape(shape))
    for name, arr in consts.items():
        shared[name] = arr
    maps = []
    x = np.asarray(inputs["x"], dtype=np.float32)
    ctx = np.asarray(inputs["ctx"], dtype=np.float32)
    cc = np.asarray(inputs["c"], dtype=np.float32)
    c_ctx = np.asarray(inputs["c_ctx"], dtype=np.float32)
    for b in range(n):
        m = dict(shared)
        m["x"] = np.ascontiguousarray(x[b])
        m["ctx"] = np.ascontiguousarray(ctx[b])
        m["cvec"] = np.ascontiguousarray(np.stack([cc[b], c_ctx], axis=0))
        if names is not None:
            m = {k: v for k, v in m.items() if k in names}
        maps.append(m)
    return maps


def kernel(**inputs):
    kb = KB()
    nc = kb.build()
    maps = make_in_maps(inputs, set(kb.d.keys()))
    res = run_bass_kernel_spmd(nc, maps, core_ids=list(range(8)))
    return np.stack([np.asarray(r["out"], dtype=np.float32) for r in res.results], axis=0)
```

```python
import contextlib
import math
import numpy as np
import ml_dtypes
import concourse.bass as bass
import concourse.mybir as mybir
from concourse.bass_utils import run_bass_kernel_spmd

F32 = mybir.dt.float32
BF16 = mybir.dt.bfloat16
AF = mybir.ActivationFunctionType
ALU = mybir.AluOpType


class Tok:
    __slots__ = ("w", "r")

    def __init__(self):
        self.w = None
        self.r = {}


class Op:
    __slots__ = ("eng", "fn", "deps", "sig", "cnt", "dma", "dsem", "dval", "idx", "epoch", "drain")


class Prog:
    ENGS = ("pe", "act", "dve", "pool", "sp")
    DMAQ = ("sp", "pool", "act")
    NDMA = 8

    def __init__(self, nc):
        self.nc = nc
        self.ops = []
        self.start = 0
        self.epoch = 0
        self.nops = 0
        self.ndma = {e: 0 for e in self.ENGS}
        self.cnt = {e: 0 for e in self.ENGS}
        self.waited = {e: {} for e in self.ENGS}
        self.esem = {e: nc.alloc_semaphore(name="s_" + e) for e in self.ENGS}
        self.dsem = {}
        for e in self.DMAQ:
            for j in range(self.NDMA):
                self.dsem[(e, j)] = nc.alloc_semaphore(name="d_%s%d" % (e, j))

    def add(self, eng, fn, reads=(), writes=(), dma=False, drain=False):
        op = Op()
        op.eng, op.fn, op.dma, op.sig, op.cnt, op.drain = eng, fn, dma, False, 0, drain
        op.idx = self.nops
        self.nops += 1
        op.epoch = self.epoch
        deps = {}
        raw = set()
        for t in reads:
            if t.w is not None:
                deps[t.w.idx] = t.w
                raw.add(t.w.idx)
        for t in writes:
            if t.w is not None:
                deps[t.w.idx] = t.w
            for o in t.r.values():
                deps[o.idx] = o
        op.deps = [d for d in deps.values() if d.epoch == self.epoch and
                   (d.dma or d.eng != eng or (eng != "pe" and d.idx in raw))]
        for d in op.deps:
            d.sig = True
        key = ("d", op.idx) if dma else eng
        for t in reads:
            t.r[key] = op
        for t in writes:
            t.w = op
            t.r = {}
        if dma:
            k = self.ndma[eng]
            self.ndma[eng] += 1
            op.dsem = (eng, k % self.NDMA)
            op.dval = 16 * (k // self.NDMA + 1)
        self.ops.append(op)
        return op

    def barrier(self):
        toks = {e: Tok() for e in self.ENGS}
        for e in self.ENGS:
            self.add(e, lambda h: (h.drain(), h.nop())[1], writes=[toks[e]], drain=True)
        for e in self.ENGS:
            self.add(e, lambda h: h.nop(), reads=list(toks.values()))
        self.epoch += 1
        self.flush()

    def flush(self):
        nc = self.nc
        ops = self.ops[self.start:]
        self.start = len(self.ops)
        for op in ops:
            if not op.dma and op.sig:
                self.cnt[op.eng] += 1
                op.cnt = self.cnt[op.eng]
        per = {e: [o for o in ops if o.eng == e] for e in self.ENGS}
        issued = {e: 0 for e in self.ENGS}
        base = {e: self.ndma[e] - sum(1 for o in per[e] if o.dma) for e in self.ENGS}

        def run(e, h):
            waited = self.waited[e]
            n_issued = base[e]
            for op in per[e]:
                need = {}
                for d in op.deps:
                    if d.dma:
                        k, v = ("D",) + d.dsem, d.dval
                    else:
                        k, v = ("E", d.eng), d.cnt
                    if need.get(k, 0) < v:
                        need[k] = v
                if op.dma and op.dval > 16:
                    k = ("D",) + op.dsem
                    need[k] = max(need.get(k, 0), op.dval - 16)
                if op.drain and e in self.DMAQ:
                    for j in range(self.NDMA):
                        c = (n_issued - j + self.NDMA - 1) // self.NDMA
                        if c > 0:
                            need[("D", e, j)] = max(need.get(("D", e, j), 0), 16 * c)
                for k, v in need.items():
                    if waited.get(k, 0) < v:
                        s = self.dsem[k[1:]] if k[0] == "D" else self.esem[k[1]]
                        h.wait_ge(s, v)
                        waited[k] = v
                ins = op.fn(h)
                inc = None
                if op.dma:
                    n_issued += 1
                    ins.then_inc(self.dsem[op.dsem], 16)
                    inc = (("D",) + op.dsem, 16)
                elif op.sig:
                    ins.then_inc(self.esem[e], 1)
                    inc = (("E", e), 1)
                self.simlog[e].append((dict(need), inc, op.idx))

        self.simlog = {e: [] for e in self.ENGS}
        with nc.Block() as block:
            if per["pe"]:
                block.tensor(lambda h: run("pe", h))
            if per["act"]:
                block.scalar(lambda h: run("act", h))
            if per["dve"]:
                block.vector(lambda h: run("dve", h))
            if per["pool"]:
                block.gpsimd(lambda h: run("pool", h))
            if per["sp"]:
                block.sync(lambda h: run("sp", h))
        self.simcheck()

    def simcheck(self):
        if not hasattr(self, "simval"):
            self.simval = {}
        val = self.simval
        pos = {e: 0 for e in self.ENGS}
        progress = True
        while progress:
            progress = False
            for e in self.ENGS:
                lg = self.simlog[e]
                while pos[e] < len(lg):
                    need, inc, idx = lg[pos[e]]
                    if all(val.get(k, 0) >= v for k, v in need.items()):
                        if inc is not None:
                            val[inc[0]] = val.get(inc[0], 0) + inc[1]
                        pos[e] += 1
                        progress = True
                    else:
                        break
        for e in self.ENGS:
            if pos[e] < len(self.simlog[e]):
                need, inc, idx = self.simlog[e][pos[e]]
                bad = {k: (v, val.get(k, 0)) for k, v in need.items() if val.get(k, 0) < v}
                raise RuntimeError("DEADLOCK on %s at op %d: waits %s" % (e, idx, bad))

    def dma(self, q, out, in_, reads=(), writes=(), **kw):
        return self.add(q, lambda h: h.dma_start(out=out, in_=in_, **kw), reads, writes, dma=True)

    def mm(self, out, lhsT, rhs, start, stop, reads=(), writes=()):
        return self.add("pe", lambda h: h.matmul(out, lhsT, rhs, start=start, stop=stop), reads, writes)

    def tr(self, out, in_, ident, reads=(), writes=()):
        return self.add("pe", lambda h: h.transpose(out, in_, ident), reads, writes)

    def act(self, out, in_, func, reads=(), writes=(), **kw):
        return self.add("act", lambda h: h.activation(out, in_, func, **kw), reads, writes)

    def ts(self, eng, out, in0, s1, s2, op0, op1=None, reads=(), writes=()):
        if op1 is None:
            return self.add(eng, lambda h: h.tensor_scalar(out, in0, s1, None, op0), reads, writes)
        return self.add(eng, lambda h: h.tensor_scalar(out, in0, s1, s2, op0, op1), reads, writes)

    def tt(self, eng, out, in0, in1, op, reads=(), writes=()):
        return self.add(eng, lambda h: h.tensor_tensor(out, in0, in1, op), reads, writes)

    def stt(self, out, in0, scalar, in1, op0, op1, reads=(), writes=()):
        return self.add("dve", lambda h: h.scalar_tensor_tensor(out, in0, scalar, in1, op0, op1), reads, writes)

    def cp(self, eng, out, in_, reads=(), writes=()):
        if eng == "act":
            return self.add(eng, lambda h: h.copy(out, in_), reads, writes)
        return self.add(eng, lambda h: h.tensor_copy(out, in_), reads, writes)

    def memset(self, eng, ap, val, writes=()):
        return self.add(eng, lambda h: h.memset(ap, val), (), writes)

    def rsum(self, out, in_, reads=(), writes=()):
        return self.add("dve", lambda h: h.tensor_reduce(out, in_, mybir.AxisListType.X, ALU.add), reads, writes)

    def recip(self, out, in_, reads=(), writes=()):
        return self.add("dve", lambda h: h.reciprocal(out, in_), reads, writes)


class Ring:
    def __init__(self, items):
        self.items = items
        self.i = 0

    def next(self):
        it = self.items[self.i % len(self.items)]
        self.i += 1
        return it


D = 2048
L = 2048
LC = 256
T = L + LC
NT = T // 128
COLT = [(0, 256), (256, 512), (768, 512), (1280, 512), (1792, 512)]
FFH = 5632
NHC = FFH // 128
NORM_EPS = 1e-6
LN_EPS = 1e-5
THETA = 10000.0
IN_EVEN = 3584
IN_ODD = 3904

WEIGHT_SPECS = [
    ("ada_w", (2, 2048, 12288)), ("ada_b", (2, 12288)), ("norm_mix_g", (2, 2048)), ("norm_ffn_g", (2, 2048)),
    ("e_w_in", (2048, 3584)), ("e_w_out", (2048, 2048)), ("e_dw_w", (31, 1024)), ("e_dw_b", (1, 1024)),
    ("e_ln_g", (1, 1024)), ("e_ln_b", (1, 1024)), ("e_qn_g", (1, 128)), ("e_kn_g", (1, 128)),
    ("o_w_in", (2048, 3904)), ("o_w_out", (2048, 2048)), ("o_short_w", (3, 3072)), ("o_short_b", (1, 3072)),
    ("o_f_w1", (33, 64)), ("o_f_b1", (1, 64)), ("o_f_w2", (64, 64)), ("o_f_b2", (1, 64)), ("o_f_w3", (64, 64)),
    ("o_f_b3", (1, 64)), ("o_f_w4", (64, 4096)), ("o_f_freq", (3, 64)), ("o_skip", (2, 1024)),
    ("o_q_norm_g", (1, 512)), ("o_kv_norm_g", (1, 256)), ("o_w_uq", (512, 1536)), ("o_w_ukv", (256, 2048)),
    ("ffn_w_gate", (2, 2048, 5632)), ("ffn_w_up", (2, 2048, 5632)), ("ffn_w_down", (2, 5632, 2048)),
    ("final_norm_g", (1, 2048)),
]

_CONST_CACHE = {}


def host_constants():
    if _CONST_CACHE:
        return _CONST_CACHE
    bf = ml_dtypes.bfloat16
    c = {}
    c["ident_f"] = np.eye(128, dtype=np.float32)
    c["ident_b"] = np.eye(128, dtype=np.float32).astype(bf)
    c["ones_f"] = np.ones((128, 128), np.float32)
    c["ones_b"] = np.ones((128, 128), np.float32).astype(bf)
    p128 = np.zeros((128, 128), np.float32)
    for m in range(128):
        if (m % 64) < 32:
            p128[m + 32, m] = -1.0
        else:
            p128[m - 32, m] = 1.0
    c["perm128"] = p128
    c["perm128b"] = p128.astype(bf)
    p64 = np.zeros((64, 64), np.float32)
    for m in range(64):
        if (m % 32) < 16:
            p64[m + 16, m] = -1.0
        else:
            p64[m - 16, m] = 1.0
    c["perm64"] = p64
    t = np.arange(L)
    row = (t // 64).astype(np.float32)
    col = (t % 64).astype(np.float32)
    inv32 = (THETA ** (-np.arange(32, dtype=np.float32) / 32)).astype(np.float32)
    inv16 = (THETA ** (-np.arange(16, dtype=np.float32) / 16)).astype(np.float32)
    ang128 = np.zeros((128, L), np.float32)
    for d in range(128):
        pos = row if d < 64 else col
        ang128[d] = pos * inv32[d % 32]
    c["cos128"] = np.cos(ang128).astype(np.float32)
    c["sin128"] = np.sin(ang128).astype(np.float32)
    ang64 = np.zeros((64, L), np.float32)
    for d in range(64):
        pos = row if d < 32 else col
        ang64[d] = pos * inv16[d % 16]
    c["cos64"] = np.cos(ang64).astype(np.float32)
    c["sin64"] = np.sin(ang64).astype(np.float32)
    tl = np.linspace(0.0, 1.0, L, dtype=np.float32)[:, None]
    w = (2.0 * math.pi / L) * np.arange(L, dtype=np.float32)[:, None]
    f = np.linspace(1e-4, 15, 16, dtype=np.float32)[None, :]
    feat = np.concatenate([tl, np.cos(f * w), -np.sin(f * w)], axis=-1).astype(np.float32)
    c["featT"] = np.ascontiguousarray(feat.T)
    c["tneg"] = np.ascontiguousarray((-tl[:, 0]).reshape(16, 128).T).astype(np.float32)
    max_decay = math.log(1e-2) / 0.3
    min_decay = math.log(1e-2) / 1.5
    c["deltas"] = np.abs(np.linspace(min_decay, max_decay, 1024, dtype=np.float32)).reshape(1, 1024).astype(np.float32)
    ff = (np.arange(L, dtype=np.float64) + 0.5) * (2.0 * math.pi / 4096.0)
    tt = np.arange(L, dtype=np.float64)
    ang = np.outer(tt, ff)
    C = np.cos(ang)
    S = np.sin(ang)
    c["dft_cf"] = np.ascontiguousarray(C.reshape(16, 128, 16, 128).transpose(2, 1, 0, 3).reshape(16, 128, 2048)).astype(bf)
    c["dft_sf"] = np.ascontiguousarray(S.reshape(16, 128, 16, 128).transpose(2, 1, 0, 3).reshape(16, 128, 2048)).astype(bf)
    CT = C.T
    ST = -S.T
    c["dft_ci"] = np.ascontiguousarray(CT.reshape(16, 128, 16, 128).transpose(2, 1, 0, 3).reshape(16, 128, 2048)).astype(bf)
    c["dft_si"] = np.ascontiguousarray(ST.reshape(16, 128, 16, 128).transpose(2, 1, 0, 3).reshape(16, 128, 2048)).astype(bf)
    _CONST_CACHE.update(c)
    return c


CONST_DT = {"ident_b": BF16, "ones_b": BF16, "perm128b": BF16, "dft_cf": BF16, "dft_sf": BF16, "dft_ci": BF16, "dft_si": BF16}


class KB:
    def __init__(self, stop=None, taps=()):
        self.stop = stop
        self.taps = set(taps)
        self.nc = bass.Bass("TRN2", target_bir_lowering=False)
        self.P = Prog(self.nc)
        self.uid = 0
        nc = self.nc
        specs = {"x": ([L, D], F32), "ctx": ([LC, D], F32), "cvec": ([2, D], F32)}
        for name, shape in WEIGHT_SPECS:
            specs[name] = (list(shape), F32)
        for name, arr in host_constants().items():
            specs[name] = (list(arr.shape), CONST_DT.get(name, F32))
        kb = self

        class LazyD(dict):
            def __missing__(self, name):
                shape, dt = specs[name]
                ap = nc.dram_tensor(name, shape, dt, kind="ExternalInput").ap()
                self[name] = ap
                return ap

        d = LazyD()
        d["out"] = nc.dram_tensor("out", [L, D], F32, kind="ExternalOutput").ap()
        self.d = d
        self.outs = ["out"]
        self.modd = nc.dram_tensor("modd", [2, 2, 12288], F32).ap()
        self.xres = nc.dram_tensor("xres", [T, D], F32).ap()
        self.t_xres = [[Tok() for _ in range(4)] for _ in range(NT)]
        self.mixd = nc.dram_tensor("mixd", [D, T], BF16).ap()
        self.t_mixd = [[Tok() for _ in range(len(COLT))] for _ in range(16)]
        self.convd = nc.dram_tensor("convd", [1024, T], F32).ap()
        self.t_convd = [Tok() for _ in range(8)]
        self.zt = nc.dram_tensor("zt", [L, 3072], F32).ap()
        self.t_zt = [[Tok() for _ in range(6)] for _ in range(16)]
        self.kfd = nc.dram_tensor("kfd", [2, 2, L, 1024], F32).ap()
        self.t_kfd = [[[Tok() for _ in range(16)] for _ in range(2)] for _ in range(2)]
        self.pes = contextlib.ExitStack()

    def nm(self, name):
        self.uid += 1
        return "%s_%d" % (name, self.uid)

    def sb(self, es, name, shape, dt):
        return es.enter_context(self.nc.sbuf_tensor(self.nm(name), list(shape), dt))

    def ps(self, es, name, shape, dt=F32):
        return es.enter_context(self.nc.psum_tensor(self.nm(name), list(shape), dt))

    def ring(self, es, name, n, shape, dt, psum=False):
        f = self.ps if psum else self.sb
        return Ring([(f(es, name, shape, dt), Tok()) for _ in range(n)])

    def tap(self, name, shape, dt=F32):
        ap = self.nc.dram_tensor(name, list(shape), dt, kind="ExternalOutput").ap()
        self.outs.append(name)
        return ap

    def setup(self):
        P, es, d = self.P, self.pes, self.d
        self.c = {}
        self.ct = {}
        for name, shape, dt in [("ident_f", [128, 128], F32), ("ident_b", [128, 128], BF16), ("ones_f", [128, 128], F32),
                                ("ones_b", [128, 128], BF16), ("perm128b", [128, 128], BF16), ("perm64", [64, 64], F32)]:
            t = self.sb(es, name, shape, dt)
            tk = Tok()
            P.dma("sp", t[:], d[name][:, :], writes=[tk])
            self.c[name] = t
            self.ct[name] = tk
        self.cols = self.sb(es, "cols", [128, 448], F32)
        self.t_cols = Tok()
        self.ncol = 0
        self.colreg = {}
        self.fcols = self.sb(es, "fcols", [64, 8], F32)
        self.t_fcols = Tok()
        self.modT = [[self.sb(es, "modT", [128, 96], F32) for _ in range(2)] for _ in range(2)]
        self.t_modT = [[Tok() for _ in range(2)] for _ in range(2)]
        self.acols = self.sb(es, "acols", [128, 128], F32)
        self.t_acols = Tok()
        self.condTb = self.sb(es, "condTb", [128, 32], BF16)
        self.t_condTb = Tok()
        with contextlib.ExitStack() as s2:
            stage = self.ring(s2, "cstage", 2, [128, 128], F32)
            cps = self.ring(s2, "cps", 2, [128, 128], F32, psum=True)

            def load_cols(key, src, R):
                off = self.ncol
                self.ncol += R
                self.colreg[key] = off
                st, t_st = stage.next()
                P.dma("sp", st[0:R, :], src, writes=[t_st])
                pt, t_pt = cps.next()
                P.tr(pt[:, 0:R], st[0:R, :], self.c["ident_f"][0:R, 0:R], reads=[t_st, self.ct["ident_f"]], writes=[t_pt])
                P.cp("dve", self.cols[:, off:off + R], pt[:, 0:R], reads=[t_pt], writes=[self.t_cols])

            r128 = lambda ap: ap.rearrange("a (k p) -> (a k) p", p=128)
            load_cols("gmix", r128(d["norm_mix_g"]), 32)
            load_cols("gffn", r128(d["norm_ffn_g"]), 32)
            dww = r128(d["e_dw_w"])
            load_cols("dww", dww[0:128, :], 128)
            load_cols("dww2", dww[128:248, :], 120)
            load_cols("dwb", r128(d["e_dw_b"]), 8)
            load_cols("lng", r128(d["e_ln_g"]), 8)
            load_cols("lnb", r128(d["e_ln_b"]), 8)
            load_cols("qng", d["e_qn_g"], 1)
            load_cols("kng", d["e_kn_g"], 1)
            load_cols("shw", r128(d["o_short_w"]), 72)
            load_cols("shb", r128(d["o_short_b"]), 24)
            load_cols("oqg", r128(d["o_q_norm_g"]), 4)
            load_cols("okg", r128(d["o_kv_norm_g"]), 2)
            st, t_st = stage.next()
            for j, nme in enumerate(["o_f_b1", "o_f_b2", "o_f_b3"]):
                P.dma("sp", st[j:j + 1, 0:64], d[nme], writes=[t_st])
            P.dma("sp", st[3:6, 0:64], d["o_f_freq"], writes=[t_st])
            pt, t_pt = cps.next()
            P.tr(pt[0:64, 0:6], st[0:6, 0:64], self.c["ident_f"][0:6, 0:6], reads=[t_st, self.ct["ident_f"]], writes=[t_pt])
            P.cp("dve", self.fcols[:, 0:6], pt[0:64, 0:6], reads=[t_pt], writes=[self.t_fcols])
            P.barrier()

    def col(self, key, j=0, n=1):
        o = self.colreg[key] + j
        return self.cols[:, o:o + n]

    def phase0(self):
        P, d = self.P, self.d
        with contextlib.ExitStack() as es:
            cc = self.sb(es, "cc", [32, 128], F32)
            t_cc = Tok()
            cv = d["cvec"]
            P.dma("sp", cc[0:16, :], cv[0:1, :].rearrange("a (k p) -> (a k) p", p=128), writes=[t_cc])
            P.dma("sp", cc[16:32, :], cv[1:2, :].rearrange("a (k p) -> (a k) p", p=128), writes=[t_cc])
            P.act(cc[:], cc[:], AF.Silu, reads=[t_cc], writes=[t_cc])
            self._pc2 = self.ps(es, "pc", [128, 128], F32)
            self._t_pc2 = Tok()
            P.tr(self._pc2[:, 0:32], cc[0:32, :], self.c["ident_f"][0:32, 0:32], reads=[t_cc, self.ct["ident_f"]], writes=[self._t_pc2])
            P.cp("dve", self.condTb[:], self._pc2[:, 0:32], reads=[self._t_pc2], writes=[self.t_condTb])
            for _ in self.mods_gen(es, 0, mode="mixed"):
                pass
            if "mod" in self.taps:
                tp = self.tap("tap_mod", [2, 2, 12288])
                P.dma("sp", tp[0:1, :, :], self.modd[0:1, :, :])
            P.barrier()

    def mods_gen(self, es, l, mode="dma", SW=512):
        P, d, c, ct = self.P, self.d, self.c, self.ct
        nsl = 12288 // SW
        FW = 16 * SW
        lhs = lambda k: self.condTb[:, :].rearrange("p (s k) -> p k s", k=16)[:, k, :]
        wr = Ring([(self.sb(es, "adawb", [128, FW], BF16), [Tok(), Tok(), Tok()]) for _ in range(2 if mode == "dma" else 3)])
        if mode != "dma":
            sr = self.ring(es, "adaws", 2, [128, FW], F32)
        br = self.ring(es, "adabb", 2, [2, SW], F32)
        pm = self.ring(es, "pmb", 1 if mode == "dma" else 2, [2, SW], F32, psum=True)
        st = self.ring(es, "mstb", 2, [2, SW], F32)
        t_modd = [Tok() for _ in range(nsl)]
        c1 = (FW // 3) // 64 * 64
        c2 = 2 * c1
        nh = 0
        for n in range(nsl):
            w, t_w = wr.next()
            src = d["ada_w"][l, :, n * SW:(n + 1) * SW].rearrange("(k p) c -> p k c", p=128)
            if mode == "hw" or (mode == "mixed" and n % 3 != 0):
                sf, t_sf = sr.next()
                P.dma("sp" if nh % 2 == 0 else "act", sf[:].rearrange("p (k c) -> p k c", c=SW), src, writes=[t_sf])
                nh += 1
                P.cp("dve", w[:, 0:c1], sf[:, 0:c1], reads=[t_sf], writes=[t_w[0]])
                P.cp("act", w[:, c1:c2], sf[:, c1:c2], reads=[t_sf], writes=[t_w[1]])
                P.cp("pool", w[:, c2:FW], sf[:, c2:FW], reads=[t_sf], writes=[t_w[2]])
            else:
                P.dma("pool", w[:].rearrange("p (k c) -> p k c", c=SW), src, writes=t_w)
            b, t_b = br.next()
            P.dma("sp", b[:], d["ada_b"][l, n * SW:(n + 1) * SW].partition_broadcast(2), writes=[t_b])
            p, t_p = pm.next()
            for k in range(16):
                P.mm(p[:], lhs(k), w[:, k * SW:(k + 1) * SW], k == 0, k == 15, reads=[self.t_condTb] + t_w, writes=[t_p])
            s_, t_s = st.next()
            P.tt("dve", s_[:], p[:], b[:], ALU.add, reads=[t_p, t_b], writes=[t_s])
            P.dma("sp", self.modd[l, :, n * SW:(n + 1) * SW], s_[:], reads=[t_s], writes=[t_modd[n]])
            yield
        st2 = self.ring(es, "mst2b", 2, [96, 128], F32)
        for s in range(2):
            t2, t_t2 = st2.next()
            P.dma("sp", t2[:], self.modd[l, s, :].rearrange("(r c) -> r c", c=128), reads=t_modd, writes=[t_t2])
            P.tr(self._pc2[:, 0:96], t2[0:96, :], c["ident_f"][0:96, 0:96], reads=[t_t2, ct["ident_f"]], writes=[self._t_pc2])
            P.cp("dve", self.modT[l][s][:], self._pc2[:, 0:96], reads=[self._t_pc2], writes=[self.t_modT[l][s]])
            for kind, (sc0, gkey) in enumerate([(16, "gmix"), (64, "gffn")]):
                o = ((l * 2 + s) * 2 + kind) * 16
                P.stt(self.acols[:, o:o + 16], self.modT[l][s][:, sc0:sc0 + 16], 1.0, self.col(gkey, l * 16, 16),
                      ALU.add, ALU.mult, reads=[self.t_modT[l][s], self.t_cols], writes=[self.t_acols])
        yield

    def a_col(self, l, s, kind, k):
        o = ((l * 2 + s) * 2 + kind) * 16 + k
        return self.acols[:, o:o + 1]

    def b_col(self, l, s, kind, k):
        o = (0 if kind == 0 else 48) + k
        return self.modT[l][s][:, o:o + 1]

    def src_rows(self, l, i):
        if l == 0:
            if i < 2:
                return self.d["ctx"][i * 128:(i + 1) * 128, :], []
            return self.d["x"][(i - 2) * 128:(i - 1) * 128, :], []
        return self.xres[i * 128:(i + 1) * 128, :], self.t_xres[i]

    def norm_tile(self, es_rings, l, i, kind, src, src_toks, dst, dst_col, dst_tok, dstw):
        P = self.P
        xr, jr, ssr, xnr, ptr = es_rings
        s = 1 if i < 2 else 0
        xt, t_xt = xr.next()
        P.dma("sp" if i % 2 == 0 else "pool", xt[:], src, reads=src_toks, writes=[t_xt])
        jk, t_jk = jr.next()
        ss, t_ss = ssr.next()
        P.act(jk[:], xt[:], AF.Square, reads=[t_xt], writes=[t_jk])
        import os
        step = int(os.environ.get("NT_STEP", "9"))
        if step < 2:
            return
        P.rsum(ss[:, 0:1], jk[:], reads=[t_jk], writes=[t_ss])
        if step < 3:
            return
        P.act(ss[:, 1:2], ss[:, 0:1], AF.Sqrt, reads=[t_ss], writes=[t_ss], scale=1.0 / D, bias=NORM_EPS)
        P.recip(ss[:, 2:3], ss[:, 1:2], reads=[t_ss], writes=[t_ss])
        if step < 4:
            return
        xn, t_xn = xnr.next()
        P.ts("dve", xn[:], xt[:], ss[:, 2:3], None, ALU.mult, reads=[t_xt, t_ss], writes=[t_xn])
        if step < 5:
            return
        pt, t_pt = ptr.next()
        for k in range(16):
            P.tr(pt[:, k * 128:(k + 1) * 128], xn[:, k * 128:(k + 1) * 128], self.c["ident_b"][:],
                 reads=[t_xn, self.ct["ident_b"]], writes=[t_pt])
        if step < 6:
            return
        for k in range(16):
            o = dst[:, k * dstw + dst_col:k * dstw + dst_col + 128]
            a, b = self.a_col(l, s, kind, k), self.b_col(l, s, kind, k)
            if k < 8:
                P.act(o, pt[:, k * 128:(k + 1) * 128], AF.Identity, reads=[t_pt, self.t_acols, self.t_modT[l][s]],
                      writes=[dst_tok[0]], scale=a, bias=b)
            else:
                P.ts("dve", o, pt[:, k * 128:(k + 1) * 128], a, b, ALU.mult, ALU.add,
                     reads=[t_pt, self.t_acols, self.t_modT[l][s]], writes=[dst_tok[1]])
        return ss, t_ss

    def norm_rings(self, es):
        return (self.ring(es, "nx", 4, [128, D], F32), self.ring(es, "njk", 3, [128, D], F32),
                self.ring(es, "nss", 6, [128, 4], F32), self.ring(es, "nxn", 3, [128, D], BF16),
                self.ring(es, "npt", 3, [128, D], BF16, psum=True))

    def phaseA(self, l, hT, t_hT):
        with contextlib.ExitStack() as es:
            rings = self.norm_rings(es)
            for i in range(NT):
                src, toks = self.src_rows(l, i)
                self.norm_tile(rings, l, i, 0, src, toks, hT, i * 128, t_hT[i], T)
            if ("hT%d" % l) in self.taps:
                tp = self.tap("tap_hT%d" % l, [D, T], BF16)
                for k in range(16):
                    self.P.dma("sp", tp[k * 128:(k + 1) * 128, :], hT[:, k * T:(k + 1) * T], reads=self.hT_toks(t_hT, 0, T))
            self.P.barrier()

    def hT_toks(self, t_hT, c0, n):
        return [t for pr in t_hT[c0 // 128:(c0 + n) // 128] for t in pr]

    def load_w(self, wb, t_wb, w, c0, ncol, nk=16, q="pool"):
        self.P.dma(q, wb[:, 0:nk * ncol].rearrange("p (k c) -> p k c", c=ncol),
                   w[0:nk * 128, c0:c0 + ncol].rearrange("(k p) c -> p k c", p=128), writes=[t_wb])

    def even_qkv(self, hT, t_hT, qT, t_qT, kT, t_kT, Vsb, t_V):
        P, d, c, ct = self.P, self.d, self.c, self.ct
        with contextlib.ExitStack() as es:
            wr = self.ring(es, "wqkv", 2, [128, 16 * 512], BF16)
            cosT = self.sb(es, "cosT", [128, L], F32)
            sinT = self.sb(es, "sinT", [128, L], F32)
            t_rope = Tok()
            P.dma("sp", cosT[:], d["cos128"][:, :], writes=[t_rope])
            P.dma("sp", sinT[:], d["sin128"][:, :], writes=[t_rope])
            pz = self.ring(es, "pz", 3, [128, 512], F32, psum=True)
            pss = self.ring(es, "pss", 2, [128, 512], F32, psum=True)
            ppq = self.ring(es, "ppq", 2, [128, 512], F32, psum=True)
            wk = {nme: self.ring(es, nme, 2, [128, 512], BF16 if nme in ("q2", "qn") else F32)
                  for nme in ("qf", "q2", "rt", "qn", "t1", "t2")}

            def qk_process(z, t_z, n, c0, gcol, out, t_out):
                qf, t_qf = wk["qf"].next()
                P.cp("act", qf[:, :n], z[:, :n], reads=[t_z], writes=[t_qf])
                q2, t_q2 = wk["q2"].next()
                P.act(q2[:, :n], z[:, :n], AF.Square, reads=[t_z], writes=[t_q2])
                ssp, t_ssp = pss.next()
                P.mm(ssp[:, :n], c["ones_b"][:], q2[:, :n], True, True, reads=[t_q2, ct["ones_b"]], writes=[t_ssp])
                rt, t_rt = wk["rt"].next()
                P.act(rt[:, :n], ssp[:, :n], AF.Sqrt, reads=[t_ssp], writes=[t_rt], scale=1.0 / 128, bias=NORM_EPS)
                P.recip(rt[:, :n], rt[:, :n], reads=[t_rt], writes=[t_rt])
                if c0 < LC:
                    P.stt(out, qf[:, :n], gcol, rt[:, :n], ALU.mult, ALU.mult, reads=[t_qf, t_rt, self.t_cols], writes=[t_out])
                    return
                qn, t_qn = wk["qn"].next()
                P.stt(qn[:, :n], qf[:, :n], gcol, rt[:, :n], ALU.mult, ALU.mult, reads=[t_qf, t_rt, self.t_cols], writes=[t_qn])
                pq, t_pq = ppq.next()
                P.mm(pq[:, :n], c["perm128b"][:], qn[:, :n], True, True, reads=[t_qn, ct["perm128b"]], writes=[t_pq])
                t1, t_t1 = wk["t1"].next()
                P.tt("pool", t1[:, :n], qn[:, :n], cosT[:, c0 - LC:c0 - LC + n], ALU.mult, reads=[t_qn, t_rope], writes=[t_t1])
                t2, t_t2 = wk["t2"].next()
                P.tt("dve", t2[:, :n], pq[:, :n], sinT[:, c0 - LC:c0 - LC + n], ALU.mult, reads=[t_pq, t_rope], writes=[t_t2])
                P.tt("pool", out, t1[:, :n], t2[:, :n], ALU.add, reads=[t_t1, t_t2], writes=[t_out])

            def proj(wb, t_wb, wc0, c0, n):
                z, t_z = pz.next()
                for k in range(16):
                    P.mm(z[:, :n], wb[:, k * 512 + wc0:k * 512 + wc0 + 128], hT[:, k * T + c0:k * T + c0 + n], k == 0, k == 15,
                         reads=[t_wb] + self.hT_toks(t_hT, c0, n), writes=[t_z])
                return z, t_z

            def run_items(items):
                nxt = proj(*items[0][0]) if items else None
                for idx, (pargs, qargs) in enumerate(items):
                    cur = nxt
                    if idx + 1 < len(items):
                        nxt = proj(*items[idx + 1][0])
                    qk_process(cur[0], cur[1], *qargs)

            wb6, t_wb6 = wr.next()
            self.load_w(wb6, t_wb6, d["e_w_in"], 3072, 512)
            wq = []
            wbq, t_wbq = wr.next()
            self.load_w(wbq, t_wbq, d["e_w_in"], 2048, 512)
            wq.append((wbq, t_wbq))
            items = []
            for hk in range(2):
                for ci, (c0, n) in enumerate(COLT):
                    items.append(((wb6, t_wb6, hk * 128, c0, n), (n, c0, self.col("kng"), kT[:, hk * T + c0:hk * T + c0 + n], t_kT[hk][ci])))
            run_items(items)
            for i in range(NT):
                z, t_z = pz.next()
                for k in range(16):
                    P.mm(z[:, 0:256], hT[:, k * T + i * 128:k * T + (i + 1) * 128], wb6[:, k * 512 + 256:k * 512 + 512], k == 0, k == 15,
                         reads=[t_wb6] + list(t_hT[i]), writes=[t_z])
                P.cp("act", Vsb[:, i * 256:(i + 1) * 256], z[:, 0:256], reads=[t_z], writes=[t_V[i]])
            wbq, t_wbq = wr.next()
            self.load_w(wbq, t_wbq, d["e_w_in"], 2048 + 512, 512)
            wq.append((wbq, t_wbq))
            items = []
            for g in range(2):
                wb, t_wb = wq[g]
                for hh in range(4):
                    h = g * 4 + hh
                    for ci, (c0, n) in enumerate(COLT):
                        items.append(((wb, t_wb, hh * 128, c0, n), (n, c0, self.col("qng"), qT[:, h * T + c0:h * T + c0 + n], t_qT[h][ci])))
            run_items(items)
            if "qk0" in self.taps:
                tq = self.tap("tap_qT", [1024, T], BF16)
                tk_ = self.tap("tap_kT", [256, T], BF16)
                tv = self.tap("tap_V", [T, 256], BF16)
                for h in range(8):
                    P.dma("sp", tq[h * 128:(h + 1) * 128, :], qT[:, h * T:(h + 1) * T], reads=t_qT[h])
                for h in range(2):
                    P.dma("sp", tk_[h * 128:(h + 1) * 128, :], kT[:, h * T:(h + 1) * T], reads=t_kT[h])
                for i in range(NT):
                    P.dma("sp", tv[i * 128:(i + 1) * 128, :], Vsb[:, i * 256:(i + 1) * 256], reads=[t_V[i]])
            P.barrier()

    def attention(self, es, n_heads, q_tiles, s_mm, v_ap, scale, out_row0, prep=None, pS=None):
        P, c, ct = self.P, self.c, self.ct
        if pS is None:
            pS = self.ring(es, "pS", 3, [128, 512], F32, psum=True)
        pO = self.ring(es, "pO", 2, [128, 512], F32, psum=True)
        pL = self.ring(es, "pL", 2, [128, 512], F32, psum=True)
        PT = self.ring(es, "PT", 4, [128, 512], BF16)
        rc = self.ring(es, "rc", 2, [128, 512], F32)
        ao = self.ring(es, "ao", 2, [128, 512], BF16)
        for h in range(n_heads):
            if prep is not None:
                prep(h)
            for (ci, c0, n, keys) in q_tiles:
                O, t_O = pO.next()
                Ls, t_L = pL.next()
                Sq = []
                LOOK = 2
                for j in range(min(LOOK, len(keys))):
                    S, t_S = pS.next()
                    s_mm(h, S, t_S, keys[j], c0, n)
                    Sq.append((S, t_S))
                for j, kc in enumerate(keys):
                    if j + LOOK < len(keys):
                        S, t_S = pS.next()
                        s_mm(h, S, t_S, keys[j + LOOK], c0, n)
                        Sq.append((S, t_S))
                    S, t_S = Sq.pop(0)
                    pt, t_pt = PT.next()
                    P.act(pt[:, :n], S[:, :n], AF.Exp, reads=[t_S], writes=[t_pt], scale=scale)
                    va, vt = v_ap(h, kc)
                    P.mm(O[:, :n], va, pt[:, :n], j == 0, j == len(keys) - 1, reads=[t_pt] + vt, writes=[t_O])
                    P.mm(Ls[:, :n], c["ones_b"][:], pt[:, :n], j == 0, j == len(keys) - 1, reads=[t_pt, ct["ones_b"]], writes=[t_L])
                r, t_r = rc.next()
                P.recip(r[:, :n], Ls[:, :n], reads=[t_L], writes=[t_r])
                a, t_a = ao.next()
                P.tt("dve", a[:, :n], O[:, :n], r[:, :n], ALU.mult, reads=[t_O, t_r], writes=[t_a])
                rk = (out_row0 + h * 128) // 128
                P.dma("sp", self.mixd[out_row0 + h * 128:out_row0 + (h + 1) * 128, c0:c0 + n], a[:, :n], reads=[t_a],
                      writes=[self.t_mixd[rk][ci]])

    def even_attention(self, qT, t_qT, kT, t_kT, Vsb, t_V):
        P = self.P
        with contextlib.ExitStack() as es:
            def s_mm(h, S, t_S, kc, c0, n):
                kvh = h // 4
                ci = [x[0] for x in COLT].index(c0)
                kci = 0 if kc < 2 else 1 + (kc - 2) // 4
                P.mm(S[:, :n], kT[:, kvh * T + kc * 128:kvh * T + (kc + 1) * 128], qT[:, h * T + c0:h * T + c0 + n], True, True,
                     reads=[t_kT[kvh][kci], t_qT[h][ci]], writes=[t_S])

            def v_ap(h, kc):
                kvh = h // 4
                return Vsb[:, kc * 256 + kvh * 128:kc * 256 + (kvh + 1) * 128], [t_V[kc]]

            q_tiles = [(0, 0, 256, [0, 1])] + [(ci, c0, n, list(range(NT))) for ci, (c0, n) in enumerate(COLT) if ci > 0]
            self.attention(es, 8, q_tiles, s_mm, v_ap, 128 ** -0.5, 1024)
            P.barrier()

    def even_conv(self, hT, t_hT):
        P, d, c, ct = self.P, self.d, self.c, self.ct
        W = 15 + LC + 15 + L + 15
        boff = lambda c0: 15 + c0 if c0 < LC else 30 + c0
        with contextlib.ExitStack() as es:
            wr = self.ring(es, "wconv", 3, [128, 16 * 512], BF16)
            up = self.ring(es, "upad", 2, [128, W], BF16)
            for u, t_u in up.items:
                P.memset("pool", u[:], 0.0, writes=[t_u])
            dgr = self.ring(es, "dg", 2, [128, 31 * 128], BF16)
            pa = self.ring(es, "pa", 2, [128, 512], F32, psum=True)
            pg = self.ring(es, "pg", 2, [128, 512], F32, psum=True)
            pcv = self.ring(es, "pcv", 2, [128, 512], F32, psum=True)
            sgr = self.ring(es, "sg", 2, [128, 512], F32)
            cst = self.ring(es, "cst", 3, [128, 512], F32)
            self._pc2 = self.ps(es, "pc2b", [128, 128], F32)
            self._t_pc2 = Tok()
            gen = self.mods_gen(es, 1, mode="dma")
            gen_live = True
            for half in range(2):
                wa, t_wa = wr.next()
                self.load_w(wa, t_wa, d["e_w_in"], half * 512, 512)
                wg, t_wg = wr.next()
                self.load_w(wg, t_wg, d["e_w_in"], 1024 + half * 512, 512)
                for cc in range(4):
                    ch = half * 4 + cc
                    u, t_u = up.next()
                    dg, t_dg = dgr.next()
                    wcol = lambda k: (self.col("dww", k * 8 + ch) if k * 8 + ch < 128 else self.col("dww2", k * 8 + ch - 128))
                    for k in range(31):
                        P.ts("pool" if k % 2 else "dve", dg[:, k * 128:(k + 1) * 128], c["ident_f"][:], wcol(k), None, ALU.mult,
                             reads=[ct["ident_f"], self.t_cols], writes=[t_dg])
                    for (c0, n) in COLT:
                        za, t_za = pa.next()
                        zg, t_zg = pg.next()
                        for k in range(16):
                            P.mm(za[:, :n], wa[:, k * 512 + cc * 128:k * 512 + (cc + 1) * 128], hT[:, k * T + c0:k * T + c0 + n],
                                 k == 0, k == 15, reads=[t_wa] + self.hT_toks(t_hT, c0, n), writes=[t_za])
                        for k in range(16):
                            P.mm(zg[:, :n], wg[:, k * 512 + cc * 128:k * 512 + (cc + 1) * 128], hT[:, k * T + c0:k * T + c0 + n],
                                 k == 0, k == 15, reads=[t_wg] + self.hT_toks(t_hT, c0, n), writes=[t_zg])
                        sg, t_sg = sgr.next()
                        P.act(sg[:, :n], zg[:, :n], AF.Sigmoid, reads=[t_zg], writes=[t_sg])
                        P.tt("dve", u[:, boff(c0):boff(c0) + n], za[:, :n], sg[:, :n], ALU.mult, reads=[t_za, t_sg], writes=[t_u])
                        if gen_live:
                            gen_live = next(gen, "end") != "end"
                    for (c0, n) in COLT:
                        p0 = boff(c0) - 15
                        cv, t_cv = pcv.next()
                        for k in range(31):
                            P.mm(cv[:, :n], dg[:, k * 128:(k + 1) * 128], u[:, p0 + k:p0 + k + n], k == 0, k == 30, reads=[t_dg, t_u], writes=[t_cv])
                        st, t_st = cst.next()
                        P.act(st[:, :n], cv[:, :n], AF.Identity, reads=[t_cv, self.t_cols], writes=[t_st], bias=self.col("dwb", ch))
                        P.dma("sp", self.convd[ch * 128:(ch + 1) * 128, c0:c0 + n], st[:, :n], reads=[t_st], writes=[self.t_convd[ch]])
            while gen_live:
                gen_live = next(gen, "end") != "end"
            P.barrier()
        with contextlib.ExitStack() as es:
            cvr = self.ring(es, "cv", 2, [128, 8 * 512], F32)
            cvbr = self.ring(es, "cvb", 2, [128, 8 * 512], BF16)
            sqr = self.ring(es, "csq", 2, [128, 8 * 512], BF16)
            pS = self.ring(es, "lnS", 2, [128, 512], F32, psum=True)
            pQ = self.ring(es, "lnQ", 2, [128, 512], F32, psum=True)
            mr = self.ring(es, "lnm", 2, [128, 512], F32)
            vr = self.ring(es, "lnv", 2, [128, 512], F32)
            tr_ = self.ring(es, "lnt", 3, [128, 512], F32)
            yr = self.ring(es, "lny", 3, [128, 512], BF16)
            for ci, (c0, n) in enumerate(COLT):
                cv, t_cv = cvr.next()
                P.dma("sp", cv[:].rearrange("p (c t) -> p c t", t=512)[:, :, 0:n],
                      self.convd[:, c0:c0 + n].rearrange("(c p) t -> p c t", p=128), reads=self.t_convd, writes=[t_cv])
                cvb, t_cvb = cvbr.next()
                P.dma("pool", cvb[:].rearrange("p (c t) -> p c t", t=512)[:, :, 0:n],
                      self.convd[:, c0:c0 + n].rearrange("(c p) t -> p c t", p=128), reads=self.t_convd, writes=[t_cvb])
                sq, t_sq = sqr.next()
                for ch in range(8):
                    P.act(sq[:, ch * 512:ch * 512 + n], cv[:, ch * 512:ch * 512 + n], AF.Square, reads=[t_cv], writes=[t_sq])
                S, t_S = pS.next()
                Q, t_Q = pQ.next()
                for ch in range(8):
                    P.mm(S[:, :n], c["ones_b"][:], cvb[:, ch * 512:ch * 512 + n], ch == 0, ch == 7, reads=[t_cvb, ct["ones_b"]], writes=[t_S])
                for ch in range(8):
                    P.mm(Q[:, :n], c["ones_b"][:], sq[:, ch * 512:ch * 512 + n], ch == 0, ch == 7, reads=[t_sq, ct["ones_b"]], writes=[t_Q])
                m, t_m = mr.next()
                P.act(m[:, :n], S[:, :n], AF.Identity, reads=[t_S], writes=[t_m], scale=1.0 / 1024)
                v, t_v = vr.next()
                P.tt("pool", v[:, :n], m[:, :n], m[:, :n], ALU.mult, reads=[t_m], writes=[t_v])
                P.stt(v[:, :n], Q[:, :n], 1.0 / 1024, v[:, :n], ALU.mult, ALU.subtract, reads=[t_Q, t_v], writes=[t_v])
                P.act(v[:, :n], v[:, :n], AF.Sqrt, reads=[t_v], writes=[t_v], bias=LN_EPS)
                P.recip(v[:, :n], v[:, :n], reads=[t_v], writes=[t_v])
                for ch in range(8):
                    t1, t_t1 = tr_.next()
                    P.tt("dve", t1[:, :n], cv[:, ch * 512:ch * 512 + n], m[:, :n], ALU.subtract, reads=[t_cv, t_m], writes=[t_t1])
                    P.tt("pool", t1[:, :n], t1[:, :n], v[:, :n], ALU.mult, reads=[t_t1, t_v], writes=[t_t1])
                    y, t_y = yr.next()
                    P.act(y[:, :n], t1[:, :n], AF.Silu, reads=[t_t1, self.t_cols], writes=[t_y], scale=self.col("lng", ch), bias=self.col("lnb", ch))
                    P.dma("sp" if ch % 2 else "act", self.mixd[ch * 128:(ch + 1) * 128, c0:c0 + n], y[:, :n], reads=[t_y], writes=[self.t_mixd[ch][ci]])
            P.barrier()

    def out_proj(self, l, w_out, tiles, with_mods=None):
        P, d = self.P, self.d
        with contextlib.ExitStack() as es:
            mixT = self.sb(es, "mixT", [128, 16 * T], BF16)
            t_mixc = [Tok() for _ in COLT]
            qi = 0
            for ci, (c0, n) in enumerate(COLT):
                if c0 < LC and 0 not in tiles:
                    continue
                P.dma("sp" if qi % 2 == 0 else "act", mixT[:, :].rearrange("p (k t) -> p k t", t=T)[:, :, c0:c0 + n],
                      self.mixd[:, c0:c0 + n].rearrange("(k p) t -> p k t", p=128), reads=[self.t_mixd[k][ci] for k in range(16)], writes=[t_mixc[ci]])
                qi += 1
            t_mix_of = lambda i: t_mixc[0 if i < 2 else 1 + (i - 2) // 4]
            gbc = self.sb(es, "gbc", [128, 2 * D], F32)
            t_gbc = Tok()
            for s in range(2):
                P.dma("sp", gbc[:, s * D:(s + 1) * D], self.modd[l, s, 2 * D:3 * D].partition_broadcast(128), writes=[t_gbc])
            wr = self.ring(es, "wout", 2, [128, 16 * 512], BF16)
            po = self.ring(es, "po", 4, [128, 512], F32, psum=True)
            xo = self.ring(es, "xo", 4, [128, 512], F32)
            tm = self.ring(es, "otm", 3, [128, 512], F32)
            wbs = {}

            def issue(ns_):
                wb_, t_wb_ = wr.next()
                self.load_w(wb_, t_wb_, w_out, ns_ * 512, 512)
                wbs[ns_] = (wb_, t_wb_)

            gen = None
            if with_mods is not None:
                self._pc2 = self.ps(es, "pc2b", [128, 128], F32)
                self._t_pc2 = Tok()
                gen = self.mods_gen(es, with_mods, mode="hw", SW=256)
            issue(0)
            for ns in range(4):
                if ns + 1 < 4:
                    issue(ns + 1)
                wb, t_wb = wbs[ns]
                for i in tiles:
                    if gen is not None and next(gen, "end") == "end":
                        gen = None
                    s = 1 if i < 2 else 0
                    o, t_o = po.next()
                    for k in range(16):
                        P.mm(o[:], mixT[:, k * T + i * 128:k * T + (i + 1) * 128], wb[:, k * 512:(k + 1) * 512], k == 0, k == 15,
                             reads=[t_mix_of(i), t_wb], writes=[t_o])
                    src, toks = self.src_rows(l, i)
                    x, t_x = xo.next()
                    P.dma("sp", x[:], src[:, ns * 512:(ns + 1) * 512], reads=([toks[ns]] if toks else []), writes=[t_x])
                    tmp, t_tmp = tm.next()
                    P.tt("dve", tmp[:], o[:], gbc[:, s * D + ns * 512:s * D + (ns + 1) * 512], ALU.mult, reads=[t_o, t_gbc], writes=[t_tmp])
                    P.tt("pool", x[:], x[:], tmp[:], ALU.add, reads=[t_x, t_tmp], writes=[t_x])
                    P.dma("act", self.xres[i * 128:(i + 1) * 128, ns * 512:(ns + 1) * 512], x[:], reads=[t_x], writes=[self.t_xres[i][ns]])
            while gen is not None:
                if next(gen, "end") == "end":
                    gen = None
            if ("xmid%d" % l) in self.taps:
                tp = self.tap("tap_xmid%d" % l, [T, D])
                for i in tiles:
                    P.dma("sp", tp[i * 128:(i + 1) * 128, :], self.xres[i * 128:(i + 1) * 128, :], reads=self.t_xres[i])
            P.barrier()

    def ffn(self, l, blocks):
        P, d = self.P, self.d
        BW = 768
        wg_d, wu_d, wd_d = d["ffn_w_gate"][l], d["ffn_w_up"][l], d["ffn_w_down"][l]
        with contextlib.ExitStack() as es:
            h2T = self.sb(es, "h2T", [128, 16 * BW], BF16)
            actT = self.sb(es, "actT", [128, NHC * BW], BF16)
            gbc = self.sb(es, "gfbc", [128, 2 * D], F32)
            t_gbc = Tok()
            for s in range(2):
                P.dma("sp", gbc[:, s * D:(s + 1) * D], self.modd[l, s, 5 * D:6 * D].partition_broadcast(128), writes=[t_gbc])
            xr = self.ring(es, "fx", 2, [128, D], F32)
            jr = self.ring(es, "fjk", 1, [128, D], BF16)
            ssr = self.ring(es, "fss", 4, [128, 4], F32)
            xnr = self.ring(es, "fxn", 1, [128, D], BF16)
            banks = [(self.ps(es, "fbank", [128, 512], F32), Tok()) for _ in range(8)]
            ptr = Ring(banks[0:2])
            wgr = self.ring(es, "wg", 2, [128, 16 * 256], BF16)
            wur = self.ring(es, "wu", 2, [128, 16 * 256], BF16)
            wdr = self.ring(es, "wd", 2, [128, 11 * 512], BF16)
            pG = Ring(banks[0:2])
            pU = Ring(banks[2:4])
            pD = banks[2:8]
            sgr = self.ring(es, "fsg", 2, [128, 512], F32)
            xo = self.ring(es, "fxo", 2, [128, 512], F32)
            tm = self.ring(es, "ftm", 2, [128, 512], F32)
            for blk in blocks:
                nb = len(blk)
                t_h2 = [(Tok(), Tok()) for _ in range(nb)]
                t_act = [[Tok() for _ in range(nb)] for _ in range(NHC)]
                ctiles = []
                c0 = 0
                while c0 < nb * 128:
                    n = min(512, nb * 128 - c0)
                    ctiles.append((c0, n))
                    c0 += n
                for j, i in enumerate(blk):
                    s = 1 if i < 2 else 0
                    xt, t_xt = xr.next()
                    P.dma("sp", xt[:], self.xres[i * 128:(i + 1) * 128, :], reads=self.t_xres[i], writes=[t_xt])
                    jk, t_jk = jr.next()
                    ss, t_ss = ssr.next()
                    P.act(jk[:], xt[:], AF.Square, reads=[t_xt], writes=[t_jk])
                    P.rsum(ss[:, 0:1], jk[:], reads=[t_jk], writes=[t_ss])
                    P.act(ss[:, 1:2], ss[:, 0:1], AF.Sqrt, reads=[t_ss], writes=[t_ss], scale=1.0 / D, bias=NORM_EPS)
                    P.recip(ss[:, 2:3], ss[:, 1:2], reads=[t_ss], writes=[t_ss])
                    xn, t_xn = xnr.next()
                    P.ts("dve", xn[:], xt[:], ss[:, 2:3], None, ALU.mult, reads=[t_xt, t_ss], writes=[t_xn])
                    for kq in range(4):
                        pt, t_pt = ptr.next()
                        ptb = pt[:, 0:256].bitcast(BF16)
                        for kk in range(4):
                            k = kq * 4 + kk
                            P.tr(ptb[:, kk * 128:(kk + 1) * 128], xn[:, k * 128:(k + 1) * 128], self.c["ident_b"][:],
                                 reads=[t_xn, self.ct["ident_b"]], writes=[t_pt])
                        for kk in range(4):
                            k = kq * 4 + kk
                            o = h2T[:, k * BW + j * 128:k * BW + (j + 1) * 128]
                            a, b = self.a_col(l, s, 1, k), self.b_col(l, s, 1, k)
                            if kq % 2 == 0:
                                P.act(o, ptb[:, kk * 128:(kk + 1) * 128], AF.Identity, reads=[t_pt, self.t_acols, self.t_modT[l][s]],
                                      writes=[t_h2[j][0]], scale=a, bias=b)
                            else:
                                P.ts("dve", o, ptb[:, kk * 128:(kk + 1) * 128], a, b, ALU.mult, ALU.add,
                                     reads=[t_pt, self.t_acols, self.t_modT[l][s]], writes=[t_h2[j][1]])
                for jp in range(NHC // 2):
                    wg, t_wg = wgr.next()
                    self.load_w(wg, t_wg, wg_d, jp * 256, 256)
                    wu, t_wu = wur.next()
                    self.load_w(wu, t_wu, wu_d, jp * 256, 256)
                    for jj in range(2):
                        hc = jp * 2 + jj
                        for (c0, n) in ctiles:
                            ht = [t for pr in t_h2[c0 // 128:(c0 + n) // 128] for t in pr]
                            G, t_G = pG.next()
                            U, t_U = pU.next()
                            for k in range(16):
                                P.mm(G[:, :n], wg[:, k * 256 + jj * 128:k * 256 + (jj + 1) * 128], h2T[:, k * BW + c0:k * BW + c0 + n],
                                     k == 0, k == 15, reads=[t_wg] + ht, writes=[t_G])
                            for k in range(16):
                                P.mm(U[:, :n], wu[:, k * 256 + jj * 128:k * 256 + (jj + 1) * 128], h2T[:, k * BW + c0:k * BW + c0 + n],
                                     k == 0, k == 15, reads=[t_wu] + ht, writes=[t_U])
                            sg, t_sg = sgr.next()
                            P.act(sg[:, :n], G[:, :n], AF.Silu, reads=[t_G], writes=[t_sg])
                            P.tt("dve", actT[:, hc * BW + c0:hc * BW + c0 + n], U[:, :n], sg[:, :n], ALU.mult, reads=[t_U, t_sg],
                                 writes=t_act[hc][c0 // 128:(c0 + n) // 128])
                for ns in range(4):
                    for pc in range(4):
                        wd, t_wd = wdr.next()
                        P.dma("pool", wd[:].rearrange("p (j c) -> p j c", c=512),
                              wd_d[pc * 11 * 128:(pc + 1) * 11 * 128, ns * 512:(ns + 1) * 512].rearrange("(j p) c -> p j c", p=128),
                              writes=[t_wd])
                        for j, i in enumerate(blk):
                            o, t_o = pD[j]
                            for jj in range(11):
                                hc = pc * 11 + jj
                                P.mm(o[:], actT[:, hc * BW + j * 128:hc * BW + (j + 1) * 128], wd[:, jj * 512:(jj + 1) * 512],
                                     hc == 0, hc == NHC - 1, reads=[t_act[hc][j], t_wd], writes=[t_o])
                    for j, i in enumerate(blk):
                        s = 1 if i < 2 else 0
                        o, t_o = pD[j]
                        x, t_x = xo.next()
                        P.dma("sp", x[:], self.xres[i * 128:(i + 1) * 128, ns * 512:(ns + 1) * 512], reads=[self.t_xres[i][ns]], writes=[t_x])
                        tmp, t_tmp = tm.next()
                        P.tt("dve", tmp[:], o[:], gbc[:, s * D + ns * 512:s * D + (ns + 1) * 512], ALU.mult, reads=[t_o, t_gbc], writes=[t_tmp])
                        P.tt("pool", x[:], x[:], tmp[:], ALU.add, reads=[t_x, t_tmp], writes=[t_x])
                        P.dma("act", self.xres[i * 128:(i + 1) * 128, ns * 512:(ns + 1) * 512], x[:], reads=[t_x], writes=[self.t_xres[i][ns]])
            if ("x%d" % l) in self.taps:
                tp = self.tap("tap_x%d" % l, [T, D])
                for blk in blocks:
                    for i in blk:
                        P.dma("sp", tp[i * 128:(i + 1) * 128, :], self.xres[i * 128:(i + 1) * 128, :], reads=self.t_xres[i])
            P.barrier()

    def final(self):
        P, d = self.P, self.d
        with contextlib.ExitStack() as es:
            gb = self.sb(es, "gfin", [128, D], F32)
            t_gb = Tok()
            P.dma("sp", gb[:], d["final_norm_g"][0, :].partition_broadcast(128), writes=[t_gb])
            xr = self.ring(es, "lx", 3, [128, D], F32)
            jr = self.ring(es, "ljk", 2, [128, D], F32)
            ssr = self.ring(es, "lss", 4, [128, 4], F32)
            for i in range(2, NT):
                xt, t_xt = xr.next()
                P.dma("sp", xt[:], self.xres[i * 128:(i + 1) * 128, :], reads=self.t_xres[i], writes=[t_xt])
                jk, t_jk = jr.next()
                ss, t_ss = ssr.next()
                P.act(jk[:], xt[:], AF.Square, reads=[t_xt], writes=[t_jk])
                P.rsum(ss[:, 0:1], jk[:], reads=[t_jk], writes=[t_ss])
                P.act(ss[:, 1:2], ss[:, 0:1], AF.Sqrt, reads=[t_ss], writes=[t_ss], scale=1.0 / D, bias=NORM_EPS)
                P.recip(ss[:, 2:3], ss[:, 1:2], reads=[t_ss], writes=[t_ss])
                P.stt(xt[:], xt[:], ss[:, 2:3], gb[:], ALU.mult, ALU.mult, reads=[t_xt, t_ss, t_gb], writes=[t_xt])
                P.dma("act", d["out"][(i - 2) * 128:(i - 1) * 128, :], xt[:], reads=[t_xt])
            P.barrier()


    def odd_hyena_proj(self, hT, t_hT):
        P, d, c, ct = self.P, self.d, self.c, self.ct
        LT = COLT[1:]
        with contextlib.ExitStack() as es:
            wr = self.ring(es, "whp", 2, [128, 16 * 512], BF16)
            zr = self.ring(es, "zraw", 2, [128, L + 2], F32)
            for z, t_z in zr.items:
                P.memset("pool", z[:], 0.0, writes=[t_z])
            zcr = self.ring(es, "zc", 8, [128, L], F32)
            pz = self.ring(es, "hpz", 3, [128, 512], F32, psum=True)
            ptr = self.ring(es, "hpt", 2, [128, 512], F32, psum=True)
            stg = self.ring(es, "hstg", 3, [128, 512], F32)
            for g in range(6):
                wb, t_wb = wr.next()
                self.load_w(wb, t_wb, d["o_w_in"], g * 512, 512)
                zcs = []
                for cc in range(4):
                    ch = g * 4 + cc
                    zraw, t_zraw = zr.next()
                    for (c0, n) in LT:
                        z, t_z = pz.next()
                        for k in range(16):
                            P.mm(z[:, :n], wb[:, k * 512 + cc * 128:k * 512 + (cc + 1) * 128], hT[:, k * T + c0:k * T + c0 + n],
                                 k == 0, k == 15, reads=[t_wb] + self.hT_toks(t_hT, c0, n), writes=[t_z])
                        P.cp("act", zraw[:, 1 + c0 - LC:1 + c0 - LC + n], z[:, :n], reads=[t_z], writes=[t_zraw])
                    zc, t_zc = zcr.next()
                    w = lambda k: self.col("shw", k * 24 + ch)
                    P.ts("dve", zc[:], zraw[:, 0:L], w(0), self.col("shb", ch), ALU.mult, ALU.add, reads=[t_zraw, self.t_cols], writes=[t_zc])
                    P.stt(zc[:], zraw[:, 1:L + 1], w(1), zc[:], ALU.mult, ALU.add, reads=[t_zraw, t_zc, self.t_cols], writes=[t_zc])
                    P.stt(zc[:], zraw[:, 2:L + 2], w(2), zc[:], ALU.mult, ALU.add, reads=[t_zraw, t_zc, self.t_cols], writes=[t_zc])
                    zcs.append((zc, t_zc))
                for tt in range(16):
                    pt, t_pt = ptr.next()
                    for cc in range(4):
                        zc, t_zc = zcs[cc]
                        P.tr(pt[:, cc * 128:(cc + 1) * 128], zc[:, tt * 128:(tt + 1) * 128], c["ident_f"][:], reads=[t_zc, ct["ident_f"]], writes=[t_pt])
                    st, t_st = stg.next()
                    P.cp("act" if tt % 2 == 0 else "dve", st[:], pt[:], reads=[t_pt], writes=[t_st])
                    P.dma("sp", self.zt[tt * 128:(tt + 1) * 128, g * 512:(g + 1) * 512], st[:], reads=[t_st], writes=[self.t_zt[tt][g]])
            if "zt" in self.taps:
                tp = self.tap("tap_zt", [L, 3072])
                for tt in range(16):
                    P.dma("sp", tp[tt * 128:(tt + 1) * 128, :], self.zt[tt * 128:(tt + 1) * 128, :], reads=self.t_zt[tt])
            P.barrier()

    def rope64(self, rings, src_psum, t_src, n, lc0, cosT, sinT, t_rope, out, t_out):
        P, c, ct = self.P, self.c, self.ct
        rf, ppq, t1r, t2r = rings
        f, t_f = rf.next()
        P.cp("act", f[0:64, :n], src_psum[0:64, :n], reads=[t_src], writes=[t_f])
        pq, t_pq = ppq.next()
        P.mm(pq[0:64, :n], c["perm64"][:], f[0:64, :n], True, True, reads=[t_f, ct["perm64"]], writes=[t_pq])
        t1, t_t1 = t1r.next()
        P.tt("pool", t1[0:64, :n], f[0:64, :n], cosT[:, lc0:lc0 + n], ALU.mult, reads=[t_f, t_rope], writes=[t_t1])
        t2, t_t2 = t2r.next()
        P.tt("dve", t2[0:64, :n], pq[0:64, :n], sinT[:, lc0:lc0 + n], ALU.mult, reads=[t_pq, t_rope], writes=[t_t2])
        P.tt("pool", out, t1[0:64, :n], t2[0:64, :n], ALU.add, reads=[t_t1, t_t2], writes=[t_out])

    def odd_mla(self, hT, t_hT):
        P, d, c, ct = self.P, self.d, self.c, self.ct
        LT = COLT[1:]
        with contextlib.ExitStack() as es:
            zqnT = self.sb(es, "zqnT", [128, 4 * L], BF16)
            t_zqn = [Tok() for _ in LT]
            ckvT = self.sb(es, "ckvT", [128, 2 * T], BF16)
            t_ckv = [Tok() for _ in COLT]
            krT = self.sb(es, "krT", [64, T], BF16)
            t_kr = [Tok() for _ in COLT]
            cosT = self.sb(es, "cos64", [64, L], F32)
            sinT = self.sb(es, "sin64", [64, L], F32)
            t_rope = Tok()
            P.dma("sp", cosT[:], d["cos64"][:, :], writes=[t_rope])
            P.dma("sp", sinT[:], d["sin64"][:, :], writes=[t_rope])
            rrings = (self.ring(es, "rf", 2, [64, 512], F32), self.ring(es, "rpq", 1, [128, 512], F32, psum=True),
                      self.ring(es, "rt1", 2, [64, 512], F32), self.ring(es, "rt2", 2, [64, 512], F32))
            with contextlib.ExitStack() as es2:
                wb = self.sb(es2, "wmla", [128, 16 * 832], BF16)
                t_wb = Tok()
                self.load_w(wb, t_wb, d["o_w_in"], 3072, 832)
                pz = self.ring(es2, "mpz", 4, [128, 512], F32, psum=True)
                pss = self.ring(es2, "mpss", 1, [128, 512], F32, psum=True)
                qfr = self.ring(es2, "mqf", 2, [128, 4 * 512], F32)
                q2r = self.ring(es2, "mq2", 2, [128, 4 * 512], BF16)
                rtr = self.ring(es2, "mrt", 2, [128, 512], F32)

                def normed(nch, wc0, tiles, gkey, dst, dstw, dcol, t_dst):
                    for ci, (c0, n) in tiles:
                        zs = []
                        for j in range(nch):
                            z, t_z = pz.next()
                            for k in range(16):
                                P.mm(z[:, :n], wb[:, k * 832 + wc0 + j * 128:k * 832 + wc0 + (j + 1) * 128], hT[:, k * T + c0:k * T + c0 + n],
                                     k == 0, k == 15, reads=[t_wb] + self.hT_toks(t_hT, c0, n), writes=[t_z])
                            zs.append((z, t_z))
                        qf, t_qf = qfr.next()
                        q2, t_q2 = q2r.next()
                        for j, (z, t_z) in enumerate(zs):
                            P.cp("act", qf[:, j * 512:j * 512 + n], z[:, :n], reads=[t_z], writes=[t_qf])
                            P.act(q2[:, j * 512:j * 512 + n], z[:, :n], AF.Square, reads=[t_z], writes=[t_q2])
                        ssp, t_ssp = pss.next()
                        for j in range(nch):
                            P.mm(ssp[:, :n], c["ones_b"][:], q2[:, j * 512:j * 512 + n], j == 0, j == nch - 1, reads=[t_q2, ct["ones_b"]], writes=[t_ssp])
                        rt, t_rt = rtr.next()
                        P.act(rt[:, :n], ssp[:, :n], AF.Sqrt, reads=[t_ssp], writes=[t_rt], scale=1.0 / (nch * 128), bias=NORM_EPS)
                        P.recip(rt[:, :n], rt[:, :n], reads=[t_rt], writes=[t_rt])
                        for j in range(nch):
                            P.stt(dst[:, j * dstw + dcol(c0):j * dstw + dcol(c0) + n], qf[:, j * 512:j * 512 + n], self.col(gkey, j), rt[:, :n],
                                  ALU.mult, ALU.mult, reads=[t_qf, t_rt, self.t_cols], writes=[t_dst[ci]])

                normed(4, 0, list(enumerate(LT)), "oqg", zqnT, L, lambda c0: c0 - LC, t_zqn)
                normed(2, 512, list(enumerate(COLT)), "okg", ckvT, T, lambda c0: c0, t_ckv)
                for ci, (c0, n) in enumerate(COLT):
                    z, t_z = pz.next()
                    for k in range(16):
                        P.mm(z[0:64, :n], wb[:, k * 832 + 768:k * 832 + 832], hT[:, k * T + c0:k * T + c0 + n], k == 0, k == 15,
                             reads=[t_wb] + self.hT_toks(t_hT, c0, n), writes=[t_z])
                    if c0 < LC:
                        P.cp("act", krT[:, c0:c0 + n], z[0:64, :n], reads=[t_z], writes=[t_kr[ci]])
                    else:
                        self.rope64(rrings, z, t_z, n, c0 - LC, cosT, sinT, t_rope, krT[:, c0:c0 + n], t_kr[ci])
                if "mla" in self.taps:
                    t1 = self.tap("tap_zqnT", [512, L], BF16)
                    for j in range(4):
                        P.dma("sp", t1[j * 128:(j + 1) * 128, :], zqnT[:, j * L:(j + 1) * L], reads=t_zqn)
                    t2 = self.tap("tap_ckvT", [256, T], BF16)
                    for j in range(2):
                        P.dma("sp", t2[j * 128:(j + 1) * 128, :], ckvT[:, j * T:(j + 1) * T], reads=t_ckv)
                    t3 = self.tap("tap_krT", [64, T], BF16)
                    P.dma("sp", t3[:, :], krT[:], reads=t_kr)
                P.barrier()
            with contextlib.ExitStack() as es2:
                wuq = self.sb(es2, "wuq", [128, 4 * 1536], BF16)
                t_wuq = Tok()
                self.load_w(wuq, t_wuq, d["o_w_uq"], 0, 1536, nk=4)
                wukv = self.sb(es2, "wukv", [128, 2 * 2048], BF16)
                t_wukv = Tok()
                self.load_w(wukv, t_wukv, d["o_w_ukv"], 0, 2048, nk=2)
                pu = self.ring(es2, "pS", 3, [128, 512], F32, psum=True)
                qnr = self.ring(es2, "qnh", 2, [128, L], BF16)
                qrr = self.ring(es2, "qrh", 2, [64, L], BF16)
                knr = self.ring(es2, "knh", 2, [128, T], BF16)
                vhr = self.ring(es2, "vh", 2, [128, NT * 128], BF16)
                cur = {}

                def prep(h):
                    qn, t_qn = qnr.next()
                    qr, t_qr = qrr.next()
                    kn, t_kn = knr.next()
                    vh, t_vh = vhr.next()
                    cur.update(qn=qn, t_qn=t_qn, qr=qr, t_qr=t_qr, kn=kn, t_kn=t_kn, vh=vh, t_vh=t_vh)
                    for ci, (c0, n) in enumerate(LT):
                        z, t_z = pu.next()
                        for k in range(4):
                            P.mm(z[:, :n], wuq[:, k * 1536 + h * 192:k * 1536 + h * 192 + 128], zqnT[:, k * L + c0 - LC:k * L + c0 - LC + n],
                                 k == 0, k == 3, reads=[t_wuq, t_zqn[ci]], writes=[t_z])
                        P.cp("act", qn[:, c0 - LC:c0 - LC + n], z[:, :n], reads=[t_z], writes=[t_qn])
                        z, t_z = pu.next()
                        for k in range(4):
                            P.mm(z[0:64, :n], wuq[:, k * 1536 + h * 192 + 128:k * 1536 + h * 192 + 192], zqnT[:, k * L + c0 - LC:k * L + c0 - LC + n],
                                 k == 0, k == 3, reads=[t_wuq, t_zqn[ci]], writes=[t_z])
                        self.rope64(rrings, z, t_z, n, c0 - LC, cosT, sinT, t_rope, qr[:, c0 - LC:c0 - LC + n], t_qr)
                    for ci, (c0, n) in enumerate(COLT):
                        z, t_z = pu.next()
                        for k in range(2):
                            P.mm(z[:, :n], wukv[:, k * 2048 + h * 256:k * 2048 + h * 256 + 128], ckvT[:, k * T + c0:k * T + c0 + n],
                                 k == 0, k == 1, reads=[t_wukv, t_ckv[ci]], writes=[t_z])
                        P.cp("act", kn[:, c0:c0 + n], z[:, :n], reads=[t_z], writes=[t_kn])
                    for i0 in range(0, NT, 4):
                        z, t_z = pu.next()
                        nn = min(4, NT - i0)
                        for ii in range(nn):
                            i = i0 + ii
                            kci = 0 if i < 2 else 1 + (i - 2) // 4
                            for k in range(2):
                                P.mm(z[:, ii * 128:(ii + 1) * 128], ckvT[:, k * T + i * 128:k * T + (i + 1) * 128],
                                     wukv[:, k * 2048 + h * 256 + 128:k * 2048 + h * 256 + 256], k == 0, k == 1,
                                     reads=[t_wukv, t_ckv[kci]], writes=[t_z])
                        P.cp("act", vh[:, i0 * 128:(i0 + nn) * 128], z[:, 0:nn * 128], reads=[t_z], writes=[t_vh])

                def s_mm(h, S, t_S, kc, c0, n):
                    kci = 0 if kc < 2 else 1 + (kc - 2) // 4
                    P.mm(S[:, :n], cur["kn"][:, kc * 128:(kc + 1) * 128], cur["qn"][:, c0 - LC:c0 - LC + n], True, False,
                         reads=[cur["t_kn"], cur["t_qn"]], writes=[t_S])
                    P.mm(S[:, :n], krT[:, kc * 128:(kc + 1) * 128], cur["qr"][:, c0 - LC:c0 - LC + n], False, True,
                         reads=[t_kr[kci], cur["t_qr"]], writes=[t_S])

                def v_ap(h, kc):
                    return cur["vh"][:, kc * 128:(kc + 1) * 128], [cur["t_vh"]]

                q_tiles = [(ci, c0, n, list(range(NT))) for ci, (c0, n) in enumerate(COLT) if ci > 0]
                self.attention(es2, 8, q_tiles, s_mm, v_ap, 192 ** -0.5, 1024, prep=prep, pS=pu)
                P.barrier()

    def hyena_filters(self):
        P, d, c, ct = self.P, self.d, self.c, self.ct
        PI = math.pi
        with contextlib.ExitStack() as es:
            featT = self.sb(es, "featT", [33, L], F32)
            w1 = self.sb(es, "fw1", [33, 64], F32)
            w2 = self.sb(es, "fw2", [64, 64], F32)
            w3 = self.sb(es, "fw3", [64, 64], F32)
            w4 = self.sb(es, "fw4", [64, 4096], BF16)
            tneg = self.sb(es, "tneg", [128, 16], F32)
            dbc = self.sb(es, "dbc", [128, 1024], F32)
            t_in = Tok()
            P.dma("sp", featT[:], d["featT"][:, :], writes=[t_in])
            P.dma("sp", w1[:], d["o_f_w1"][:, :], writes=[t_in])
            P.dma("sp", w2[:], d["o_f_w2"][:, :], writes=[t_in])
            P.dma("sp", w3[:], d["o_f_w3"][:, :], writes=[t_in])
            P.dma("pool", w4[:], d["o_f_w4"][:, :], writes=[t_in])
            P.dma("sp", tneg[:], d["tneg"][:, :], writes=[t_in])
            P.dma("sp", dbc[:], d["deltas"][0, :].partition_broadcast(128), writes=[t_in])
            P.tt("dve", self.fcols[:, 6:8], self.fcols[:, 0:2], self.fcols[:, 3:5], ALU.mult, reads=[self.t_fcols], writes=[self.t_fcols])
            bfr2 = self.sb(es, "bfr2", [64, 1], F32)
            t_bfr2 = Tok()
            P.tt("dve", bfr2[:], self.fcols[:, 2:3], self.fcols[:, 5:6], ALU.mult, reads=[self.t_fcols], writes=[t_bfr2])
            hh = [self.sb(es, "hh", [64, L], F32) for _ in range(2)]
            t_hh = [Tok(), Tok()]
            pf = self.ring(es, "pf", 3, [128, 512], F32, psum=True)
            argr = self.ring(es, "farg", 2, [64, 512], F32)
            m1r = self.ring(es, "fm1", 2, [64, 512], F32)
            m2r = self.ring(es, "fm2", 2, [64, 512], F32)
            layers = [(w1, 33, featT, t_in), (w2, 64, hh[0], t_hh[0]), (w3, 64, hh[1], t_hh[1])]
            for i, (w, kdim, src, t_src) in enumerate(layers):
                dst, t_dst = hh[i % 2], t_hh[i % 2]
                bcol = self.fcols[:, 6 + i:7 + i] if i < 2 else bfr2[:, 0:1]
                for q in range(4):
                    p, t_p = pf.next()
                    P.mm(p[0:64, :], w[0:kdim, :], src[0:kdim, q * 512:(q + 1) * 512], True, True, reads=[t_in, t_src], writes=[t_p])
                    a, t_a = argr.next()
                    P.ts("dve", a[:], p[0:64, :], self.fcols[:, 3 + i:4 + i], bcol, ALU.mult, ALU.add, reads=[t_p, self.t_fcols, t_bfr2], writes=[t_a])
                    m1, t_m1 = m1r.next()
                    P.ts("dve", m1[:], a[:], PI, -2.0 * PI, ALU.is_gt, ALU.mult, reads=[t_a], writes=[t_m1])
                    m2, t_m2 = m2r.next()
                    P.ts("dve", m2[:], a[:], -PI, 2.0 * PI, ALU.is_lt, ALU.mult, reads=[t_a], writes=[t_m2])
                    P.tt("pool", m1[:], m1[:], m2[:], ALU.add, reads=[t_m1, t_m2], writes=[t_m1])
                    P.tt("pool", a[:], a[:], m1[:], ALU.add, reads=[t_a, t_m1], writes=[t_a])
                    P.act(dst[:, q * 512:(q + 1) * 512], a[:], AF.Sin, reads=[t_a], writes=[t_dst])
            h3f, t_h3f = hh[0], t_hh[0]
            h3 = self.sb(es, "hh3b", [64, L], BF16)
            t_h3 = Tok()
            P.cp("act", h3[:], h3f[:], reads=[t_h3f], writes=[t_h3])
            if "hh3" in self.taps:
                tp = self.tap("tap_hh3", [64, L])
                P.dma("sp", tp[:, :], h3f[:], reads=[t_h3f])
            hd = [self.sb(es, "hd", [128, 16 * 512], F32) for _ in range(2)]
            t_hd = [[Tok() for _ in range(16)] for _ in range(2)]
            A = self.sb(es, "fA", [128, 16 * 512], BF16)
            B = self.sb(es, "fB", [128, 16 * 512], BF16)
            t_A, t_B = [Tok() for _ in range(16)], [Tok() for _ in range(16)]
            decr = self.ring(es, "dec", 2, [128, 512], F32)
            abr = self.ring(es, "fab", 3, [128, 512], BF16)
            pn = self.ring(es, "pn", 1, [128, 512], F32, psum=True)
            rnr = self.ring(es, "frn", 2, [128, 512], F32)
            cfr = self.ring(es, "fcf", 3, [128, 2048], BF16)
            sfr = self.ring(es, "fsf", 3, [128, 2048], BF16)
            pk = self.ring(es, "pk", 2, [128, 512], F32, psum=True)
            pki = self.ring(es, "pki", 2, [128, 512], F32, psum=True)
            kst = self.ring(es, "kst", 2, [128, 512], F32)
            ksti = self.ring(es, "ksti", 2, [128, 512], F32)
            for n in range(2):
                for s in range(2):
                    rns = []
                    for dirn in range(2):
                        col0 = dirn * 2048 + n * 1024 + s * 512
                        H, t_H = hd[dirn], t_hd[dirn]
                        nrm, t_nrm = pn.next()

                        def gen_hraw(jt_):
                            p_, t_p_ = pf.next()
                            P.mm(p_[:], h3[0:64, jt_ * 128:(jt_ + 1) * 128], w4[0:64, col0:col0 + 512], True, True, reads=[t_h3, t_in], writes=[t_p_])
                            return p_, t_p_
                        pend = [gen_hraw(0)]
                        for jt in range(16):
                            if jt + 1 < 16:
                                pend.append(gen_hraw(jt + 1))
                            p, t_p = pend.pop(0)
                            dec, t_dec = decr.next()
                            P.act(dec[:], dbc[:, s * 512:(s + 1) * 512], AF.Exp, reads=[t_in], writes=[t_dec], scale=tneg[:, jt:jt + 1])
                            P.tt("dve", H[:, jt * 512:(jt + 1) * 512], p[:], dec[:], ALU.mult, reads=[t_p, t_dec], writes=[t_H[jt]])
                            ab, t_ab = abr.next()
                            P.act(ab[:], H[:, jt * 512:(jt + 1) * 512], AF.Abs, reads=[t_H[jt]], writes=[t_ab])
                            P.mm(nrm[:], c["ones_b"][:], ab[:], jt == 0, jt == 15, reads=[t_ab, ct["ones_b"]], writes=[t_nrm])
                        rn, t_rn = rnr.next()
                        P.ts("dve", rn[:], nrm[:], 1e-6, None, ALU.add, reads=[t_nrm], writes=[t_rn])
                        P.recip(rn[:], rn[:], reads=[t_rn], writes=[t_rn])
                        rns.append((rn, t_rn))
                    H0, H1 = hd[0], hd[1]
                    P.memset("dve", H1[0:1, 0:512], 0.0, writes=[t_hd[1][0]])
                    for jt in range(16):
                        sl = slice(jt * 512, (jt + 1) * 512)
                        P.tt("dve", H0[:, sl], H0[:, sl], rns[0][0][:], ALU.mult, reads=[t_hd[0][jt], rns[0][1]], writes=[t_hd[0][jt]])
                        P.tt("dve", H1[:, sl], H1[:, sl], rns[1][0][:], ALU.mult, reads=[t_hd[1][jt], rns[1][1]], writes=[t_hd[1][jt]])
                        P.tt("dve", A[:, sl], H0[:, sl], H1[:, sl], ALU.add, reads=[t_hd[0][jt], t_hd[1][jt]], writes=[t_A[jt]])
                        P.tt("pool", B[:, sl], H1[:, sl], H0[:, sl], ALU.subtract, reads=[t_hd[0][jt], t_hd[1][jt]], writes=[t_B[jt]])
                    if "hfilt" in self.taps and n == 0 and s == 0:
                        tp = self.tap("tap_hd0", [128, 16 * 512])
                        P.dma("sp", tp[:, :], hd[0][:], reads=t_hd[0])
                        tp = self.tap("tap_hd1", [128, 16 * 512])
                        P.dma("sp", tp[:, :], hd[1][:], reads=t_hd[1])
                    for fc in range(16):
                        cf, t_cf = cfr.next()
                        P.dma("sp", cf[:], d["dft_cf"][fc, :, :], writes=[t_cf])
                        sf, t_sf = sfr.next()
                        P.dma("sp", sf[:], d["dft_sf"][fc, :, :], writes=[t_sf])
                        kr, t_kr = pk.next()
                        ki, t_ki = pki.next()
                        for jt in range(16):
                            P.mm(kr[:], cf[:, jt * 128:(jt + 1) * 128], A[:, jt * 512:(jt + 1) * 512], jt == 0, jt == 15, reads=[t_cf, t_A[jt]], writes=[t_kr])
                        for jt in range(16):
                            P.mm(ki[:], sf[:, jt * 128:(jt + 1) * 128], B[:, jt * 512:(jt + 1) * 512], jt == 0, jt == 15, reads=[t_sf, t_B[jt]], writes=[t_ki])
                        st, t_st = kst.next()
                        P.cp("act", st[:], kr[:], reads=[t_kr], writes=[t_st])
                        P.dma("act", self.kfd[n, 0, fc * 128:(fc + 1) * 128, s * 512:(s + 1) * 512], st[:], reads=[t_st], writes=[self.t_kfd[n][s][fc]])
                        st2, t_st2 = ksti.next()
                        P.cp("dve", st2[:], ki[:], reads=[t_ki], writes=[t_st2])
                        P.dma("pool", self.kfd[n, 1, fc * 128:(fc + 1) * 128, s * 512:(s + 1) * 512], st2[:], reads=[t_st2], writes=[self.t_kfd[n][s][fc]])
            P.barrier()

    def hyena_conv(self):
        P, d, c, ct = self.P, self.d, self.c, self.ct
        with contextlib.ExitStack() as es:
            ya = self.sb(es, "ya", [128, 16 * 1024], BF16)
            yb = self.sb(es, "yb", [128, 16 * 1024], BF16)
            t_y = {id(ya): [Tok() for _ in range(16)], id(yb): [Tok() for _ in range(16)]}
            skb = self.sb(es, "skb", [128, 2048], F32)
            t_skb = Tok()
            P.dma("sp", skb[:], d["o_skip"].rearrange("a c -> (a c)").partition_broadcast(128), writes=[t_skb])
            for tt in range(16):
                P.dma("pool", ya[:, tt * 1024:(tt + 1) * 1024], self.zt[tt * 128:(tt + 1) * 128, 2048:3072],
                      reads=self.t_zt[tt][4:6], writes=[t_y[id(ya)][tt]])
            Yre = self.sb(es, "Yre", [128, 16 * 512], BF16)
            Yim = self.sb(es, "Yim", [128, 16 * 512], BF16)
            t_Yre = [Tok() for _ in range(16)]
            t_Yim = [Tok() for _ in range(16)]
            tw = self.ring(es, "tw", 6, [128, 2048], BF16)
            kre_r = self.ring(es, "kre", 2, [128, 512], F32)
            kim_r = self.ring(es, "kim", 2, [128, 512], F32)
            pU = self.ring(es, "pUr", 2, [128, 512], F32, psum=True)
            pV = self.ring(es, "pUs", 2, [128, 512], F32, psum=True)
            pY = self.ring(es, "pY", 2, [128, 512], F32, psum=True)
            tmp = {k: self.ring(es, "hc" + k, 2, [128, 512], F32) for k in ("t1", "t2", "t3", "t4", "ts", "r")}
            xgr = self.ring(es, "xg", 2, [128, 512], F32)
            for n in range(2):
                yin, yout = (ya, yb) if n == 0 else (yb, ya)
                t_in, t_out = t_y[id(yin)], t_y[id(yout)]
                for s in range(2):
                    for fc in range(16):
                        cf, t_cf = tw.next()
                        P.dma("sp", cf[:], d["dft_cf"][fc, :, :], writes=[t_cf])
                        sf, t_sf = tw.next()
                        P.dma("sp", sf[:], d["dft_sf"][fc, :, :], writes=[t_sf])
                        kre, t_kre = kre_r.next()
                        P.dma("sp", kre[:], self.kfd[n, 0, fc * 128:(fc + 1) * 128, s * 512:(s + 1) * 512], reads=[self.t_kfd[n][s][fc]], writes=[t_kre])
                        kim, t_kim = kim_r.next()
                        P.dma("sp", kim[:], self.kfd[n, 1, fc * 128:(fc + 1) * 128, s * 512:(s + 1) * 512], reads=[self.t_kfd[n][s][fc]], writes=[t_kim])
                        U, t_U = pU.next()
                        V, t_V = pV.next()
                        for tt in range(16):
                            P.mm(U[:], cf[:, tt * 128:(tt + 1) * 128], yin[:, tt * 1024 + s * 512:tt * 1024 + (s + 1) * 512], tt == 0, tt == 15,
                                 reads=[t_cf, t_in[tt]], writes=[t_U])
                        for tt in range(16):
                            P.mm(V[:], sf[:, tt * 128:(tt + 1) * 128], yin[:, tt * 1024 + s * 512:tt * 1024 + (s + 1) * 512], tt == 0, tt == 15,
                                 reads=[t_sf, t_in[tt]], writes=[t_V])
                        t1, t_t1 = tmp["t1"].next()
                        P.tt("dve", t1[:], U[:], kre[:], ALU.mult, reads=[t_U, t_kre], writes=[t_t1])
                        t2, t_t2 = tmp["t2"].next()
                        P.tt("dve", t2[:], V[:], kim[:], ALU.mult, reads=[t_V, t_kim], writes=[t_t2])
                        P.tt("pool", Yre[:, fc * 512:(fc + 1) * 512], t1[:], t2[:], ALU.add, reads=[t_t1, t_t2], writes=[t_Yre[fc]])
                        t3, t_t3 = tmp["t3"].next()
                        P.tt("dve", t3[:], U[:], kim[:], ALU.mult, reads=[t_U, t_kim], writes=[t_t3])
                        t4, t_t4 = tmp["t4"].next()
                        P.tt("dve", t4[:], V[:], kre[:], ALU.mult, reads=[t_V, t_kre], writes=[t_t4])
                        P.tt("pool", Yim[:, fc * 512:(fc + 1) * 512], t3[:], t4[:], ALU.subtract, reads=[t_t3, t_t4], writes=[t_Yim[fc]])
                    for tt in range(16):
                        ci_, t_ci = tw.next()
                        P.dma("sp", ci_[:], d["dft_ci"][tt, :, :], writes=[t_ci])
                        si_, t_si = tw.next()
                        P.dma("sp", si_[:], d["dft_si"][tt, :, :], writes=[t_si])
                        Y, t_Y = pY.next()
                        for fc in range(16):
                            P.mm(Y[:], ci_[:, fc * 128:(fc + 1) * 128], Yre[:, fc * 512:(fc + 1) * 512], fc == 0, False, reads=[t_ci, t_Yre[fc]], writes=[t_Y])
                        for fc in range(16):
                            P.mm(Y[:], si_[:, fc * 128:(fc + 1) * 128], Yim[:, fc * 512:(fc + 1) * 512], False, fc == 15, reads=[t_si, t_Yim[fc]], writes=[t_Y])
                        xg, t_xg = xgr.next()
                        P.dma("sp", xg[:], self.zt[tt * 128:(tt + 1) * 128, n * 1024 + s * 512:n * 1024 + (s + 1) * 512],
                              reads=[self.t_zt[tt][n * 2 + s]], writes=[t_xg])
                        tsk, t_tsk = tmp["ts"].next()
                        P.tt("pool", tsk[:], yin[:, tt * 1024 + s * 512:tt * 1024 + (s + 1) * 512], skb[:, n * 1024 + s * 512:n * 1024 + (s + 1) * 512],
                             ALU.mult, reads=[t_in[tt], t_skb], writes=[t_tsk])
                        r, t_r = tmp["r"].next()
                        P.stt(r[:], Y[:], 1.0 / 2048, tsk[:], ALU.mult, ALU.add, reads=[t_Y, t_tsk], writes=[t_r])
                        P.tt("pool", yout[:, tt * 1024 + s * 512:tt * 1024 + (s + 1) * 512], r[:], xg[:], ALU.mult, reads=[t_r, t_xg], writes=[t_out[tt]])
            yfin, t_fin = ya, t_y[id(ya)]
            if "yh" in self.taps:
                tp = self.tap("tap_yh", [L, 1024], BF16)
                for tt in range(16):
                    P.dma("sp", tp[tt * 128:(tt + 1) * 128, :], yfin[:, tt * 1024:(tt + 1) * 1024], reads=[t_fin[tt]])
            ptr = self.ring(es, "ypt", 1, [128, 512], F32, psum=True)
            stg = self.ring(es, "ystg", 2, [128, L], BF16)
            for ch in range(8):
                st, t_st = stg.next()
                for q in range(4):
                    pt, t_pt = ptr.next()
                    ptb = pt[:, 0:256].bitcast(BF16)
                    for kk in range(4):
                        tt = q * 4 + kk
                        P.tr(ptb[:, kk * 128:(kk + 1) * 128], yfin[:, tt * 1024 + ch * 128:tt * 1024 + (ch + 1) * 128], c["ident_b"][:],
                             reads=[t_fin[tt], ct["ident_b"]], writes=[t_pt])
                    P.cp("act", st[:, q * 512:(q + 1) * 512], ptb[:, :], reads=[t_pt], writes=[t_st])
                for ci in range(1, 5):
                    c0, nn = COLT[ci]
                    P.dma("sp", self.mixd[ch * 128:(ch + 1) * 128, c0:c0 + nn], st[:, c0 - LC:c0 - LC + nn], reads=[t_st], writes=[self.t_mixd[ch][ci]])
            P.barrier()

    def layer_odd(self, l):
        P, d = self.P, self.d
        with contextlib.ExitStack() as es:
            hT = self.sb(es, "hT", [128, 16 * T], BF16)
            t_hT = [(Tok(), Tok()) for _ in range(NT)]
            self.phaseA(l, hT, t_hT)
            if self.stop == "A1":
                return False
            self.odd_hyena_proj(hT, t_hT)
            if self.stop == "zt":
                return False
            self.odd_mla(hT, t_hT)
        if self.stop == "mla":
            return False
        self.hyena_filters()
        if self.stop == "filt":
            return False
        self.hyena_conv()
        if self.stop == "hconv":
            return False
        lat = list(range(2, NT))
        self.out_proj(l, d["o_w_out"], lat)
        if self.stop == "xmid1":
            return False
        self.ffn(l, [lat[0:6], lat[6:12], lat[12:16]])
        return self.stop != "x1"

    def layer_even(self, l):
        P, d = self.P, self.d
        with contextlib.ExitStack() as es:
            hT = self.sb(es, "hT", [128, 16 * T], BF16)
            import os
            if os.environ.get("EVAC_SINGLE"):
                t_hT = [(lambda t: (t, t))(Tok()) for _ in range(NT)]
            else:
                t_hT = [(Tok(), Tok()) for _ in range(NT)]
            self.phaseA(l, hT, t_hT)
            if self.stop == "A0":
                return False
            self.even_conv(hT, t_hT)
            if self.stop == "conv0":
                return False
            with contextlib.ExitStack() as es2:
                qT = self.sb(es2, "qT", [128, 8 * T], BF16)
                t_qT = [[Tok() for _ in COLT] for _ in range(8)]
                kT = self.sb(es2, "kT", [128, 2 * T], BF16)
                t_kT = [[Tok() for _ in COLT] for _ in range(2)]
                Vsb = self.sb(es2, "Vsb", [128, NT * 256], BF16)
                t_V = [Tok() for _ in range(NT)]
                self.even_qkv(hT, t_hT, qT, t_qT, kT, t_kT, Vsb, t_V)
                if self.stop == "qkv0":
                    return False
                self.even_attention(qT, t_qT, kT, t_kT, Vsb, t_V)
        if self.stop == "att0":
            return False
        self.out_proj(l, d["e_w_out"], list(range(NT)))
        if self.stop == "xmid0":
            return False
        self.ffn(l, [list(range(0, 6)), list(range(6, 12)), list(range(12, 18))])
        return self.stop != "x0"

    def build(self):
        import os
        self.setup()
        if not os.environ.get("SKIP0"):
            self.phase0()
        ok = self.stop != "mod"
        if ok and not os.environ.get("SKIP_EVEN"):
            ok = self.layer_even(0)
        if ok and hasattr(self, "layer_odd"):
            ok = self.layer_odd(1)
        if ok:
            self.final()
        self.pes.close()
        return self.nc


def make_in_maps(inputs, names=None):
    n = 8
    consts = host_constants()
    shared = {}
    for name, shape in WEIGHT_SPECS:
        shared[name] = np.ascontiguousarray(np.asarray(inputs[name], dtype=np.float32).reshape(shape))
    for name, arr in consts.items():
        shared[name] = arr
    maps = []
    x = np.asarray(inputs["x"], dtype=np.float32)
    ctx = np.asarray(inputs["ctx"], dtype=np.float32)
    cc = np.asarray(inputs["c"], dtype=np.float32)
    c_ctx = np.asarray(inputs["c_ctx"], dtype=np.float32)
    for b in range(n):
        m = dict(shared)
        m["x"] = np.ascontiguousarray(x[b])
        m["ctx"] = np.ascontiguousarray(ctx[b])
        m["cvec"] = np.ascontiguousarray(np.stack([cc[b], c_ctx], axis=0))
        if names is not None:
            m = {k: v for k, v in m.items() if k in names}
        maps.append(m)
    return maps


def kernel(**inputs):
    kb = KB()
    nc = kb.build()
    maps = make_in_maps(inputs, set(kb.d.keys()))
    res = run_bass_kernel_spmd(nc, maps, core_ids=list(range(8)))
    return np.stack([np.asarray(r["out"], dtype=np.float32) for r in res.results], axis=0)
```
